# Optimizing a Trainium2 kernel written in Bass

```python
import math
import jax, jax.numpy as jnp
from jax import lax
import numpy as np


D_MODEL = 1024
BATCH = 4
SEQ = 4096
DEPTH = 2
DEC_BATCH = 128
DEC_SEQ = 4
PAST_LEN = 2048
PAGE_SIZE = 128

MIX = D_MODEL
MIX_A = MIX // 2
H_A = 4
DA = MIX_A // (2 * H_A)
DV_A = 2 * DA
MIX_B = MIX - MIX_A
H_B = 4
DB = MIX_B // H_B
HI = 8
DI = 64
TOPK_MAX = 256
QBLOCK = 128
MIX_C = MIX // 2
MIX_D = MIX - MIX_C
POOL_WINDOWS = (2, 4, 8, 16)
N_POOL_GROUPS = len(POOL_WINDOWS)
C_GROUP = MIX_C // N_POOL_GROUPS
POOL_BUF = max(POOL_WINDOWS) - 1
CHUNK = 128
D_GROUPS = 4
D_GROUP_W = MIX_D // D_GROUPS
D_FF = 4 * D_MODEL
N_EVEN = (DEPTH + 1) // 2
N_ODD = DEPTH // 2
ALPHA = (2 * DEPTH) ** 0.25
BETA = (8 * DEPTH) ** -0.25
EPS = 1e-5
E_SIZES = (2 * H_A * DA, 2 * H_A * DA, H_A * DV_A, H_B * DB, DB, DB, HI * DI, DI, HI)
E_COLS = sum(E_SIZES)
O_COLS = MIX_C + 2 * MIX_D

kernel_name = 'hybrid_diffattn_dsa_pool_sgu_step'


def _layernorm(x, g, b):
    xf = x.astype(jnp.float32)
    mu = jnp.mean(xf, -1, keepdims=True)
    var = jnp.mean(jnp.square(xf - mu), -1, keepdims=True)
    return ((xf - mu) * lax.rsqrt(var + EPS) * g + b).astype(x.dtype)


def _rmsnorm(x, g):
    xf = x.astype(jnp.float32)
    return (xf * lax.rsqrt(jnp.mean(xf * xf, -1, keepdims=True) + EPS) * g).astype(x.dtype)


def _split(h, sizes):
    cuts = [int(c) for c in np.cumsum(sizes)[:-1]]
    return jnp.split(h, cuts, axis=-1)


def _to_blocks(x, nblk):
    return jnp.moveaxis(x.reshape((x.shape[0], nblk, -1) + x.shape[2:]), 1, 0)


def _gather_pages(pool, page_table):
    g = pool[page_table]
    return g.reshape((page_table.shape[0], page_table.shape[1] * pool.shape[1]) + pool.shape[2:])


def _mlp(x, w1, w2):
    h = jax.nn.relu(x @ w1)
    return (h * h) @ w2


def _attn_block(q_a, q_b, q_i, w_i, q_pos, k1, k2, va, kb, vb, kidx, lam, subln_g, lam_init, topk):
    B, Q = q_a.shape[:2]
    L = k1.shape[1]
    neg = -jnp.inf
    k_pos = jnp.arange(L)
    causal = k_pos[None, :] <= q_pos[:, None]
    s1 = jnp.einsum('bqhd,bkhd->bhqk', q_a[..., :DA], k1, preferred_element_type=jnp.float32) * DA ** -0.5
    s2 = jnp.einsum('bqhd,bkhd->bhqk', q_a[..., DA:], k2, preferred_element_type=jnp.float32) * DA ** -0.5
    p1 = jax.nn.softmax(jnp.where(causal, s1, neg), axis=-1)
    p2 = jax.nn.softmax(jnp.where(causal, s2, neg), axis=-1)
    att = (p1 - lam * p2).astype(va.dtype)
    o_a = jnp.einsum('bhqk,bkhd->bqhd', att, va)
    o_a = _rmsnorm(o_a, subln_g) * (1.0 - lam_init)
    dots = jnp.einsum('bqhd,bkd->bqhk', q_i, kidx, preferred_element_type=jnp.float32) * DI ** -0.5
    score = jnp.einsum('bqh,bqhk->bqk', w_i.astype(jnp.float32), jax.nn.relu(dots))
    score = jnp.where(causal, score, neg)
    _, idx = lax.top_k(score, topk)
    sel_ok = idx <= q_pos[None, :, None]
    take = jax.vmap(lambda m, i: m[i])
    kb_sel = take(kb, idx)
    vb_sel = take(vb, idx)
    sb = jnp.einsum('bqhd,bqkd->bhqk', q_b, kb_sel, preferred_element_type=jnp.float32) * DB ** -0.5
    pb = jax.nn.softmax(jnp.where(sel_ok[:, None], sb, neg), axis=-1).astype(vb.dtype)
    o_b = jnp.einsum('bhqk,bqkd->bqhd', pb, vb_sel)
    return jnp.concatenate([o_a.reshape(B, Q, MIX_A), o_b.reshape(B, Q, MIX_B)], -1)


def _even_mixer(x, past_a, past_b, w_in, lam_p, subln_g, w_out, lam_init):
    B, T, _ = x.shape
    q_a, k_a, v_a, q_b, k_b, v_b, q_i, k_i, w_i = _split(x @ w_in, E_SIZES)
    new_a = jnp.concatenate([k_a.reshape(B, T, H_A, 2 * DA), v_a.reshape(B, T, H_A, DV_A)], -1)
    new_b = jnp.concatenate([k_b, v_b, k_i], -1)
    keys_a = new_a if past_a is None else jnp.concatenate([past_a.astype(new_a.dtype), new_a], 1)
    keys_b = new_b if past_b is None else jnp.concatenate([past_b.astype(new_b.dtype), new_b], 1)
    L = keys_a.shape[1]
    topk = min(TOPK_MAX, L // 4)
    k1, k2, va = keys_a[..., :DA], keys_a[..., DA:2 * DA], keys_a[..., 2 * DA:]
    kb, vb, kidx = keys_b[..., :DB], keys_b[..., DB:2 * DB], keys_b[..., 2 * DB:]
    lp = lam_p.astype(jnp.float32)
    lam = jnp.exp(jnp.sum(lp[0] * lp[1])) - jnp.exp(jnp.sum(lp[2] * lp[3])) + lam_init
    q_pos = (L - T) + jnp.arange(T)
    qb = min(QBLOCK, T)
    nblk = T // qb
    blocks = (_to_blocks(q_a.reshape(B, T, H_A, 2 * DA), nblk),
              _to_blocks(q_b.reshape(B, T, H_B, DB), nblk),
              _to_blocks(q_i.reshape(B, T, HI, DI), nblk),
              _to_blocks(w_i * HI ** -0.5, nblk),
              q_pos.reshape(nblk, qb))
    out = lax.map(lambda blk: _attn_block(*blk, k1, k2, va, kb, vb, kidx, lam, subln_g, lam_init, topk), blocks)
    out = jnp.moveaxis(out, 0, 1).reshape(B, T, MIX)
    return out @ w_out, new_a, new_b


def _odd_mixer(x, prev, start, w_in, w_pool, pool_scale, sgu_g, sgu_b, w_s, b_s, w_out):
    B, T, _ = x.shape
    xc, z = jnp.split(x @ w_in, [MIX_C], axis=-1)
    u, v = jnp.split(jax.nn.gelu(z), 2, axis=-1)
    ext = jnp.concatenate([prev.astype(xc.dtype), xc], 1)
    n_prev = prev.shape[1]
    cs = jnp.cumsum(ext.astype(jnp.float32), axis=1)
    cs = jnp.concatenate([jnp.zeros_like(cs[:, :1]), cs], 1)
    csg = cs.reshape(B, -1, N_POOL_GROUPS, C_GROUP)
    end = n_prev + 1 + jnp.arange(T)
    pos = start + jnp.arange(T)
    means = []
    for g, w in enumerate(POOL_WINDOWS):
        s = csg[:, end, g] - csg[:, end - w, g]
        cnt = jnp.minimum(w, pos + 1).astype(jnp.float32)
        means.append(s / cnt[None, :, None])
    pooled = jnp.stack(means, 2).astype(xc.dtype) - xc.reshape(B, T, N_POOL_GROUPS, C_GROUP)
    c_out = jnp.einsum('btgc,gcd->btgd', pooled, w_pool).reshape(B, T, MIX_C) * pool_scale
    new_prev = ext[:, -POOL_BUF:]
    vn = _layernorm(v, sgu_g, sgu_b)
    n = min(T, CHUNK)
    ws = jnp.where(jnp.tril(jnp.ones((n, n), bool)), w_s[:, :n, :n], 0)
    vc = vn.reshape(B, T // n, n, D_GROUPS, D_GROUP_W)
    s = jnp.einsum('gij,bkjgc->bkigc', ws, vc) + b_s[:, :n].T[None, None, :, :, None]
    d_out = u * s.reshape(B, T, MIX_D)
    y = jnp.concatenate([c_out, d_out], -1) @ w_out
    return y, new_prev, vn


def setup_inputs(seed: int = 0) -> dict:
    key = jax.random.key(seed)
    ks = jax.random.split(key, 24)
    n_pages = PAST_LEN // PAGE_SIZE
    n_used = DEC_BATCH * n_pages
    n_pool = n_used + max(1, n_used // 4)

    def nrm(k, shape, s):
        return jax.random.normal(k, shape, jnp.float32) * s

    x_prompt = nrm(ks[0], (BATCH, SEQ, D_MODEL), 1.0)
    x_sample = nrm(ks[1], (DEC_BATCH, DEC_SEQ, D_MODEL), 1.0)
    cache_a = nrm(ks[2], (N_EVEN, n_pool, PAGE_SIZE, H_A, 2 * DA + DV_A), 1.0)
    cache_b = nrm(ks[3], (N_EVEN, n_pool, PAGE_SIZE, 2 * DB + DI), 1.0)
    state_pool = nrm(ks[4], (N_ODD, DEC_BATCH, POOL_BUF, MIX_C), 1.0)
    page_table = jax.random.permutation(ks[5], n_pool)[:n_used].reshape(DEC_BATCH, n_pages).astype(jnp.int32)
    col_scale = jnp.concatenate([jnp.full((s,), BETA if j in (2, 5) else 1.0, jnp.float32)
                                 for j, s in enumerate(E_SIZES)])
    w_in_e = nrm(ks[6], (N_EVEN, D_MODEL, E_COLS), D_MODEL ** -0.5) * col_scale
    lam_e = nrm(ks[7], (N_EVEN, 4, DA), 0.1)
    subln_g = 1.0 + nrm(ks[8], (N_EVEN, 2 * DA), 0.05)
    w_out_e = nrm(ks[9], (N_EVEN, MIX, D_MODEL), MIX ** -0.5 * BETA)
    w_in_o = nrm(ks[10], (N_ODD, D_MODEL, O_COLS), D_MODEL ** -0.5)
    w_pool = nrm(ks[11], (N_ODD, N_POOL_GROUPS, C_GROUP, C_GROUP), C_GROUP ** -0.5)
    pool_scale = 1.0 + nrm(ks[12], (N_ODD, MIX_C), 0.05)
    sgu_g = 1.0 + nrm(ks[13], (N_ODD, MIX_D), 0.05)
    sgu_b = nrm(ks[14], (N_ODD, MIX_D), 0.02)
    w_s = nrm(ks[15], (N_ODD, D_GROUPS, CHUNK, CHUNK), CHUNK ** -0.5)
    b_s = 1.0 + nrm(ks[16], (N_ODD, D_GROUPS, CHUNK), 0.05)
    w_out_o = nrm(ks[17], (N_ODD, MIX, D_MODEL), MIX ** -0.5 * BETA)
    w_mlp1 = nrm(ks[18], (DEPTH, D_MODEL, D_FF), D_MODEL ** -0.5)
    w_mlp2 = nrm(ks[19], (DEPTH, D_FF, D_MODEL), D_FF ** -0.5 * BETA)
    ln_g = 1.0 + nrm(ks[20], (DEPTH, 2, D_MODEL), 0.05)
    ln_b = nrm(ks[21], (DEPTH, 2, D_MODEL), 0.02)
    return {'x_prompt': x_prompt, 'x_sample': x_sample, 'cache_a': cache_a, 'cache_b': cache_b,
            'state_pool': state_pool, 'page_table': page_table, 'w_in_e': w_in_e, 'lam_e': lam_e,
            'subln_g': subln_g, 'w_out_e': w_out_e, 'w_in_o': w_in_o, 'w_pool': w_pool,
            'pool_scale': pool_scale, 'sgu_g': sgu_g, 'sgu_b': sgu_b, 'w_s': w_s, 'b_s': b_s,
            'w_out_o': w_out_o, 'w_mlp1': w_mlp1, 'w_mlp2': w_mlp2, 'ln_g': ln_g, 'ln_b': ln_b}


def reference(x_prompt, x_sample, cache_a, cache_b, state_pool, page_table, w_in_e, lam_e, subln_g,
              w_out_e, w_in_o, w_pool, pool_scale, sgu_g, sgu_b, w_s, b_s, w_out_o, w_mlp1, w_mlp2,
              ln_g, ln_b):
    xp, xs = x_prompt, x_sample
    a_p, b_p, pool_p, a_s, b_s_new, pool_s, v_s = [], [], [], [], [], [], []
    for l in range(DEPTH):
        i = l // 2
        if l % 2 == 0:
            lam_init = 0.8 - 0.6 * math.exp(-0.3 * l)
            mp, na, nb = _even_mixer(xp, None, None, w_in_e[i], lam_e[i], subln_g[i], w_out_e[i], lam_init)
            ms, nas, nbs = _even_mixer(xs, _gather_pages(cache_a[i], page_table),
                                       _gather_pages(cache_b[i], page_table),
                                       w_in_e[i], lam_e[i], subln_g[i], w_out_e[i], lam_init)
            a_p.append(na); b_p.append(nb); a_s.append(nas); b_s_new.append(nbs)
        else:
            prev0 = jnp.zeros((xp.shape[0], POOL_BUF, MIX_C), xp.dtype)
            mp, npool, _ = _odd_mixer(xp, prev0, 0, w_in_o[i], w_pool[i], pool_scale[i], sgu_g[i],
                                      sgu_b[i], w_s[i], b_s[i], w_out_o[i])
            ms, npools, nvs = _odd_mixer(xs, state_pool[i], PAST_LEN, w_in_o[i], w_pool[i], pool_scale[i],
                                         sgu_g[i], sgu_b[i], w_s[i], b_s[i], w_out_o[i])
            pool_p.append(npool); pool_s.append(npools); v_s.append(nvs)
        xp = _layernorm(ALPHA * xp + mp, ln_g[l, 0], ln_b[l, 0])
        xp = _layernorm(ALPHA * xp + _mlp(xp, w_mlp1[l], w_mlp2[l]), ln_g[l, 1], ln_b[l, 1])
        xs = _layernorm(ALPHA * xs + ms, ln_g[l, 0], ln_b[l, 0])
        xs = _layernorm(ALPHA * xs + _mlp(xs, w_mlp1[l], w_mlp2[l]), ln_g[l, 1], ln_b[l, 1])
    return (xp, xs, jnp.stack(a_p), jnp.stack(b_p), jnp.stack(pool_p), jnp.stack(a_s),
            jnp.stack(b_s_new), jnp.stack(pool_s), jnp.stack(v_s))
```

```python
import numpy as np
import concourse.bass as bass
import concourse.mybir as mybir
from concourse.bass_utils import run_bass_kernel_spmd

F32 = mybir.dt.float32
BF16 = mybir.dt.bfloat16
I32 = mybir.dt.int32
U32 = mybir.dt.uint32
ALU = mybir.AluOpType
AF = mybir.ActivationFunctionType
AX = mybir.AxisListType

QA, KA, VA, QB, KB, VB, QI, KI, WI = 0, 512, 1024, 1536, 2048, 2176, 2304, 2816, 2880
ALPHA = 4 ** 0.25
EPS = 1e-5
LAM_INIT0 = 0.8 - 0.6
NBIS = 18


class Buf:
    def __init__(self, name, lo=0, hi=0):
        self.name = name
        self.w = None
        self.r = {}
        self.lo, self.hi = lo, hi
        self.overlaps = []
        self.psum = False


class Sched:
    def __init__(self, nc):
        self.nc = nc
        self.eng = {'pe': nc.tensor, 'act': nc.scalar, 'dve': nc.vector, 'pool': nc.gpsimd, 'sp': nc.sync}
        self.sems = {}
        self.cnt = {}
        self.seen = {e: {} for e in self.eng}
        self.swdge_tag = None

    def newsem(self, name):
        if name not in self.sems:
            self.sems[name] = self.nc.alloc_semaphore(name)
            self.cnt[name] = 0
        return name

    def _wait(self, e, deps):
        best = {}
        for d in deps:
            if d is None:
                continue
            s, v = d
            if best.get(s, 0) < v:
                best[s] = v
        for s, v in best.items():
            if self.seen[e].get(s, 0) >= v:
                continue
            self.eng[e].wait_ge(self.sems[s], v)
            self.seen[e][s] = v

    def op(self, e, fn, reads=(), writes=(), sem=None, inc=1, multi=False):
        deps = set()
        for b in reads:
            deps.add(b.w)
            for o in b.overlaps:
                deps.add(o.w)
            if b.psum:
                deps.update(b.r.items())
        for b in writes:
            deps.add(b.w)
            deps.update(b.r.items())
            for o in b.overlaps:
                deps.add(o.w)
                deps.update(o.r.items())
        self._wait(e, deps)
        if sem is None:
            sem = self.newsem('c_' + e)
        if multi:
            inss = fn()
            for ins in inss:
                ins.then_inc(self.sems[sem], inc)
                self.cnt[sem] += inc
        else:
            ins = fn()
            ins.then_inc(self.sems[sem], inc)
            self.cnt[sem] += inc
        tag = (sem, self.cnt[sem])
        if e == 'pool' and inc == 16:
            self.swdge_tag = tag
        for b in writes:
            b.w = tag
            b.r = {}
        for b in reads:
            if b.r.get(sem, 0) < self.cnt[sem]:
                b.r[sem] = self.cnt[sem]

    def finish(self, e, bufs):
        deps = set()
        for b in bufs:
            deps.add(b.w)
            deps.update(b.r.items())
        self._wait(e, deps)


class Arena:
    def __init__(self, nc, nbytes):
        self.nbytes = nbytes
        self.t = nc.alloc_sbuf_tensor("arena", [128, nbytes // 2], BF16).ap()
        self.top = 0
        self.all = []
        self.peak = 0

    def alloc(self, name, free_shape, dtype):
        esz = 4 if dtype in (F32, I32, U32) else 2
        n = int(np.prod(free_shape))
        nb = (n * esz + 63) // 64 * 64
        off = self.top
        self.top += nb
        self.peak = max(self.peak, self.top)
        assert self.top <= self.nbytes, (name, self.top, self.nbytes)
        ap = self.t[:, off // 2:(off + n * esz) // 2]
        if esz == 4:
            ap = ap.bitcast(dtype)
        elif dtype != BF16:
            ap = ap.bitcast(dtype)
        if len(free_shape) == 2:
            ap = ap.rearrange("p (a b) -> p a b", a=free_shape[0])
        elif len(free_shape) == 3:
            ap = ap.rearrange("p (a b c) -> p a b c", a=free_shape[0], b=free_shape[1])
        b = Buf(name, off, off + nb)
        for o in self.all:
            if o.lo < b.hi and b.lo < o.hi:
                o.overlaps.append(b)
                b.overlaps.append(o)
        self.all.append(b)
        return ap, b


def build(cfg):
    SEQ, NB, NPAGE, NPOOL = cfg['SEQ'], cfg['NB'], cfg['NPAGE'], cfg['NPOOL']
    PAST = NPAGE * 128
    NG = SEQ // 1024
    GQ = 512
    TOWN = NG * GQ
    NH = 16 * NG
    NS = NB * 4
    NQ = NH + TOWN
    NKT = SEQ // 128
    LCAP = max(SEQ, PAST + 128)
    KP = min(256, SEQ // 4)
    KS = min(256, (PAST + 4) // 4)
    NTQ = 1 + TOWN // 128 + 1

    nc = bass.Bass("TRN2", target_bir_lowering=False)
    S = Sched(nc)

    def din(name, shape, dt=F32):
        return nc.dram_tensor(name, list(shape), dt, kind="ExternalInput").ap()

    def dout(name, shape, dt=F32):
        return nc.dram_tensor(name, list(shape), dt, kind="ExternalOutput").ap()

    xT_seq = din("xT_seq", [1024, SEQ])
    xT_q = din("xT_q", [1024, NQ])
    xT_s = din("xT_s", [1024, NS])
    posq_row_d = din("posq_row", [128, NQ + NS])
    posq_col_d = din("posq_col", [128, NTQ])
    consts_d = din("consts", [128, 512 + 64 + 32 + 128 + 128 + 128])
    cache_a = din("cache_a", [NPOOL * 128, 1024])
    cache_b = din("cache_b", [NPOOL * 128, 320])
    pt_d = din("pt", [128, NB * NPAGE], I32)
    stateT = din("stateT", [512, NB, 15])
    w_in_e = din("w_in_e", [1024, 2888])
    lam_rep = din("lam_rep", [128, 256])
    sublng = din("sublng", [128, 1])
    w_out_e = din("w_out_e", [1024, 1024])
    w_in_o = din("w_in_o", [1024, 1536])
    w_out_o = din("w_out_o", [1024, 1024])
    w_pool = din("w_pool", [4, 128, 128])
    pool_scaleT = din("pool_scaleT", [128, 4])
    sgu_gb = din("sgu_gb", [128, 1024])
    w_sT = din("w_sT", [128, 4, 128])
    w_sT_s = din("w_sT_s", [64, 4, 64])
    mask_s_d = din("mask_s", [64, 64])
    bs_rep = din("bs_rep", [128, 4, 128])
    bs_rep_s = din("bs_rep_s", [128, 4, 64])
    ln_gb = din("ln_gb", [128, 64])
    w_mlp1 = din("w_mlp1", [2, 1024, 4096])
    w_mlp2 = din("w_mlp2", [2, 4096, 1024])
    halo_valid = din("halo_valid", [128, 64])
    invc_d = din("invc", [128, 64])

    yT_q = dout("yT_q", [1024, TOWN])
    yT_s = dout("yT_s", [1024, NS])
    kT_all = dout("kT_all", [704, SEQ])
    v_all = dout("v_all", [SEQ, 640])
    xc_tail = dout("xc_tail", [512, 16])
    kT_s_o = dout("kT_s", [704, NS])
    v_s_o = dout("v_s", [NB, 4, 640])
    poolT_s = dout("poolT_s", [512, NB, 15])
    vn_s_o = dout("vn_s", [NS, 512])
    OUTB = Buf("outputs")

    A = Arena(nc, 207 * 1024)
    kaT, b_kaT = A.alloc("kaT", [4, LCAP], BF16)
    kbT, b_kbT = A.alloc("kbT", [LCAP], BF16)
    kiT, b_kiT = A.alloc("kiT", [LCAP], BF16)
    va, b_va = A.alloc("va", [LCAP // 128, 512], BF16)
    vb, b_vb = A.alloc("vb", [LCAP // 128, 128], BF16)
    cst, b_cst = A.alloc("cst", [512 + 64 + 32 + 128 + 128 + 128], F32)
    ramp512 = cst[:, 0:512]
    blkoff = cst[:, 512:576]
    pow2 = cst[:, 576:608]
    tril_f = cst[:, 736:864]
    poskc = cst[:, 864:992]
    identb, b_identb = A.alloc("identb", [128], BF16)
    onesb, b_onesb = A.alloc("onesb", [128], BF16)
    onesd, b_onesd = A.alloc("onesd", [128], BF16)
    ones128, b_ones128 = A.alloc("ones128", [128], BF16)
    posq_col, b_posq_col = A.alloc("posq_col", [NTQ], F32)
    lngb, b_lngb = A.alloc("lngb", [64], F32)
    small, b_small = A.alloc("small", [16], F32)
    pscT, b_pscT = A.alloc("pscT", [4], F32)
    wpool_b, b_wpool = A.alloc("wpool", [4, 128], BF16)
    wsT_b, b_wsT = A.alloc("wsT", [4, 128], BF16)
    wsTs_b, b_wsTs = A.alloc("wsTs", [4, 64], BF16)
    bsr, b_bsr = A.alloc("bsr", [4, 128], F32)
    bsrs, b_bsrs = A.alloc("bsrs", [4, 64], F32)
    sgugb, b_sgugb = A.alloc("sgugb", [1024], F32)
    hval, b_hval = A.alloc("hval", [64], F32)
    invc, b_invc = A.alloc("invc", [4, 16], F32)
    xch, b_xch = A.alloc("xch", [4, 64], F32)
    wwi, b_wwi = A.alloc("wwi", [8, 8], BF16)
    A_pti = A.alloc("pti", [NB * NPAGE], I32)
    A_idx = A.alloc("idx", [NB * NPAGE], I32)
    NSLOT = 2
    slots = [A.alloc("wslot%d" % i, [4096], BF16) for i in range(NSLOT)]
    for i in range(NSLOT):
        S.newsem("ws%d" % i)
    slot_i = [0]

    psb = []
    for i in range(4):
        t = nc.alloc_psum_tensor("ps%d" % i, [128, 1024], F32).ap()
        psb.append((t, Buf("ps%d" % i)))
        psb[-1][1].psum = True

    for s in ["ld0", "ld1", "ld2", "ld3", "st0", "st1", "gath", "cst"]:
        S.newsem(s)

    def dma(e, out, in_, reads, writes, sem=None):
        writes = [w for w in writes if w is not OUTB]
        bb = (writes + reads)[0]
        sem = S.newsem("d_" + bb.name)
        S.op(e, lambda: S.eng[e].dma_start(out=out, in_=in_), reads, writes, sem=sem, inc=16)

    def wload(src2d, kc, ncols):
        i = slot_i[0] % NSLOT
        slot_i[0] += 1
        ap, b = slots[i]
        v = ap[:, 0:kc * ncols].rearrange("p (k n) -> p k n", k=kc)
        dma('pool', v, src2d.rearrange("(k p) n -> p k n", p=128), [], [b], "ws%d" % i)
        return v, b

    _op_real = S.op
    NOC = cfg.get('NOCONST', 0)
    if NOC == 1:
        S.op = lambda *a, **k: None
    elif NOC == 2:
        S.op = lambda e, fn, reads=(), writes=(), sem=None, inc=1, multi=False: (_op_real(e, fn, reads, writes, sem=sem, inc=inc, multi=multi) if inc == 16 else None)
    elif NOC == 3:
        S.op = lambda e, fn, reads=(), writes=(), sem=None, inc=1, multi=False: (None if (inc == 16 and e == 'pool') else _op_real(e, fn, reads, writes, sem=sem, inc=inc, multi=multi))
    dma('sp', cst, consts_d, [], [b_cst], "cst")
    dma('sp', posq_col, posq_col_d, [], [b_posq_col], "cst")
    dma('sp', lngb, ln_gb, [], [b_lngb], "cst")
    dma('sp', pscT, pool_scaleT, [], [b_pscT], "cst")
    dma('sp', bsr, bs_rep, [], [b_bsr], "cst")
    dma('sp', bsrs, bs_rep_s, [], [b_bsrs], "cst")
    dma('sp', sgugb, sgu_gb, [], [b_sgugb], "cst")
    dma('sp', hval, halo_valid, [], [b_hval], "cst")
    dma('sp', invc.rearrange("p a b -> p (a b)"), invc_d, [], [b_invc], "cst")
    dma('pool', identb, consts_d[:, 608:736], [], [b_identb], "ld0")
    dma('pool', wwi, w_in_e[:, WI:WI + 8].rearrange("(k p) n -> p k n", p=128), [], [b_wwi])
    dma('pool', wpool_b, w_pool.rearrange("g c d -> c g d"), [], [b_wpool], "ld0")
    KEEPC = cfg.get('KEEPC', 'ABCDEFGH')
    if 'A' in KEEPC:
        S.op('dve', lambda: nc.vector.memset(onesb, 1.0), [], [b_onesb])
    if 'A' in KEEPC:
        S.op('dve', lambda: nc.vector.memset(onesd, 1.0 / 1024), [], [b_onesd])
    if 'A' in KEEPC:
        S.op('dve', lambda: nc.vector.memset(ones128, 1.0 / 128), [], [b_ones128])

    mk = A.top
    t0, b_t0 = A.alloc("t0", [256], F32)
    t1, b_t1 = A.alloc("t1", [4, 128], F32)
    t2, b_t2 = A.alloc("t2", [64, 64], F32)
    t3, b_t3 = A.alloc("t3", [64], F32)
    if 'B' in KEEPC and 'E' in KEEPC:
        dma('sp', t0, lam_rep, [], [b_t0], "ld1")
    if 'B' in KEEPC and 'E' in KEEPC:
        S.op('dve', lambda: nc.vector.tensor_tensor(out=t0[:, 0:64], in0=t0[:, 0:64], in1=t0[:, 64:128], op=ALU.mult), [b_t0], [b_t0])
    if 'B' in KEEPC and 'E' in KEEPC:
        S.op('dve', lambda: nc.vector.tensor_tensor(out=t0[:, 128:192], in0=t0[:, 128:192], in1=t0[:, 192:256], op=ALU.mult), [b_t0], [b_t0])
    if 'B' in KEEPC and 'E' in KEEPC:
        S.op('dve', lambda: nc.vector.reduce_sum(out=small[:, 2:3], in_=t0[:, 0:64], axis=AX.X), [b_t0], [b_small])
    if 'B' in KEEPC and 'E' in KEEPC:
        S.op('dve', lambda: nc.vector.reduce_sum(out=small[:, 3:4], in_=t0[:, 128:192], axis=AX.X), [b_t0, b_small], [b_small])
    if 'B' in KEEPC and 'F' in KEEPC:
        S.op('act', lambda: nc.scalar.activation(out=small[:, 4:6], in_=small[:, 2:4], func=AF.Exp), [b_small], [b_small])
    if 'B' in KEEPC and 'G' in KEEPC:
        S.op('dve', lambda: nc.vector.tensor_tensor(out=small[:, 6:7], in0=small[:, 5:6], in1=small[:, 4:5], op=ALU.subtract), [b_small], [b_small])
    if 'B' in KEEPC and 'G' in KEEPC:
        S.op('dve', lambda: nc.vector.tensor_scalar(out=small[:, 0:1], in0=small[:, 6:7], scalar1=-LAM_INIT0, scalar2=None, op0=ALU.add), [b_small], [b_small])
    if 'B' in KEEPC and 'H' in KEEPC:
        dma('sp', small[:, 7:8], sublng, [], [b_small], "ld1")
    if 'B' in KEEPC and 'H' in KEEPC:
        S.op('dve', lambda: nc.vector.tensor_scalar(out=small[:, 1:2], in0=small[:, 7:8], scalar1=1.0 - LAM_INIT0, scalar2=None, op0=ALU.mult), [b_small], [b_small])
    if 'C' in KEEPC:
        dma('sp', t1, w_sT, [], [b_t1], "ld2")
    for g in (range(4) if 'C' in KEEPC else []):
        S.op('dve', lambda g=g: nc.vector.tensor_tensor(out=wsT_b[:, g, :], in0=t1[:, g, :], in1=tril_f, op=ALU.mult), [b_t1, b_cst], [b_wsT])
    if 'D' in KEEPC:
        dma('sp', t2[0:64, 0:4, :], w_sT_s, [], [b_t2], "ld3")
    if 'D' in KEEPC:
        dma('sp', t3[0:64, :], mask_s_d, [], [b_t3], "ld3")
    for g in (range(4) if 'D' in KEEPC else []):
        S.op('dve', lambda g=g: nc.vector.tensor_tensor(out=wsTs_b[0:64, g, :], in0=t2[0:64, g, :], in1=t3[0:64, :], op=ALU.mult), [b_t2, b_t3], [b_wsTs])
    A.top = mk

    S.op = _op_real
    dma('sp', A_pti[0], pt_d, [], [A_pti[1]])
    S.op('dve', lambda: nc.vector.tensor_scalar(out=A_idx[0], in0=A_pti[0], scalar1=128.0, scalar2=poskc[:, 0:1], op0=ALU.mult, op1=ALU.add), [A_pti[1], b_cst], [A_idx[1]])
    NEG = -1.0e30
    if cfg.get('STOP', 9) <= 0:
        for sname, v in S.cnt.items():
            if v > 0 and S.seen['sp'].get(sname, 0) < v:
                nc.sync.wait_ge(S.sems[sname], v)
        return nc, dict(peak=A.peak)

    def evac_act(out, in_, reads, writes, func=AF.Copy, scale=1.0):
        S.op('act', lambda: nc.scalar.activation(out=out, in_=in_, func=func, scale=scale), reads, writes)

    def layernorm(x, b_x, xb, b_xb, n, lnidx, scr):
        (sq, b_sq), (m2, b_m2), (rs, b_rs) = scr
        pm, b_pm = psb[0]
        pv, b_pv = psb[1]
        for c in range(8):
            evac_act(xb[:, c, 0:n], x[:, c, 0:n], [b_x], [b_xb])
            evac_act(sq[:, c, 0:n], x[:, c, 0:n], [b_x], [b_sq], func=AF.Square)

        def f():
            last = None
            for c in range(8):
                last = nc.tensor.matmul(pm[:, 0:n], lhsT=onesd, rhs=xb[:, c, 0:n], start=(c == 0), stop=(c == 7))
            return last
        S.op('pe', f, [b_xb, b_onesd], [b_pm])

        def f2():
            last = None
            for c in range(8):
                last = nc.tensor.matmul(pv[:, 0:n], lhsT=onesd, rhs=sq[:, c, 0:n], start=(c == 0), stop=(c == 7))
            return last
        S.op('pe', f2, [b_sq, b_onesd], [b_pv])
        evac_act(m2[:, 0:n], pm[:, 0:n], [b_pm], [b_m2], func=AF.Square)
        S.op('dve', lambda: nc.vector.tensor_tensor(out=rs[:, 0:n], in0=pv[:, 0:n], in1=m2[:, 0:n], op=ALU.subtract), [b_pv, b_m2], [b_rs])
        S.op('dve', lambda: nc.vector.tensor_scalar(out=rs[:, 0:n], in0=rs[:, 0:n], scalar1=0.0, scalar2=EPS, op0=ALU.max, op1=ALU.add), [b_rs], [b_rs])
        S.op('act', lambda: nc.scalar.activation(out=rs[:, 0:n], in_=rs[:, 0:n], func=AF.Sqrt), [b_rs], [b_rs])
        S.op('dve', lambda: nc.vector.reciprocal(out=rs[:, 0:n], in_=rs[:, 0:n]), [b_rs], [b_rs])
        S.op('act', lambda: nc.scalar.activation(out=m2[:, 0:n], in_=pm[:, 0:n], func=AF.Copy), [b_pm], [b_m2])
        for c in range(8):
            S.op('dve', lambda c=c: nc.vector.tensor_tensor(out=x[:, c, 0:n], in0=x[:, c, 0:n], in1=m2[:, 0:n], op=ALU.subtract), [b_x, b_m2], [b_x])
            S.op('pool', lambda c=c: nc.gpsimd.tensor_tensor(out=x[:, c, 0:n], in0=x[:, c, 0:n], in1=rs[:, 0:n], op=ALU.mult), [b_x, b_rs], [b_x])
            S.op('dve', lambda c=c: nc.vector.tensor_scalar(out=x[:, c, 0:n], in0=x[:, c, 0:n], scalar1=lngb[:, lnidx * 8 + c:lnidx * 8 + c + 1],
                                                            scalar2=lngb[:, 32 + lnidx * 8 + c:32 + lnidx * 8 + c + 1], op0=ALU.mult, op1=ALU.add), [b_x, b_lngb], [b_x])
            evac_act(xb[:, c, 0:n], x[:, c, 0:n], [b_x], [b_xb])

    def proj_fm(wsrc, ncols, xb, b_xb, n, kc, consume):
        done = 0
        pi = 0
        while done < ncols:
            cw = min(512, ncols - done)
            wv, b_w = wload(wsrc[:, done:done + cw], kc, cw)
            for j in range(0, cw, 128):
                m = min(128, cw - j)
                ps, b_ps = psb[2 + (pi % 2)]
                pi += 1

                def f(j=j, m=m, ps=ps, wv=wv):
                    last = None
                    for k in range(kc):
                        last = nc.tensor.matmul(ps[0:m, 0:n], lhsT=wv[:, k, j:j + m], rhs=xb[:, k, 0:n], start=(k == 0), stop=(k == kc - 1))
                    return last
                S.op('pe', f, [b_w, b_xb], [b_ps])
                consume((done + j) // 128, m, ps, b_ps)
            done += cw

    def mlp(l, x, b_x, xb, b_xb, n, hT, b_hT):
        for fc in range(8):
            wv, b_w = wload(w_mlp1[l][:, fc * 512:(fc + 1) * 512], 8, 512)
            for sub in range(4):
                ps, b_ps = psb[2 + (sub % 2)]

                def f(sub=sub, ps=ps, wv=wv):
                    last = None
                    for k in range(8):
                        last = nc.tensor.matmul(ps[:, 0:n], lhsT=wv[:, k, sub * 128:(sub + 1) * 128], rhs=xb[:, k, 0:n], start=(k == 0), stop=(k == 7))
                    return last
                S.op('pe', f, [b_w, b_xb], [b_ps])
                fi = fc * 4 + sub
                evac_act(hT[:, fi, 0:n], ps[:, 0:n], [b_ps], [b_hT], func=AF.Relu)
                S.op('pool', lambda fi=fi: nc.gpsimd.tensor_tensor(out=hT[:, fi, 0:n], in0=hT[:, fi, 0:n], in1=hT[:, fi, 0:n], op=ALU.mult), [b_hT], [b_hT])
        for oc in range(8):
            wv, b_w = wload(w_mlp2[l][:, oc * 128:(oc + 1) * 128], 32, 128)
            ps, b_ps = psb[2 + (oc % 2)]

            def f(ps=ps, wv=wv):
                last = None
                for k in range(32):
                    last = nc.tensor.matmul(ps[:, 0:n], lhsT=wv[:, k, :], rhs=hT[:, k, 0:n], start=(k == 0), stop=(k == 31))
                return last
            S.op('pe', f, [b_w, b_hT], [b_ps])
            S.op('dve', lambda oc=oc, ps=ps: nc.vector.scalar_tensor_tensor(out=x[:, oc, 0:n], in0=x[:, oc, 0:n], scalar=ALPHA, in1=ps[:, 0:n], op0=ALU.mult, op1=ALU.add), [b_x, b_ps], [b_x])

    def outproj(wsrc, aT, b_aT, x, b_x, n):
        for half in range(2):
            wv, b_w = wload(wsrc[:, half * 512:(half + 1) * 512], 8, 512)
            for sub in range(4):
                oc = half * 4 + sub
                ps, b_ps = psb[2 + (sub % 2)]

                def f(sub=sub, ps=ps, wv=wv):
                    last = None
                    for k in range(8):
                        last = nc.tensor.matmul(ps[:, 0:n], lhsT=wv[:, k, sub * 128:(sub + 1) * 128], rhs=aT[:, k, 0:n], start=(k == 0), stop=(k == 7))
                    return last
                S.op('pe', f, [b_w, b_aT], [b_ps])
                S.op('dve', lambda oc=oc, ps=ps: nc.vector.scalar_tensor_tensor(out=x[:, oc, 0:n], in0=x[:, oc, 0:n], scalar=ALPHA, in1=ps[:, 0:n], op0=ALU.mult, op1=ALU.add), [b_x, b_ps], [b_x])

    mk = A.top
    wkv, b_wkv = A.alloc("wkv", [8, 1344], BF16)
    for (dst, src, w) in [(0, KA, 512), (512, KB, 128), (640, KI, 64), (704, VA, 512), (1216, VB, 128)]:
        dma('pool', wkv[:, :, dst:dst + w], w_in_e[:, src:src + w].rearrange("(k p) n -> p k n", p=128), [], [b_wkv], "ld1")
    xsb = [A.alloc("xsb%d" % i, [8, 512], BF16) for i in range(2)]
    stg = [A.alloc("stg%d" % i, [704], F32) for i in range(2)]
    stgk = [A.alloc("stgk%d" % i, [512], F32) for i in range(2)]
    sti = 0
    for gi in range(min(SEQ // 512, cfg.get('P1G', 99))):
        xs, b_xs = xsb[gi % 2]
        dma('pool', xs, xT_seq[:, gi * 512:(gi + 1) * 512].rearrange("(k p) n -> p k n", p=128), [], [b_xs], "ld%d" % (2 + gi % 2))
        P1V = cfg.get('P1V', 9)
        for j in range(min(6, cfg.get('P1J', 6)) if P1V >= 2 else 0):
            m = 128 if j < 5 else 64
            ps, b_ps = psb[j % 2]

            def f(j=j, m=m, ps=ps):
                last = None
                for k in range(8):
                    last = nc.tensor.matmul(ps[0:m, 0:512], lhsT=wkv[:, k, j * 128:j * 128 + m], rhs=xs[:, k, :], start=(k == 0), stop=(k == 7))
                return last
            S.op('pe', f, [b_wkv, b_xs], [b_ps])
            if j < 4:
                dst, b_dst = kaT[:, j, gi * 512:(gi + 1) * 512], b_kaT
            elif j == 4:
                dst, b_dst = kbT[:, gi * 512:(gi + 1) * 512], b_kbT
            else:
                dst, b_dst = kiT[0:64, gi * 512:(gi + 1) * 512], b_kiT
            evac_act(dst, ps[0:m, 0:512], [b_ps], [b_dst])
            if P1V < 3:
                continue
            sk, b_sk = stgk[sti % 2]
            sti += 1
            if cfg.get('P1X', 3) == 3:
                evac_act(sk[0:m, :], ps[0:m, 0:512], [b_ps], [b_sk])
            elif cfg.get('P1X', 0) == 4:
                S.op('dve', lambda m=m, ps=ps, sk=sk: nc.vector.tensor_scalar(out=sk[0:m, :], in0=ps[0:m, 0:512], scalar1=1.0, scalar2=None, op0=ALU.mult), [b_ps], [b_sk])
            elif cfg.get('P1X', 0) != 2:
                S.op('dve', lambda m=m, ps=ps, sk=sk: nc.vector.tensor_copy(out=sk[0:m, :], in_=ps[0:m, 0:512]), [b_ps], [b_sk])
            if cfg.get('P1X', 0) != 1:
                dma('sp', kT_all[j * 128:j * 128 + m, gi * 512:(gi + 1) * 512], sk[0:m, :], [b_sk], [OUTB], "st0")
        for t in range(4 if P1V >= 4 else 0):
            kt = gi * 4 + t
            ps, b_ps = psb[2 + (t % 2)]

            def f(t=t, ps=ps):
                last = None
                for k in range(8):
                    nc.tensor.matmul(ps[:, 0:512], lhsT=xs[:, k, t * 128:(t + 1) * 128], rhs=wkv[:, k, 704:1216], start=(k == 0), stop=(k == 7))
                    last = nc.tensor.matmul(ps[:, 512:640], lhsT=xs[:, k, t * 128:(t + 1) * 128], rhs=wkv[:, k, 1216:1344], start=(k == 0), stop=(k == 7))
                return last
            S.op('pe', f, [b_wkv, b_xs], [b_ps])
            evac_act(va[:, kt, :], ps[:, 0:512], [b_ps], [b_va])
            evac_act(vb[:, kt, :], ps[:, 512:640], [b_ps], [b_vb])
            sg, b_sg = stg[t % 2]
            evac_act(sg[:, 0:512], ps[:, 0:512], [b_ps], [b_sg])
            evac_act(sg[:, 512:640], ps[:, 512:640], [b_ps, b_sg], [b_sg])
            dma('sp', v_all[kt * 128:(kt + 1) * 128, :], sg[:, 0:640], [b_sg], [OUTB], "st1")
    A.top = mk

    class _Stop(Exception):
        pass
    SUB = cfg.get('SUB', 99)

    def chk(level):
        if SUB <= level:
            raise _Stop()

    def finish_all():
        for sname, v in S.cnt.items():
            if v > 0 and S.seen['sp'].get(sname, 0) < v:
                nc.sync.wait_ge(S.sems[sname], v)
        return nc, dict(peak=A.peak)
    STOP = cfg.get('STOP', 9)
    if STOP <= 1:
        return finish_all()
    xb, b_xb = A.alloc("xb", [8, GQ], BF16)
    aT, b_aT = A.alloc("aT", [8, GQ], BF16)
    pqr, b_pqr = A.alloc("pqr", [GQ], F32)
    mkg = A.top
    qT, b_qT = A.alloc("qT", [8, GQ], BF16)
    qiT, b_qiT = A.alloc("qiT", [8, GQ], BF16)
    wtok, b_wtok = A.alloc("wtok", [4, 8], F32)
    sc0 = A.top
    score, b_score = A.alloc("score", [max(LCAP, 4096)], F32)
    sc1 = A.top
    A.top = sc0
    PTs = [A.alloc("PT%d" % i, [1024], BF16) for i in range(2)]
    f1, b_f1 = A.alloc("f1", [1024], F32)
    f2, b_f2 = A.alloc("f2", [1024], F32)
    f3, b_f3 = A.alloc("f3", [512], F32)
    f4, b_f4 = A.alloc("f4", [512], BF16)
    assert A.top <= sc1
    A.top = sc1
    maskb, b_maskb = A.alloc("maskb", [LCAP], BF16)
    maskT, b_maskT = A.alloc("maskT", [LCAP // 128, 128], BF16)
    cm = [A.alloc("cm%d" % i, [128], BF16) for i in range(2)]
    rtmp = [A.alloc("rtmp%d" % i, [512], F32) for i in range(2)]
    bis, b_bis = A.alloc("bis", [64], F32)
    pqb, b_pqb = A.alloc("pqb", [64], F32)
    attn_top = A.top
    A.top = mkg
    hT, b_hT = A.alloc("hT", [32, GQ], BF16)
    lnscr = (A.alloc("lnsq", [8, GQ], BF16), A.alloc("lnm2", [GQ], F32), A.alloc("lnrs", [GQ], F32))
    xf, b_xf = A.alloc("xf", [8, GQ], F32)
    post_top = A.top
    A.top = mkg
    xce, b_xce = A.alloc("xce", [4, 16 + GQ], F32)
    lv = [A.alloc("lv%d" % i, [16 + GQ], F32) for i in range(2)]
    poolb, b_poolb = A.alloc("poolb", [4, GQ], BF16)
    uT, b_uT = A.alloc("uT", [4, GQ], BF16)
    vtok, b_vtok = A.alloc("vtok", [4, 512], F32)
    vnb, b_vnb = A.alloc("vnb", [4, 512], BF16)
    vst, b_vst = A.alloc("vst", [4, 8], F32)
    l1_top = A.top
    A.top = max(attn_top, post_top, l1_top)

    def attention_tile(q0, nq, ktiles, krow_len, pqcol, Ktop, sampleb=None):
        nkt = len(ktiles)
        (SA0, b_SA0), (SA1, b_SA1), (O, b_O), (L, b_L) = psb
        SAs = [(SA0, b_SA0), (SA1, b_SA1)]
        W4a = 4 * nq
        for ti, kt in enumerate(ktiles):
            nk = kt['nk']
            SA, b_SA = SAs[ti % 2]
            PT, b_PT = PTs[ti % 2]
            cmk, b_cmk = cm[ti % 2]

            def f(kt=kt, nk=nk, SA=SA):
                last = None
                for h in range(4):
                    for s in range(2):
                        last = nc.tensor.matmul(SA[0:nk, s * 512 + h * nq:s * 512 + (h + 1) * nq], lhsT=kt['kaT'][h][64 * s:64 * s + 64, :],
                                                rhs=qT[64 * s:64 * s + 64, h, q0:q0 + nq], start=True, stop=True)
                return last
            S.op('pe', f, kt['bufs'] + [b_qT], [b_SA])
            for s_ in range(2):
                c0 = s_ * 512
                S.op('act', lambda nk=nk, SA=SA, PT=PT, c0=c0: nc.scalar.activation(out=PT[0:nk, c0:c0 + W4a], in_=SA[0:nk, c0:c0 + W4a], func=AF.Exp, scale=0.125), [b_SA], [b_PT])
            S.op('dve', lambda nk=nk, kt=kt, cmk=cmk: nc.vector.tensor_scalar(out=cmk[0:nk, 0:nq], in0=pqr[0:nk, q0:q0 + nq], scalar1=kt['pkc'], scalar2=None, op0=ALU.is_ge),
                 [b_pqr, b_cst], [b_cmk])
            for s_ in range(2):
                c0 = s_ * 512
                S.op('pool', lambda nk=nk, PT=PT, cmk=cmk, c0=c0: nc.gpsimd.tensor_tensor(out=PT[0:nk, c0:c0 + W4a].rearrange("p (a q) -> p a q", a=4), in0=PT[0:nk, c0:c0 + W4a].rearrange("p (a q) -> p a q", a=4),
                                                                                        in1=cmk[0:nk, 0:nq].unsqueeze(1).to_broadcast([nk, 4, nq]), op=ALU.mult), [b_PT, b_cmk], [b_PT])

            def g(kt=kt, nk=nk, PT=PT, ti=ti):
                last = None
                for s in range(2):
                    for h in range(4):
                        c = s * 512 + h * nq
                        nc.tensor.matmul(O[:, c:c + nq], lhsT=kt['va'][h], rhs=PT[0:nk, c:c + nq],
                                         start=(ti == 0 and h == 0), stop=(ti == nkt - 1), skip_group_check=True)
                for s in range(2):
                    c0 = s * 512
                    last = nc.tensor.matmul(L[:, c0:c0 + W4a], lhsT=onesb[0:nk, :], rhs=PT[0:nk, c0:c0 + W4a], start=(ti == 0), stop=(ti == nkt - 1))
                return last
            S.op('pe', g, kt['bufs'] + [b_PT, b_onesb], [b_O, b_L])
        chk(2)
        for s_ in range(2):
            c0 = s_ * 512
            S.op('dve', lambda c0=c0: nc.vector.reciprocal(out=f1[:, c0:c0 + W4a], in_=L[:, c0:c0 + W4a]), [b_L], [b_f1])
            S.op('dve', lambda c0=c0: nc.vector.tensor_tensor(out=f2[:, c0:c0 + W4a], in0=O[:, c0:c0 + W4a], in1=f1[:, c0:c0 + W4a], op=ALU.mult), [b_O, b_f1], [b_f2])
        f2s0 = f2[:, 0:W4a].rearrange("p (h q) -> p h q", h=4)
        f2s1 = f2[:, 512:512 + W4a].rearrange("p (h q) -> p h q", h=4)
        f3v = f3[:, 0:4 * nq].rearrange("p (h q) -> p h q", h=4)
        S.op('dve', lambda: nc.vector.scalar_tensor_tensor(out=f3v, in0=f2s1, scalar=small[:, 0:1], in1=f2s0, op0=ALU.mult, op1=ALU.add), [b_f2, b_small], [b_f3])
        S.op('act', lambda: nc.scalar.activation(out=f4[:, 0:4 * nq], in_=f3[:, 0:4 * nq], func=AF.Square), [b_f3], [b_f4])
        S.op('pe', lambda: nc.tensor.matmul(SA0[:, 0:4 * nq], lhsT=ones128, rhs=f4[:, 0:4 * nq], start=True, stop=True), [b_f4, b_ones128], [b_SA0])
        S.op('dve', lambda: nc.vector.tensor_scalar(out=f1[:, 0:4 * nq], in0=SA0[:, 0:4 * nq], scalar1=EPS, scalar2=None, op0=ALU.add), [b_SA0], [b_f1])
        S.op('act', lambda: nc.scalar.activation(out=f1[:, 0:4 * nq], in_=f1[:, 0:4 * nq], func=AF.Sqrt), [b_f1], [b_f1])
        S.op('dve', lambda: nc.vector.reciprocal(out=f1[:, 0:4 * nq], in_=f1[:, 0:4 * nq]), [b_f1], [b_f1])
        S.op('dve', lambda: nc.vector.tensor_tensor(out=f3[:, 0:4 * nq], in0=f3[:, 0:4 * nq], in1=f1[:, 0:4 * nq], op=ALU.mult), [b_f3, b_f1], [b_f3])
        S.op('dve', lambda: nc.vector.tensor_scalar(out=aT[:, 0:4, q0:q0 + nq], in0=f3v, scalar1=small[:, 1:2], scalar2=None, op0=ALU.mult), [b_f3, b_small], [b_aT])

        chk(3)
        (D0, b_D0), (D1, b_D1), (TPp, b_TP), (OL, b_OL) = psb
        Ds = [(D0, b_D0), (D1, b_D1)]
        di = 0
        col = 0
        blocks = []
        for kt in ktiles:
            if blocks and blocks[-1][1] + kt['nk'] <= 512 and blocks[-1][3] is kt['kiT_base']:
                blocks[-1][1] += kt['nk']
                blocks[-1][4] += kt['bufs']
            else:
                blocks.append([col, kt['nk'], kt['kiT_off'], kt['kiT_base'], list(kt['bufs'])])
            col += kt['nk']
        Ltot = col
        for (c0, bw, koff, kbase, kbufs) in blocks:
            for h in range(8):
                D, b_D = Ds[di % 2]
                rt, b_rt = rtmp[di % 2]
                di += 1
                S.op('pe', lambda D=D, h=h, bw=bw, koff=koff, kbase=kbase: nc.tensor.matmul(D[0:nq, 0:bw], lhsT=qiT[0:64, h, q0:q0 + nq], rhs=kbase[0:64, koff:koff + bw], start=True, stop=True),
                     kbufs + [b_qiT], [b_D])
                S.op('act', lambda D=D, rt=rt, bw=bw: nc.scalar.activation(out=rt[0:nq, 0:bw], in_=D[0:nq, 0:bw], func=AF.Relu), [b_D], [b_rt])
                if h == 0:
                    S.op('dve', lambda rt=rt, c0=c0, bw=bw: nc.vector.tensor_scalar(out=score[0:nq, c0:c0 + bw], in0=rt[0:nq, 0:bw], scalar1=wtokv[0:nq, 0:1], scalar2=None, op0=ALU.mult),
                         [b_rt, b_wtok, b_wtok_s], [b_score])
                else:
                    S.op('dve', lambda rt=rt, c0=c0, bw=bw, h=h: nc.vector.scalar_tensor_tensor(out=score[0:nq, c0:c0 + bw], in0=rt[0:nq, 0:bw], scalar=wtokv[0:nq, h:h + 1],
                                                                                          in1=score[0:nq, c0:c0 + bw], op0=ALU.mult, op1=ALU.add), [b_rt, b_wtok, b_wtok_s, b_score], [b_score])
        chk(3.3)
        S.op('dve', lambda: nc.vector.tensor_reduce(out=bis[0:nq, 0:1], in_=score[0:nq, 0:Ltot], axis=AX.X, op=ALU.max), [b_score], [b_bis])
        S.op('dve', lambda: nc.vector.tensor_reduce(out=bis[0:nq, 1:2], in_=score[0:nq, 0:Ltot], axis=AX.X, op=ALU.min), [b_score, b_bis], [b_bis])
        S.op('dve', lambda: nc.vector.tensor_tensor(out=bis[0:nq, 2:3], in0=bis[0:nq, 0:1], in1=bis[0:nq, 1:2], op=ALU.subtract), [b_bis], [b_bis])
        S.op('dve', lambda: nc.vector.tensor_scalar(out=bis[0:nq, 2:3], in0=bis[0:nq, 2:3], scalar1=2.0, scalar2=None, op0=ALU.add), [b_bis], [b_bis])
        S.op('dve', lambda: nc.vector.tensor_scalar(out=bis[0:nq, 8:8 + NBIS + 2], in0=pow2[0:nq, 0:NBIS + 2], scalar1=bis[0:nq, 2:3], scalar2=None, op0=ALU.mult), [b_bis, b_cst], [b_bis])
        S.op('dve', lambda: nc.vector.scalar_tensor_tensor(out=bis[0:nq, 3:4], in0=bis[0:nq, 1:2], scalar=-1.0, in1=bis[0:nq, 8:9], op0=ALU.add, op1=ALU.add), [b_bis], [b_bis])
        S.op('dve', lambda: nc.vector.tensor_scalar(out=pqb[0:nq, 0:16], in0=blkoff[0:nq, 0:16], scalar1=-1.0, scalar2=pqcol, op0=ALU.mult, op1=ALU.add), [b_cst, b_posq_col], [b_pqb])
        bi = 0
        for c0 in range(0, Ltot, 512):
            bw = min(512, Ltot - c0)
            rt, b_rt = rtmp[bi % 2]
            S.op('pool', lambda rt=rt, bw=bw, bi=bi: nc.gpsimd.tensor_scalar(out=rt[0:nq, 0:bw], in0=ramp512[0:nq, 0:bw], scalar1=pqb[0:nq, bi:bi + 1], scalar2=0.0, op0=ALU.subtract, op1=ALU.max),
                 [b_cst, b_pqb], [b_rt])
            S.op('dve', lambda rt=rt, bw=bw, c0=c0: nc.vector.scalar_tensor_tensor(out=score[0:nq, c0:c0 + bw], in0=rt[0:nq, 0:bw], scalar=NEG, in1=score[0:nq, c0:c0 + bw], op0=ALU.mult, op1=ALU.add),
                 [b_rt, b_score], [b_score])
            bi += 1
        chk(3.5)
        for it in range(NBIS):
            S.op('dve', lambda: nc.vector.tensor_scalar(out=maskb[0:nq, 0:Ltot], in0=score[0:nq, 0:Ltot], scalar1=bis[0:nq, 3:4], scalar2=None, op0=ALU.is_ge, op1=ALU.add, accum_out=bis[0:nq, 4:5]),
                 [b_score, b_bis], [b_maskb, b_bis])
            S.op('dve', lambda it=it: nc.vector.tensor_scalar(out=bis[0:nq, 5:6], in0=bis[0:nq, 4:5], scalar1=float(Ktop) - 0.5, scalar2=bis[0:nq, 8 + it:9 + it], op0=ALU.is_ge, op1=ALU.mult), [b_bis], [b_bis])
            S.op('dve', lambda it=it: nc.vector.scalar_tensor_tensor(out=bis[0:nq, 3:4], in0=bis[0:nq, 5:6], scalar=bis[0:nq, 9 + it:10 + it], in1=bis[0:nq, 3:4], op0=ALU.subtract, op1=ALU.add), [b_bis], [b_bis])
        S.op('dve', lambda: nc.vector.tensor_tensor(out=bis[0:nq, 6:7], in0=bis[0:nq, 3:4], in1=bis[0:nq, 8 + NBIS:9 + NBIS], op=ALU.subtract), [b_bis], [b_bis])
        S.op('dve', lambda: nc.vector.tensor_scalar(out=maskb[0:nq, 0:Ltot], in0=score[0:nq, 0:Ltot], scalar1=bis[0:nq, 6:7], scalar2=None, op0=ALU.is_ge), [b_score, b_bis], [b_maskb])
        chk(3.7)
        TPb = TPp.bitcast(BF16)
        col = 0
        for g0 in range(0, nkt, 8):
            g1 = min(nkt, g0 + 8)

            def f(g0=g0, g1=g1, col=col):
                last = None
                c = col
                for ti in range(g0, g1):
                    nk = ktiles[ti]['nk']
                    last = nc.tensor.transpose(TPb[0:nk, (ti - g0) * 128:(ti - g0) * 128 + nq], maskb[0:nq, c:c + nk], identb[0:nq, 0:nq])
                    c += nk
                return last
            S.op('pe', f, [b_maskb, b_identb], [b_TP])
            for ti in range(g0, g1):
                col += ktiles[ti]['nk']
            S.op('act', lambda g0=g0, g1=g1: nc.scalar.activation(out=maskT[:, g0:g1, 0:nq], in_=TPb[:, 0:(g1 - g0) * 128].rearrange("p (t q) -> p t q", q=128)[:, :, 0:nq], func=AF.Copy), [b_TP], [b_maskT])
        chk(4)
        W4 = 4 * nq
        for ti, kt in enumerate(ktiles):
            nk = kt['nk']
            D, b_D = Ds[ti % 2]
            PT, b_PT = PTs[ti % 2]

            def f(kt=kt, nk=nk, D=D):
                last = None
                for h in range(4):
                    last = nc.tensor.matmul(D[0:nk, h * nq:(h + 1) * nq], lhsT=kt['kbT'], rhs=qT[:, 4 + h, q0:q0 + nq], start=True, stop=True)
                return last
            S.op('pe', f, kt['bufs'] + [b_qT], [b_D])
            S.op('act', lambda nk=nk, D=D, PT=PT: nc.scalar.activation(out=PT[0:nk, 0:W4], in_=D[0:nk, 0:W4], func=AF.Exp, scale=128 ** -0.5), [b_D], [b_PT])
            S.op('pool', lambda nk=nk, PT=PT, ti=ti: nc.gpsimd.tensor_tensor(out=PT[0:nk, 0:W4].rearrange("p (a q) -> p a q", a=4), in0=PT[0:nk, 0:W4].rearrange("p (a q) -> p a q", a=4),
                                                                              in1=maskT[0:nk, ti, 0:nq].unsqueeze(1).to_broadcast([nk, 4, nq]), op=ALU.mult), [b_PT, b_maskT], [b_PT])

            def g(kt=kt, nk=nk, PT=PT, ti=ti):
                nc.tensor.matmul(OL[:, 0:W4], lhsT=kt['vb'], rhs=PT[0:nk, 0:W4], start=(ti == 0), stop=(ti == nkt - 1))
                return nc.tensor.matmul(OL[:, 512:512 + W4], lhsT=onesb[0:nk, :], rhs=PT[0:nk, 0:W4], start=(ti == 0), stop=(ti == nkt - 1))
            S.op('pe', g, kt['bufs'] + [b_PT, b_onesb], [b_OL])
        S.op('dve', lambda: nc.vector.reciprocal(out=f1[:, 0:W4], in_=OL[:, 512:512 + W4]), [b_OL], [b_f1])
        S.op('dve', lambda: nc.vector.tensor_tensor(out=aT[:, 4:8, q0:q0 + nq], in0=OL[:, 0:W4].rearrange("p (h q) -> p h q", h=4), in1=f1[:, 0:W4].rearrange("p (h q) -> p h q", h=4), op=ALU.mult),
             [b_OL, b_f1], [b_aT])

    wtokv = None

    def prompt_ktiles(n):
        kts = []
        for t in range(n):
            kts.append(dict(nk=128, kaT=[kaT[:, h, t * 128:(t + 1) * 128] for h in range(4)], kbT=kbT[:, t * 128:(t + 1) * 128],
                            kiT_base=kiT, kiT_off=t * 128, va=[va[:, t, h * 128:(h + 1) * 128] for h in range(4)], vb=vb[:, t, :],
                            pkc=poskc[:, t:t + 1], bufs=[b_kaT, b_kbT, b_kiT, b_va, b_vb]))
        return kts

    def run_group(kind, gi):
        nonlocal wtokv
        if kind == 'halo':
            n, c0, xsrc, tile0 = NH, 0, xT_q, 0
        elif kind == 'own':
            n, c0, xsrc, tile0 = GQ, NH + gi * GQ, xT_q, 1 + gi * 4
        else:
            n, c0, xsrc, tile0 = NS, 0, xT_s, NTQ - 1
        pc0 = c0 if kind != 'sample' else NQ
        dma('pool', xb[:, :, 0:n], xsrc[:, c0:c0 + n].rearrange("(k p) n -> p k n", p=128), [], [b_xb], "ld0")
        dma('sp', pqr[:, 0:n], posq_row_d[:, pc0:pc0 + n], [], [b_pqr], "ld1")
        def cons_q(base):
            def c(j, m, ps, b_ps):
                evac_act(qT[:, base + j, 0:n], ps[:, 0:n], [b_ps], [b_qT])
            return c
        proj_fm(w_in_e[:, QA:QA + 512], 512, xb, b_xb, n, 8, cons_q(0))
        proj_fm(w_in_e[:, QB:QB + 512], 512, xb, b_xb, n, 8, cons_q(4))
        wv, b_w = wload(w_in_e[:, QI:QI + 512], 8, 512)
        for h in range(8):
            ps, b_ps = psb[2 + (h % 2)]

            def f(h=h, ps=ps, wv=wv):
                last = None
                for k in range(8):
                    last = nc.tensor.matmul(ps[0:64, 0:n], lhsT=wv[:, k, h * 64:(h + 1) * 64], rhs=xb[:, k, 0:n], start=(k == 0), stop=(k == 7))
                return last
            S.op('pe', f, [b_w, b_xb], [b_ps])
            evac_act(qiT[0:64, h, 0:n], ps[0:64, 0:n], [b_ps], [b_qiT])
        ntile = (n + 127) // 128
        for t in range(ntile if kind != 'sample' else 0):
            tn = min(128, n - t * 128)
            ps, b_ps = psb[2 + (t % 2)]

            def f(t=t, tn=tn, ps=ps):
                last = None
                for k in range(8):
                    last = nc.tensor.matmul(ps[0:tn, 0:8], lhsT=xb[:, k, t * 128:t * 128 + tn], rhs=wwi[:, k, :], start=(k == 0), stop=(k == 7))
                return last
            S.op('pe', f, [b_wwi, b_xb], [b_ps])
            evac_act(wtok[0:tn, t, :], ps[0:tn, 0:8], [b_ps], [b_wtok], scale=(8 ** -0.5) / 8.0)
        chk(1)
        if kind != 'sample':
            for t in range(ntile):
                tn = min(128, n - t * 128)
                wtokv = wtok[:, t, :]
                if kind == 'halo':
                    nkeys = NKT
                else:
                    nkeys = min(NKT, 8 * (gi + 1))
                attention_tile(t * 128, tn, prompt_ktiles(nkeys), nkeys * 128, posq_col[0:tn, tile0 + t:tile0 + t + 1], KP)
        else:
            sample_attention(n)
        chk(5)
        dma('sp', xf[:, :, 0:n], xsrc[:, c0:c0 + n].rearrange("(k p) n -> p k n", p=128), [], [b_xf], "ld2")
        outproj(w_out_e, aT, b_aT, xf, b_xf, n)
        chk(6)
        layernorm(xf, b_xf, xb, b_xb, n, 0, lnscr)
        chk(7)
        mlp(0, xf, b_xf, xb, b_xb, n, hT, b_hT)
        chk(8)
        layernorm(xf, b_xf, xb, b_xb, n, 1, lnscr)
        chk(9)
        layer1_mixer(kind, gi, n)
        if kind == 'halo':
            return
        outproj(w_out_o, aT, b_aT, xf, b_xf, n)
        layernorm(xf, b_xf, xb, b_xb, n, 2, lnscr)
        mlp(1, xf, b_xf, xb, b_xb, n, hT, b_hT)
        layernorm(xf, b_xf, xb, b_xb, n, 3, lnscr)
        if kind == 'own':
            dma('sp', yT_q[:, gi * GQ:(gi + 1) * GQ].rearrange("(k p) n -> p k n", p=128), xf[:, :, 0:n], [b_xf], [OUTB], "st0")
        else:
            dma('sp', yT_s.rearrange("(k p) n -> p k n", p=128), xf[:, :, 0:n], [b_xf], [OUTB], "st0")

    def layer1_mixer(kind, gi, n):
        samp = (kind == 'sample')
        HIST = 16
        def cons_xc(j, m, ps, b_ps):
            if samp:
                for g in [j]:
                    S.op('act', lambda: nc.scalar.activation(out=xce[:, j, 0:NB * 20].rearrange("p (b t) -> p b t", t=20)[:, :, 16:20], in_=ps[:, 0:n].rearrange("p (b t) -> p b t", t=4), func=AF.Copy), [b_ps], [b_xce])
            else:
                evac_act(xce[:, j, HIST:HIST + n], ps[:, 0:n], [b_ps], [b_xce])
        proj_fm(w_in_o[:, 0:512], 512, xb, b_xb, n, 8, cons_xc)
        if kind == 'halo':
            for g in range(4):
                S.op('dve', lambda g=g: nc.vector.tensor_tensor(out=xch[:, g, 0:n], in0=xce[:, g, HIST:HIST + n], in1=hval[:, 0:n], op=ALU.mult), [b_xce, b_hval], [b_xch])
            return
        if samp:
            for g in range(4):
                dma('sp', xce[:, g, 0:NB * 20].rearrange("p (b t) -> p b t", t=20)[:, :, 1:16], stateT[g * 128:(g + 1) * 128, :, :], [], [b_xce], "ld3")
            for g in range(4):
                dma('sp', poolT_s[g * 128:(g + 1) * 128, :, :], xce[:, g, 0:NB * 20].rearrange("p (b t) -> p b t", t=20)[:, :, 5:20], [b_xce], [OUTB], "st1")
        else:
            for g in range(4):
                S.op('dve', lambda g=g: nc.vector.tensor_copy(out=xce[:, g, 0:HIST], in_=xch[:, g, gi * 16:(gi + 1) * 16]), [b_xch], [b_xce])
            if gi == NG - 1:
                for g in range(4):
                    dma('sp', xc_tail[g * 128:(g + 1) * 128, :], xce[:, g, HIST + n - 16:HIST + n], [b_xce], [OUTB], "st1")
        for g in range(4):
            w = 2 << g
            if samp:
                E = NB * 20
                src = xce[:, g, 0:E].rearrange("p (b t) -> p b t", t=20)
                cur, b_cur = src, b_xce
                sh = 1
                lo = 0
                for lvl in range(g + 1):
                    dst, b_dst = lv[lvl % 2]
                    dstv = dst[:, 0:E].rearrange("p (b t) -> p b t", t=20)
                    lo += sh
                    S.op('dve', lambda cur=cur, dstv=dstv, lo=lo, sh=sh: nc.vector.tensor_tensor(out=dstv[:, :, lo:20], in0=cur[:, :, lo:20], in1=cur[:, :, lo - sh:20 - sh], op=ALU.add), [b_cur], [b_dst])
                    cur, b_cur = dstv, b_dst
                    sh *= 2
                pv = poolb[:, g, 0:n].rearrange("p (b t) -> p b t", t=4)
                S.op('dve', lambda cur=cur, pv=pv, src=src, w=w: nc.vector.scalar_tensor_tensor(out=pv, in0=cur[:, :, 16:20], scalar=1.0 / w, in1=src[:, :, 16:20], op0=ALU.mult, op1=ALU.subtract), [b_cur, b_xce], [b_poolb])
            else:
                E = HIST + n
                src = xce[:, g, 0:E]
                cur, b_cur = src, b_xce
                sh = 1
                lo = 0
                for lvl in range(g + 1):
                    dst, b_dst = lv[lvl % 2]
                    lo += sh
                    S.op('dve', lambda cur=cur, dst=dst, lo=lo, sh=sh, E=E: nc.vector.tensor_tensor(out=dst[:, lo:E], in0=cur[:, lo:E], in1=cur[:, lo - sh:E - sh], op=ALU.add), [b_cur], [b_dst])
                    cur, b_cur = dst[:, 0:E], b_dst
                    sh *= 2
                S.op('dve', lambda cur=cur, src=src, w=w, g=g: nc.vector.scalar_tensor_tensor(out=poolb[:, g, 0:n], in0=cur[:, HIST:HIST + n], scalar=1.0 / w, in1=src[:, HIST:HIST + n], op0=ALU.mult, op1=ALU.subtract),
                     [b_cur, b_xce], [b_poolb])
                if gi == 0:
                    dstl, b_dl = lv[(g + 1) % 2]
                    S.op('dve', lambda cur=cur, dstl=dstl, g=g: nc.vector.tensor_tensor(out=dstl[:, 0:16], in0=cur[:, HIST:HIST + 16], in1=invc[:, g, :], op=ALU.mult), [b_cur, b_invc], [b_dl])
                    S.op('dve', lambda dstl=dstl, src=src, g=g: nc.vector.tensor_tensor(out=poolb[:, g, 0:16], in0=dstl[:, 0:16], in1=src[:, HIST:HIST + 16], op=ALU.subtract), [b_dl, b_xce], [b_poolb])
        for g in range(4):
            ps, b_ps = psb[2 + (g % 2)]
            S.op('pe', lambda g=g, ps=ps: nc.tensor.matmul(ps[:, 0:n], lhsT=wpool_b[:, g, :], rhs=poolb[:, g, 0:n], start=True, stop=True), [b_wpool, b_poolb], [b_ps])
            S.op('dve', lambda g=g, ps=ps: nc.vector.tensor_scalar(out=aT[:, g, 0:n], in0=ps[:, 0:n], scalar1=pscT[:, g:g + 1], scalar2=None, op0=ALU.mult), [b_ps, b_pscT], [b_aT])
        def cons_u(j, m, ps, b_ps):
            evac_act(uT[:, j, 0:n], ps[:, 0:n], [b_ps], [b_uT], func=AF.Gelu)
        proj_fm(w_in_o[:, 512:1024], 512, xb, b_xb, n, 8, cons_u)
        wv, b_w = wload(w_in_o[:, 1024:1536], 8, 512)
        ntile = (n + 127) // 128
        for t in range(ntile):
            tn = min(128, n - t * 128)
            ps, b_ps = psb[2 + (t % 2)]

            def f(t=t, tn=tn, ps=ps, wv=wv):
                last = None
                for k in range(8):
                    last = nc.tensor.matmul(ps[0:tn, 0:512], lhsT=xb[:, k, t * 128:t * 128 + tn], rhs=wv[:, k, :], start=(k == 0), stop=(k == 7))
                return last
            S.op('pe', f, [b_w, b_xb], [b_ps])
            vt = vtok[0:tn, t, :]
            st = vst[0:tn, t, :]
            S.op('act', lambda ps=ps, tn=tn, vt=vt, st=st: nc.scalar.activation(out=vt, in_=ps[0:tn, 0:512], func=AF.Gelu, accum_out=st[:, 0:1]), [b_ps], [b_vtok, b_vst])
            S.op('dve', lambda st=st: nc.vector.tensor_scalar(out=st[:, 1:2], in0=st[:, 0:1], scalar1=1.0 / 512, scalar2=None, op0=ALU.mult), [b_vst], [b_vst])
            S.op('dve', lambda vt=vt, st=st: nc.vector.tensor_scalar(out=vt, in0=vt, scalar1=st[:, 1:2], scalar2=None, op0=ALU.subtract), [b_vtok, b_vst], [b_vtok])
            lvt, b_lvt = lv[t % 2]
            S.op('act', lambda vt=vt, tn=tn, lvt=lvt, st=st: nc.scalar.activation(out=lvt[0:tn, 0:512], in_=vt, func=AF.Square, accum_out=st[:, 2:3]), [b_vtok], [b_lvt, b_vst])
            S.op('dve', lambda st=st: nc.vector.tensor_scalar(out=st[:, 3:4], in0=st[:, 2:3], scalar1=1.0 / 512, scalar2=EPS, op0=ALU.mult, op1=ALU.add), [b_vst], [b_vst])
            S.op('act', lambda st=st: nc.scalar.activation(out=st[:, 4:5], in_=st[:, 3:4], func=AF.Sqrt), [b_vst], [b_vst])
            S.op('dve', lambda st=st: nc.vector.reciprocal(out=st[:, 5:6], in_=st[:, 4:5]), [b_vst], [b_vst])
            S.op('dve', lambda vt=vt, st=st, tn=tn: nc.vector.scalar_tensor_tensor(out=vt, in0=vt, scalar=st[:, 5:6], in1=sgugb[0:tn, 0:512], op0=ALU.mult, op1=ALU.mult), [b_vtok, b_vst, b_sgugb], [b_vtok])
            S.op('dve', lambda vt=vt, tn=tn: nc.vector.tensor_tensor(out=vt, in0=vt, in1=sgugb[0:tn, 512:1024], op=ALU.add), [b_vtok, b_sgugb], [b_vtok])
            S.op('act', lambda vt=vt, tn=tn, t=t: nc.scalar.activation(out=vnb[0:tn, t, :], in_=vt, func=AF.Copy), [b_vtok], [b_vnb])
            if samp:
                dma('sp', vn_s_o[:, :], vt, [b_vtok], [OUTB], "st1")
            for g in range(4):
                ps2, b_ps2 = psb[g % 2]
                if samp:
                    S.op('pe', lambda g=g, ps2=ps2, tn=tn, t=t: nc.tensor.matmul(ps2[:, 0:tn], lhsT=vnb[0:tn, t, g * 128:(g + 1) * 128], rhs=wsTs_b[0:tn, g, 0:tn], start=True, stop=True), [b_vnb, b_wsTs], [b_ps2])
                    bias = bsrs[:, g, 0:tn]
                    b_bias = b_bsrs
                else:
                    S.op('pe', lambda g=g, ps2=ps2, tn=tn, t=t: nc.tensor.matmul(ps2[:, 0:tn], lhsT=vnb[0:tn, t, g * 128:(g + 1) * 128], rhs=wsT_b[0:tn, g, 0:tn], start=True, stop=True), [b_vnb, b_wsT], [b_ps2])
                    bias = bsr[:, g, 0:tn]
                    b_bias = b_bsr
                lvt2, b_lvt2 = lv[g % 2]
                S.op('dve', lambda ps2=ps2, tn=tn, bias=bias, lvt2=lvt2: nc.vector.tensor_tensor(out=lvt2[:, 0:tn], in0=ps2[:, 0:tn], in1=bias, op=ALU.add), [b_ps2, b_bias], [b_lvt2])
                S.op('dve', lambda g=g, tn=tn, t=t, lvt2=lvt2: nc.vector.tensor_tensor(out=aT[:, 4 + g, t * 128:t * 128 + tn], in0=lvt2[:, 0:tn], in1=uT[:, g, t * 128:t * 128 + tn], op=ALU.mult), [b_lvt2, b_uT], [b_aT])

    def sample_attention(n):
        nonlocal wtokv
        kn, b_kn = f2[:, 0:6 * NS].rearrange("p (j t) -> p j t", j=6), b_f2
        knb, b_knb = A_knb
        wv, b_w = wload(w_in_e[:, KA:KA + 512], 8, 512)
        for j in range(4):
            ps, b_ps = psb[2 + (j % 2)]

            def f(j=j, ps=ps, wv=wv):
                last = None
                for k in range(8):
                    last = nc.tensor.matmul(ps[:, 0:n], lhsT=wv[:, k, j * 128:(j + 1) * 128], rhs=xb[:, k, 0:n], start=(k == 0), stop=(k == 7))
                return last
            S.op('pe', f, [b_w, b_xb], [b_ps])
            evac_act(knb[:, j, 0:n], ps[:, 0:n], [b_ps], [b_knb])
            S.op('dve', lambda j=j, ps=ps: nc.vector.tensor_copy(out=kn[:, j, 0:n], in_=ps[:, 0:n]), [b_ps], [b_kn])
        wv, b_w = wload(w_in_e[:, KB:KB + 128], 8, 128)
        ps, b_ps = psb[2]

        def f(ps=ps, wv=wv):
            last = None
            for k in range(8):
                last = nc.tensor.matmul(ps[:, 0:n], lhsT=wv[:, k, :], rhs=xb[:, k, 0:n], start=(k == 0), stop=(k == 7))
            return last
        S.op('pe', f, [b_w, b_xb], [b_ps])
        evac_act(knb[:, 4, 0:n], ps[:, 0:n], [b_ps], [b_knb])
        S.op('dve', lambda ps=ps: nc.vector.tensor_copy(out=kn[:, 4, 0:n], in_=ps[:, 0:n]), [b_ps], [b_kn])
        wv, b_w = wload(w_in_e[:, KI:KI + 64], 8, 64)
        ps, b_ps = psb[3]

        def f(ps=ps, wv=wv):
            last = None
            for k in range(8):
                last = nc.tensor.matmul(ps[0:64, 0:n], lhsT=wv[:, k, :], rhs=xb[:, k, 0:n], start=(k == 0), stop=(k == 7))
            return last
        S.op('pe', f, [b_w, b_xb], [b_ps])
        evac_act(knb[0:64, 5, 0:n], ps[0:64, 0:n], [b_ps], [b_knb])
        S.op('dve', lambda ps=ps: nc.vector.tensor_copy(out=kn[0:64, 5, 0:n], in_=ps[0:64, 0:n]), [b_ps], [b_kn])
        for j in range(6):
            m = 128 if j < 5 else 64
            dma('sp', kT_s_o[j * 128:j * 128 + m, :], kn[0:m, j, 0:n], [b_kn], [OUTB], "st0")
        wvv, b_wvv = wload(w_in_e[:, VA:VA + 512], 8, 512)
        wvb, b_wvb = wload(w_in_e[:, VB:VB + 128], 8, 128)
        idx, b_idx = A_idx
        pa, b_pa = A_pa
        pb, b_pb = A_pb
        vnew, b_vnew = A_vnew
        vnf, b_vnf = A_vnf
        TPb = psb[2][0].bitcast(BF16)
        b_TP = psb[2][1]
        for b in range(NB):
            def gat(b=b):
                out = []
                for j in range(NPAGE):
                    bj = b * NPAGE + j
                    out.append(nc.gpsimd.indirect_dma_start(out=pa[:, j, :], out_offset=None, in_=cache_a, in_offset=bass.IndirectOffsetOnAxis(ap=idx[:, bj:bj + 1], axis=0)))
                    out.append(nc.gpsimd.indirect_dma_start(out=pb[:, j, :], out_offset=None, in_=cache_b, in_offset=bass.IndirectOffsetOnAxis(ap=idx[:, bj:bj + 1], axis=0)))
                return out
            S.op('pool', gat, [b_idx], [b_pa, b_pb], sem="gath", inc=16, multi=True)
            for j in range(NPAGE):
                def tr(j=j):
                    last = None
                    for h in range(4):
                        nc.tensor.transpose(TPb[:, h * 128:(h + 1) * 128], pa[:, j, h * 256:h * 256 + 128], identb)
                    nc.tensor.transpose(TPb[:, 512:640], pb[:, j, 0:128], identb)
                    last = nc.tensor.transpose(TPb[0:64, 640:768], pb[:, j, 256:320], identb)
                    return last
                S.op('pe', tr, [b_pa, b_pb, b_identb], [b_TP])
                S.op('act', lambda j=j: nc.scalar.activation(out=kaTs[:, :, j * 128:(j + 1) * 128], in_=TPb[:, 0:512].rearrange("p (h k) -> p h k", h=4), func=AF.Copy), [b_TP], [b_kaTs])
                S.op('dve', lambda j=j: nc.vector.tensor_copy(out=kbTs[:, j * 128:(j + 1) * 128], in_=TPb[:, 512:640]), [b_TP], [b_kbTs])
                S.op('dve', lambda j=j: nc.vector.tensor_copy(out=kiTs[0:64, j * 128:(j + 1) * 128], in_=TPb[0:64, 640:768]), [b_TP], [b_kiTs])
            S.op('act', lambda b=b: nc.scalar.activation(out=kaTs[:, :, PAST:PAST + 4], in_=knb[:, 0:4, 4 * b:4 * b + 4], func=AF.Copy), [b_knb], [b_kaTs])
            S.op('dve', lambda b=b: nc.vector.tensor_copy(out=kbTs[:, PAST:PAST + 4], in_=knb[:, 4, 4 * b:4 * b + 4]), [b_knb], [b_kbTs])
            S.op('dve', lambda b=b: nc.vector.tensor_copy(out=kiTs[0:64, PAST:PAST + 4], in_=knb[0:64, 5, 4 * b:4 * b + 4]), [b_knb], [b_kiTs])
            ps, b_ps = psb[3]

            def fv(b=b, ps=ps):
                last = None
                for k in range(8):
                    nc.tensor.matmul(ps[0:4, 0:512], lhsT=xb[:, k, 4 * b:4 * b + 4], rhs=wvv[:, k, :], start=(k == 0), stop=(k == 7))
                for k in range(8):
                    nc.tensor.matmul(ps[0:4, 512:640], lhsT=xb[:, k, 4 * b:4 * b + 4], rhs=wvb[:, k, :], start=(k == 0), stop=(k == 7))
                for k in range(8):
                    last = nc.tensor.matmul(ps[0:4, 640:648], lhsT=xb[:, k, 4 * b:4 * b + 4], rhs=wwi[:, k, :], start=(k == 0), stop=(k == 7))
                return last
            S.op('pe', fv, [b_wvv, b_wvb, b_wwi, b_xb], [b_ps])
            evac_act(wtok_s[0:4, b, :], ps[0:4, 640:648], [b_ps], [b_wtok_s], scale=(8 ** -0.5) / 8.0)
            evac_act(vnew[0:4, 0:512], ps[0:4, 0:512], [b_ps], [b_vnew])
            evac_act(vnew[0:4, 512:640], ps[0:4, 512:640], [b_ps, b_vnew], [b_vnew])
            S.op('dve', lambda ps=ps: nc.vector.tensor_copy(out=vnf[0:4, 0:512], in_=ps[0:4, 0:512]), [b_ps], [b_vnf])
            S.op('dve', lambda ps=ps: nc.vector.tensor_copy(out=vnf[0:4, 512:640], in_=ps[0:4, 512:640]), [b_ps, b_vnf], [b_vnf])
            dma('sp', v_s_o[b, :, :], vnf[0:4, :], [b_vnf], [OUTB], "st1")
            kts = []
            for j in range(NPAGE):
                kts.append(dict(nk=128, kaT=[kaTs[:, h, j * 128:(j + 1) * 128] for h in range(4)], kbT=kbTs[:, j * 128:(j + 1) * 128],
                                kiT_base=kiTs, kiT_off=j * 128, va=[pa[:, j, h * 256 + 128:h * 256 + 256] for h in range(4)], vb=pb[:, j, 128:256],
                                pkc=poskc[:, j:j + 1], bufs=[b_kaTs, b_kbTs, b_kiTs, b_pa, b_pb]))
            kts.append(dict(nk=4, kaT=[kaTs[:, h, PAST:PAST + 4] for h in range(4)], kbT=kbTs[:, PAST:PAST + 4], kiT_base=kiTs, kiT_off=PAST,
                            va=[vnew[0:4, h * 128:(h + 1) * 128] for h in range(4)], vb=vnew[0:4, 512:640], pkc=poskc[0:4, NPAGE:NPAGE + 1],
                            bufs=[b_kaTs, b_kbTs, b_kiTs, b_vnew]))
            wtokv = wtok_s[:, b, :]
            attention_tile(4 * b, 4, kts, PAST + 4, posq_col[0:4, NTQ - 1:NTQ], KS)

    sv = A.top
    A.top = 0
    LS = PAST + 128
    kaTs, b_kaTs = A.alloc("kaTs", [4, LS], BF16)
    kbTs, b_kbTs = A.alloc("kbTs", [LS], BF16)
    kiTs, b_kiTs = A.alloc("kiTs", [LS], BF16)
    A_pa = A.alloc("pa", [NPAGE, 1024], BF16)
    A_pb = A.alloc("pb", [NPAGE, 320], BF16)
    assert A.top <= (4 * LCAP + 2 * LCAP + (LCAP // 128) * 640) * 2, "sample buffers exceed dead prompt K/V region"
    A.top = sv
    A_knb = A.alloc("knb", [6, 64], BF16)
    A_vnew = A.alloc("vnew", [640], BF16)
    A_vnf = A.alloc("vnf", [640], F32)
    wtok_s, b_wtok_s = A.alloc("wtok_s", [NB, 8], F32)

    GROUPS = cfg.get('GROUPS', 'hos')
    try:
        if 'h' in GROUPS:
            run_group('halo', 0)
    except _Stop:
        return finish_all()
    if STOP <= 2:
        return finish_all()
    for gi in cfg.get('GILIST', range(NG if 'o' in GROUPS else 0)):
        run_group('own', gi)
    if STOP <= 3:
        return finish_all()
    if 's' in GROUPS:
        run_group('sample', 0)
    for sname, v in S.cnt.items():
        if v > 0 and S.seen['sp'].get(sname, 0) < v:
            nc.sync.wait_ge(S.sems[sname], v)
    return nc, dict(peak=A.peak)


def _consts():
    c = np.zeros((128, 992), np.float32)
    c[:, 0:512] = np.arange(512, dtype=np.float32)[None, :]
    c[:, 512:576] = (512.0 * np.arange(64, dtype=np.float32))[None, :]
    c[:, 576:608] = (0.5 ** (np.arange(32) + 1)).astype(np.float32)[None, :]
    c[:, 608:736] = np.eye(128, dtype=np.float32)
    c[:, 736:864] = (np.arange(128)[:, None] <= np.arange(128)[None, :]).astype(np.float32)
    c[:, 864:992] = 128.0 * np.arange(128, dtype=np.float32)[None, :] + np.arange(128, dtype=np.float32)[:, None]
    return c


def make_in_maps(cfg, inp):
    SEQ, NB, NPAGE, NPOOL = cfg['SEQ'], cfg['NB'], cfg['NPAGE'], cfg['NPOOL']
    NCORE = cfg['NCORE']
    PAST = NPAGE * 128
    NG = SEQ // 1024
    TOWN, NH, NS = NG * 512, 16 * NG, NB * 4
    NQ = NH + TOWN
    NTQ = 1 + TOWN // 128 + 1
    f = lambda a: np.ascontiguousarray(a, dtype=np.float32)
    xp = np.asarray(inp['x_prompt'])
    xs = np.asarray(inp['x_sample'])
    ca = np.asarray(inp['cache_a'])[0].reshape(NPOOL * 128, 1024)
    cb = np.asarray(inp['cache_b'])[0].reshape(NPOOL * 128, 320)
    pt = np.asarray(inp['page_table']).astype(np.int32)
    st = np.asarray(inp['state_pool'])[0]
    w_s = np.asarray(inp['w_s'])[0]
    b_s = np.asarray(inp['b_s'])[0]
    shared = dict(
        consts=_consts(), cache_a=f(ca), cache_b=f(cb),
        w_in_e=f(inp['w_in_e'][0]), lam_rep=f(np.broadcast_to(np.asarray(inp['lam_e'])[0].reshape(1, 256), (128, 256))),
        sublng=f(np.asarray(inp['subln_g'])[0].reshape(128, 1)), w_out_e=f(inp['w_out_e'][0]), w_in_o=f(inp['w_in_o'][0]), w_out_o=f(inp['w_out_o'][0]),
        w_pool=f(inp['w_pool'][0]), pool_scaleT=f(np.asarray(inp['pool_scale'])[0].reshape(4, 128).T),
        sgu_gb=f(np.broadcast_to(np.concatenate([np.asarray(inp['sgu_g'])[0], np.asarray(inp['sgu_b'])[0]])[None, :], (128, 1024))),
        w_sT=f(np.transpose(w_s, (2, 0, 1))),
        w_sT_s=f(np.tile(np.transpose(w_s[:, :4, :4], (2, 0, 1)), (16, 1, 16))),
        mask_s=f(np.kron(np.eye(16), (np.arange(4)[:, None] <= np.arange(4)[None, :]).astype(np.float32))),
        bs_rep=f(np.broadcast_to(b_s[None, :, :], (128, 4, 128))),
        bs_rep_s=f(np.broadcast_to(np.tile(b_s[:, :4], (1, 16))[None, :, :], (128, 4, 64))),
        ln_gb=f(np.concatenate([np.asarray(inp['ln_g']).reshape(4, 8, 128).transpose(2, 0, 1).reshape(128, 32),
                                np.asarray(inp['ln_b']).reshape(4, 8, 128).transpose(2, 0, 1).reshape(128, 32)], 1)),
        w_mlp1=f(inp['w_mlp1']), w_mlp2=f(inp['w_mlp2']),
    )
    maps = []
    for c in range(NCORE):
        s, h = c // 2, c % 2
        xT = xp[s].T
        own_pos = np.concatenate([np.arange(1024 * i + 512 * h, 1024 * i + 512 * h + 512) for i in range(NG)])
        halo_pos = np.concatenate([np.arange(1024 * i + 512 * h - 16, 1024 * i + 512 * h) for i in range(NG)])
        hvalid = (halo_pos >= 0).astype(np.float32)
        halo_idx = np.maximum(halo_pos, 0)
        qpos = np.concatenate([halo_idx, own_pos])
        xT_q = xT[:, qpos]
        bsl = slice(NB * c, NB * (c + 1))
        xT_s = xs[bsl].reshape(NS, 1024).T
        spos = (PAST + np.tile(np.arange(4), NB)).astype(np.float32)
        posq_row = np.broadcast_to(np.concatenate([qpos.astype(np.float32), spos])[None, :], (128, NQ + NS))
        posq_col = np.zeros((128, NTQ), np.float32)
        posq_col[:NH, 0] = halo_idx
        posq_col[:, 1:1 + TOWN // 128] = own_pos.reshape(TOWN // 128, 128).T
        posq_col[:, NTQ - 1] = PAST + np.arange(128)
        hv = np.zeros((128, 64), np.float32)
        hv[:, :NH] = hvalid[None, :]
        invc = np.zeros((128, 4, 16), np.float32)
        for g in range(4):
            w = 2 << g
            p0 = own_pos[0] + np.arange(16)
            invc[:, g, :] = (1.0 / np.minimum(w, p0 + 1))[None, :]
        m = dict(shared)
        m.update(xT_seq=f(xT), xT_q=f(xT_q), xT_s=f(xT_s), posq_row=f(posq_row), posq_col=posq_col,
                 pt=np.ascontiguousarray(np.broadcast_to(pt[bsl].reshape(1, NB * NPAGE), (128, NB * NPAGE)).astype(np.int32)),
                 stateT=f(np.transpose(st[bsl], (2, 0, 1))), halo_valid=hv, invc=invc.reshape(128, 64))
        maps.append(m)
    return maps


def assemble(cfg, res, nbatch):
    SEQ, NB, NPAGE = cfg['SEQ'], cfg['NB'], cfg['NPAGE']
    NCORE = cfg['NCORE']
    NG = SEQ // 1024
    NS = NB * 4
    DB = NB * NCORE
    y_p = np.zeros((nbatch, SEQ, 1024), np.float32)
    y_s = np.zeros((DB, 4, 1024), np.float32)
    na_p = np.zeros((1, nbatch, SEQ, 4, 256), np.float32)
    nb_p = np.zeros((1, nbatch, SEQ, 320), np.float32)
    pool_p = np.zeros((1, nbatch, 15, 512), np.float32)
    na_s = np.zeros((1, DB, 4, 4, 256), np.float32)
    nb_s = np.zeros((1, DB, 4, 320), np.float32)
    pool_s = np.zeros((1, DB, 15, 512), np.float32)
    v_s = np.zeros((1, DB, 4, 512), np.float32)
    for c in range(NCORE):
        r = res[c]
        s, h = c // 2, c % 2
        for i in range(NG):
            y_p[s, 1024 * i + 512 * h:1024 * i + 512 * h + 512] = r['yT_q'][:, i * 512:(i + 1) * 512].T
        if h == 0:
            kT = r['kT_all']
            v = r['v_all']
            na_p[0, s, :, :, 0:128] = kT[0:512].T.reshape(SEQ, 4, 128)
            na_p[0, s, :, :, 128:256] = v[:, 0:512].reshape(SEQ, 4, 128)
            nb_p[0, s, :, 0:128] = kT[512:640].T
            nb_p[0, s, :, 128:256] = v[:, 512:640]
            nb_p[0, s, :, 256:320] = kT[640:704].T
        else:
            pool_p[0, s] = r['xc_tail'][:, 1:16].T
        bsl = slice(NB * c, NB * (c + 1))
        y_s[bsl] = r['yT_s'].T.reshape(NB, 4, 1024)
        kTs = r['kT_s']
        vs = r['v_s']
        na_s[0, bsl, :, :, 0:128] = kTs[0:512].T.reshape(NB, 4, 4, 128)
        na_s[0, bsl, :, :, 128:256] = vs[:, :, 0:512].reshape(NB, 4, 4, 128)
        nb_s[0, bsl, :, 0:128] = kTs[512:640].T.reshape(NB, 4, 128)
        nb_s[0, bsl, :, 128:256] = vs[:, :, 512:640]
        nb_s[0, bsl, :, 256:320] = kTs[640:704].T.reshape(NB, 4, 64)
        pool_s[0, bsl] = np.transpose(r['poolT_s'], (1, 2, 0))
        v_s[0, bsl] = r['vn_s'].reshape(NB, 4, 512)
    return (y_p, y_s, na_p, nb_p, pool_p, na_s, nb_s, pool_s, v_s)


_CACHE = {}


def run_cfg(cfg, inp, nbatch):
    key = tuple(sorted(cfg.items()))
    if key not in _CACHE:
        _CACHE[key] = build(cfg)[0]
    nc = _CACHE[key]
    maps = make_in_maps(cfg, inp)
    res = run_bass_kernel_spmd(nc, maps, core_ids=list(range(cfg['NCORE'])))
    return assemble(cfg, res.results, nbatch)


def kernel(**inputs):
    cfg = dict(SEQ=4096, NB=16, NPAGE=16, NPOOL=int(np.asarray(inputs['cache_a']).shape[1]), NCORE=8)
    return run_cfg(cfg, inputs, 4)
```

```python
import numpy as np
import concourse.bass as bass
import concourse.mybir as mybir
from concourse.bass_utils import run_bass_kernel_spmd

F32 = mybir.dt.float32
BF16 = mybir.dt.bfloat16
I32 = mybir.dt.int32
U32 = mybir.dt.uint32
ALU = mybir.AluOpType
AF = mybir.ActivationFunctionType
AX = mybir.AxisListType

QA, KA, VA, QB, KB, VB, QI, KI, WI = 0, 512, 1024, 1536, 2048, 2176, 2304, 2816, 2880
ALPHA = 4 ** 0.25
EPS = 1e-5
LAM_INIT0 = 0.8 - 0.6
NBIS = 18


class Buf:
    def __init__(self, name, lo=0, hi=0):
        self.name = name
        self.w = None
        self.r = {}
        self.lo, self.hi = lo, hi
        self.overlaps = []
        self.psum = False


class Sched:
    def __init__(self, nc):
        self.nc = nc
        self.eng = {'pe': nc.tensor, 'act': nc.scalar, 'dve': nc.vector, 'pool': nc.gpsimd, 'sp': nc.sync}
        self.sems = {}
        self.cnt = {}
        self.seen = {e: {} for e in self.eng}
        self.swdge_tag = None

    def newsem(self, name):
        if name not in self.sems:
            self.sems[name] = self.nc.alloc_semaphore(name)
            self.cnt[name] = 0
        return name

    def _wait(self, e, deps):
        best = {}
        for d in deps:
            if d is None:
                continue
            s, v = d
            if best.get(s, 0) < v:
                best[s] = v
        for s, v in best.items():
            if self.seen[e].get(s, 0) >= v:
                continue
            self.eng[e].wait_ge(self.sems[s], v)
            self.seen[e][s] = v

    def op(self, e, fn, reads=(), writes=(), sem=None, inc=1, multi=False):
        deps = set()
        for b in reads:
            deps.add(b.w)
            for o in b.overlaps:
                deps.add(o.w)
            if b.psum:
                deps.update(b.r.items())
        for b in writes:
            deps.add(b.w)
            deps.update(b.r.items())
            for o in b.overlaps:
                deps.add(o.w)
                deps.update(o.r.items())
        self._wait(e, deps)
        if sem is None:
            sem = self.newsem('c_' + e)
        if multi:
            inss = fn()
            for ins in inss:
                ins.then_inc(self.sems[sem], inc)
                self.cnt[sem] += inc
        else:
            ins = fn()
            ins.then_inc(self.sems[sem], inc)
            self.cnt[sem] += inc
        tag = (sem, self.cnt[sem])
        if e == 'pool' and inc == 16:
            self.swdge_tag = tag
        for b in writes:
            b.w = tag
            b.r = {}
        for b in reads:
            if b.r.get(sem, 0) < self.cnt[sem]:
                b.r[sem] = self.cnt[sem]

    def finish(self, e, bufs):
        deps = set()
        for b in bufs:
            deps.add(b.w)
            deps.update(b.r.items())
        self._wait(e, deps)


class WRef:
    def __init__(self, ap, bufs):
        self.ap, self.bufs = ap, bufs

    def __getitem__(self, key):
        return WRef(self.ap[key], self.bufs)


class Arena:
    def __init__(self, nc, nbytes):
        self.nbytes = nbytes
        self.t = nc.alloc_sbuf_tensor("arena", [128, nbytes // 2], BF16).ap()
        self.top = 0
        self.all = []
        self.peak = 0

    def alloc(self, name, free_shape, dtype):
        esz = 4 if dtype in (F32, I32, U32) else 2
        n = int(np.prod(free_shape))
        nb = (n * esz + 63) // 64 * 64
        off = self.top
        self.top += nb
        self.peak = max(self.peak, self.top)
        assert self.top <= self.nbytes, (name, self.top, self.nbytes)
        ap = self.t[:, off // 2:(off + n * esz) // 2]
        if esz == 4:
            ap = ap.bitcast(dtype)
        elif dtype != BF16:
            ap = ap.bitcast(dtype)
        if len(free_shape) == 2:
            ap = ap.rearrange("p (a b) -> p a b", a=free_shape[0])
        elif len(free_shape) == 3:
            ap = ap.rearrange("p (a b c) -> p a b c", a=free_shape[0], b=free_shape[1])
        b = Buf(name, off, off + nb)
        for o in self.all:
            if o.lo < b.hi and b.lo < o.hi:
                o.overlaps.append(b)
                b.overlaps.append(o)
        self.all.append(b)
        return ap, b


def build(cfg):
    SEQ, NB, NPAGE, NPOOL = cfg['SEQ'], cfg['NB'], cfg['NPAGE'], cfg['NPOOL']
    PAST = NPAGE * 128
    NG = SEQ // 1024
    GQ = 512
    TOWN = NG * GQ
    NH = 16 * NG
    NS = NB * 4
    NQ = NH + TOWN
    NKT = SEQ // 128
    LCAP = max(SEQ, PAST + 128)
    KP = min(256, SEQ // 4)
    KS = min(256, (PAST + 4) // 4)
    NTQ = 1 + TOWN // 128 + 1

    nc = bass.Bass("TRN2", target_bir_lowering=False)
    S = Sched(nc)

    def din(name, shape, dt=F32):
        return nc.dram_tensor(name, list(shape), dt, kind="ExternalInput").ap()

    def dout(name, shape, dt=F32):
        return nc.dram_tensor(name, list(shape), dt, kind="ExternalOutput").ap()

    xT_seq = din("xT_seq", [1024, SEQ])
    xT_q = din("xT_q", [1024, NQ])
    xT_s = din("xT_s", [1024, NS])
    posq_row_d = din("posq_row", [128, NQ + NS])
    posq_col_d = din("posq_col", [128, NTQ])
    consts_d = din("consts", [128, 512 + 64 + 32 + 128 + 128 + 128])
    cache_a = din("cache_a", [NPOOL * 128, 1024])
    cache_b = din("cache_b", [NPOOL * 128, 320])
    pt_d = din("pt", [128, NB * NPAGE], I32)
    stateT = din("stateT", [512, NB, 15])
    w_in_e = din("w_in_e", [1024, 2888])
    lam_rep = din("lam_rep", [128, 256])
    sublng = din("sublng", [128, 1])
    w_out_e = din("w_out_e", [1024, 1024])
    w_in_o = din("w_in_o", [1024, 1536])
    w_out_o = din("w_out_o", [1024, 1024])
    w_pool = din("w_pool", [4, 128, 128])
    pool_scaleT = din("pool_scaleT", [128, 4])
    sgu_gb = din("sgu_gb", [128, 1024])
    w_sT = din("w_sT", [128, 4, 128])
    w_sT_s = din("w_sT_s", [64, 4, 64])
    mask_s_d = din("mask_s", [64, 64])
    bs_rep = din("bs_rep", [128, 4, 128])
    bs_rep_s = din("bs_rep_s", [128, 4, 64])
    ln_gb = din("ln_gb", [128, 64])
    w_mlp1 = din("w_mlp1", [2, 1024, 4096])
    w_mlp2 = din("w_mlp2", [2, 4096, 1024])
    halo_valid = din("halo_valid", [128, 64])
    invc_d = din("invc", [128, 64])

    yT_q = dout("yT_q", [1024, TOWN])
    yT_s = dout("yT_s", [1024, NS])
    kT_all = dout("kT_all", [704, SEQ])
    v_all = dout("v_all", [SEQ, 640])
    xc_tail = dout("xc_tail", [512, 16])
    kT_s_o = dout("kT_s", [704, NS])
    v_s_o = dout("v_s", [NB, 4, 640])
    poolT_s = dout("poolT_s", [512, NB, 15])
    vn_s_o = dout("vn_s", [NS, 512])
    OUTB = Buf("outputs")

    A = Arena(nc, 207 * 1024)
    kaT, b_kaT = A.alloc("kaT", [4, LCAP], BF16)
    kbT, b_kbT = A.alloc("kbT", [LCAP], BF16)
    kiT, b_kiT = A.alloc("kiT", [LCAP], BF16)
    va, b_va = A.alloc("va", [LCAP // 128, 512], BF16)
    vb, b_vb = A.alloc("vb", [LCAP // 128, 128], BF16)
    cst, b_cst = A.alloc("cst", [512 + 64 + 32 + 128 + 128 + 128], F32)
    ramp512 = cst[:, 0:512]
    blkoff = cst[:, 512:576]
    pow2 = cst[:, 576:608]
    tril_f = cst[:, 736:864]
    poskc = cst[:, 864:992]
    identb, b_identb = A.alloc("identb", [128], BF16)
    onesb, b_onesb = A.alloc("onesb", [128], BF16)
    onesd, b_onesd = A.alloc("onesd", [128], BF16)
    ones128, b_ones128 = A.alloc("ones128", [128], BF16)
    posq_col, b_posq_col = A.alloc("posq_col", [NTQ], F32)
    lngb, b_lngb = A.alloc("lngb", [64], F32)
    small, b_small = A.alloc("small", [16], F32)
    pscT, b_pscT = A.alloc("pscT", [4], F32)
    wpool_b, b_wpool = A.alloc("wpool", [4, 128], BF16)
    wsT_b, b_wsT = A.alloc("wsT", [4, 128], BF16)
    wsTs_b, b_wsTs = A.alloc("wsTs", [4, 64], BF16)
    bsr, b_bsr = A.alloc("bsr", [4, 128], F32)
    bsrs, b_bsrs = A.alloc("bsrs", [4, 64], F32)
    sgugb, b_sgugb = A.alloc("sgugb", [1024], F32)
    hval, b_hval = A.alloc("hval", [64], F32)
    invc, b_invc = A.alloc("invc", [4, 16], F32)
    xch, b_xch = A.alloc("xch", [4, 64], F32)
    wwi, b_wwi = A.alloc("wwi", [8, 8], BF16)
    A_pti = A.alloc("pti", [NB * NPAGE], I32)
    A_idx = A.alloc("idx", [NB * NPAGE], I32)
    NSLOT = 2
    slots = [A.alloc("wslot%d" % i, [4096], BF16) for i in range(NSLOT)]
    for i in range(NSLOT):
        S.newsem("ws%d" % i)
    slot_i = [0]

    psb = []
    for i in range(4):
        t = nc.alloc_psum_tensor("ps%d" % i, [128, 1024], F32).ap()
        psb.append((t, Buf("ps%d" % i)))
        psb[-1][1].psum = True

    for s in ["ld0", "ld1", "ld2", "ld3", "st0", "st1", "gath", "cst"]:
        S.newsem(s)

    def dma(e, out, in_, reads, writes, sem=None):
        writes = [w for w in writes if w is not OUTB]
        bb = (writes + reads)[0]
        sem = S.newsem("d_" + bb.name)
        S.op(e, lambda: S.eng[e].dma_start(out=out, in_=in_), reads, writes, sem=sem, inc=16)

    def wload(src2d, kc, ncols):
        i = slot_i[0] % NSLOT
        slot_i[0] += 1
        ap, b = slots[i]
        v = ap[:, 0:kc * ncols].rearrange("p (k n) -> p k n", k=kc)
        if isinstance(src2d, WRef):
            dma('sp', v, src2d.ap.rearrange("(k p) n -> p k n", p=128), list(src2d.bufs), [b])
        else:
            dma('pool', v, src2d.rearrange("(k p) n -> p k n", p=128), [], [b], "ws%d" % i)
        return v, b

    def precast(name, src, R, C):
        dst = nc.dram_tensor("bf_" + name, [R, C], BF16, kind="Internal").ap()
        bufs = []
        for c0 in range(0, C, 2048):
            c1 = min(C, c0 + 2048)
            bb = Buf("bf_%s_%d" % (name, c0))
            bufs.append(bb)
            dma('pool', dst[:, c0:c1], src[:, c0:c1], [], [bb])
        return WRef(dst, bufs)

    _op_real = S.op
    NOC = cfg.get('NOCONST', 0)
    if NOC == 1:
        S.op = lambda *a, **k: None
    elif NOC == 2:
        S.op = lambda e, fn, reads=(), writes=(), sem=None, inc=1, multi=False: (_op_real(e, fn, reads, writes, sem=sem, inc=inc, multi=multi) if inc == 16 else None)
    elif NOC == 3:
        S.op = lambda e, fn, reads=(), writes=(), sem=None, inc=1, multi=False: (None if (inc == 16 and e == 'pool') else _op_real(e, fn, reads, writes, sem=sem, inc=inc, multi=multi))
    dma('sp', cst, consts_d, [], [b_cst], "cst")
    dma('sp', posq_col, posq_col_d, [], [b_posq_col], "cst")
    dma('sp', lngb, ln_gb, [], [b_lngb], "cst")
    dma('sp', pscT, pool_scaleT, [], [b_pscT], "cst")
    dma('sp', bsr, bs_rep, [], [b_bsr], "cst")
    dma('sp', bsrs, bs_rep_s, [], [b_bsrs], "cst")
    dma('sp', sgugb, sgu_gb, [], [b_sgugb], "cst")
    dma('sp', hval, halo_valid, [], [b_hval], "cst")
    dma('sp', invc.rearrange("p a b -> p (a b)"), invc_d, [], [b_invc], "cst")
    dma('pool', identb, consts_d[:, 608:736], [], [b_identb], "ld0")
    dma('pool', wwi, w_in_e[:, WI:WI + 8].rearrange("(k p) n -> p k n", p=128), [], [b_wwi])
    dma('pool', wpool_b, w_pool.rearrange("g c d -> c g d"), [], [b_wpool], "ld0")
    KEEPC = cfg.get('KEEPC', 'ABCDEFGH')
    if 'A' in KEEPC:
        S.op('dve', lambda: nc.vector.memset(onesb, 1.0), [], [b_onesb])
    if 'A' in KEEPC:
        S.op('dve', lambda: nc.vector.memset(onesd, 1.0 / 1024), [], [b_onesd])
    if 'A' in KEEPC:
        S.op('dve', lambda: nc.vector.memset(ones128, 1.0 / 128), [], [b_ones128])

    mk = A.top
    t0, b_t0 = A.alloc("t0", [256], F32)
    t1, b_t1 = A.alloc("t1", [4, 128], F32)
    t2, b_t2 = A.alloc("t2", [64, 64], F32)
    t3, b_t3 = A.alloc("t3", [64], F32)
    if 'B' in KEEPC and 'E' in KEEPC:
        dma('sp', t0, lam_rep, [], [b_t0], "ld1")
    if 'B' in KEEPC and 'E' in KEEPC:
        S.op('dve', lambda: nc.vector.tensor_tensor(out=t0[:, 0:64], in0=t0[:, 0:64], in1=t0[:, 64:128], op=ALU.mult), [b_t0], [b_t0])
    if 'B' in KEEPC and 'E' in KEEPC:
        S.op('dve', lambda: nc.vector.tensor_tensor(out=t0[:, 128:192], in0=t0[:, 128:192], in1=t0[:, 192:256], op=ALU.mult), [b_t0], [b_t0])
    if 'B' in KEEPC and 'E' in KEEPC:
        S.op('dve', lambda: nc.vector.reduce_sum(out=small[:, 2:3], in_=t0[:, 0:64], axis=AX.X), [b_t0], [b_small])
    if 'B' in KEEPC and 'E' in KEEPC:
        S.op('dve', lambda: nc.vector.reduce_sum(out=small[:, 3:4], in_=t0[:, 128:192], axis=AX.X), [b_t0, b_small], [b_small])
    if 'B' in KEEPC and 'F' in KEEPC:
        S.op('act', lambda: nc.scalar.activation(out=small[:, 4:6], in_=small[:, 2:4], func=AF.Exp), [b_small], [b_small])
    if 'B' in KEEPC and 'G' in KEEPC:
        S.op('dve', lambda: nc.vector.tensor_tensor(out=small[:, 6:7], in0=small[:, 5:6], in1=small[:, 4:5], op=ALU.subtract), [b_small], [b_small])
    if 'B' in KEEPC and 'G' in KEEPC:
        S.op('dve', lambda: nc.vector.tensor_scalar(out=small[:, 0:1], in0=small[:, 6:7], scalar1=-LAM_INIT0, scalar2=None, op0=ALU.add), [b_small], [b_small])
    if 'B' in KEEPC and 'H' in KEEPC:
        dma('sp', small[:, 7:8], sublng, [], [b_small], "ld1")
    if 'B' in KEEPC and 'H' in KEEPC:
        S.op('dve', lambda: nc.vector.tensor_scalar(out=small[:, 1:2], in0=small[:, 7:8], scalar1=1.0 - LAM_INIT0, scalar2=None, op0=ALU.mult), [b_small], [b_small])
    if 'C' in KEEPC:
        dma('sp', t1, w_sT, [], [b_t1], "ld2")
    for g in (range(4) if 'C' in KEEPC else []):
        S.op('dve', lambda g=g: nc.vector.tensor_tensor(out=wsT_b[:, g, :], in0=t1[:, g, :], in1=tril_f, op=ALU.mult), [b_t1, b_cst], [b_wsT])
    if 'D' in KEEPC:
        dma('sp', t2[0:64, 0:4, :], w_sT_s, [], [b_t2], "ld3")
    if 'D' in KEEPC:
        dma('sp', t3[0:64, :], mask_s_d, [], [b_t3], "ld3")
    for g in (range(4) if 'D' in KEEPC else []):
        S.op('dve', lambda g=g: nc.vector.tensor_tensor(out=wsTs_b[0:64, g, :], in0=t2[0:64, g, :], in1=t3[0:64, :], op=ALU.mult), [b_t2, b_t3], [b_wsTs])
    A.top = mk

    S.op = _op_real
    dma('sp', A_pti[0], pt_d, [], [A_pti[1]])
    S.op('dve', lambda: nc.vector.tensor_scalar(out=A_idx[0], in0=A_pti[0], scalar1=128.0, scalar2=poskc[:, 0:1], op0=ALU.mult, op1=ALU.add), [A_pti[1], b_cst], [A_idx[1]])
    NEG = -1.0e30
    if cfg.get('STOP', 9) <= 0:
        for sname, v in S.cnt.items():
            if v > 0 and S.seen['sp'].get(sname, 0) < v:
                nc.sync.wait_ge(S.sems[sname], v)
        return nc, dict(peak=A.peak)

    def evac_act(out, in_, reads, writes, func=AF.Copy, scale=1.0):
        S.op('act', lambda: nc.scalar.activation(out=out, in_=in_, func=func, scale=scale), reads, writes)

    def layernorm(x, b_x, xb, b_xb, n, lnidx, scr):
        (sq, b_sq), (m2, b_m2), (rs, b_rs) = scr
        pm, b_pm = psb[0]
        pv, b_pv = psb[1]
        for c in range(8):
            evac_act(xb[:, c, 0:n], x[:, c, 0:n], [b_x], [b_xb])
            evac_act(sq[:, c, 0:n], x[:, c, 0:n], [b_x], [b_sq], func=AF.Square)

        def f():
            last = None
            for c in range(8):
                last = nc.tensor.matmul(pm[:, 0:n], lhsT=onesd, rhs=xb[:, c, 0:n], start=(c == 0), stop=(c == 7))
            return last
        S.op('pe', f, [b_xb, b_onesd], [b_pm])

        def f2():
            last = None
            for c in range(8):
                last = nc.tensor.matmul(pv[:, 0:n], lhsT=onesd, rhs=sq[:, c, 0:n], start=(c == 0), stop=(c == 7))
            return last
        S.op('pe', f2, [b_sq, b_onesd], [b_pv])
        evac_act(m2[:, 0:n], pm[:, 0:n], [b_pm], [b_m2], func=AF.Square)
        S.op('dve', lambda: nc.vector.tensor_tensor(out=rs[:, 0:n], in0=pv[:, 0:n], in1=m2[:, 0:n], op=ALU.subtract), [b_pv, b_m2], [b_rs])
        S.op('dve', lambda: nc.vector.tensor_scalar(out=rs[:, 0:n], in0=rs[:, 0:n], scalar1=0.0, scalar2=EPS, op0=ALU.max, op1=ALU.add), [b_rs], [b_rs])
        S.op('act', lambda: nc.scalar.activation(out=rs[:, 0:n], in_=rs[:, 0:n], func=AF.Sqrt), [b_rs], [b_rs])
        S.op('dve', lambda: nc.vector.reciprocal(out=rs[:, 0:n], in_=rs[:, 0:n]), [b_rs], [b_rs])
        S.op('act', lambda: nc.scalar.activation(out=m2[:, 0:n], in_=pm[:, 0:n], func=AF.Copy), [b_pm], [b_m2])
        for c in range(8):
            S.op('dve', lambda c=c: nc.vector.tensor_tensor(out=x[:, c, 0:n], in0=x[:, c, 0:n], in1=m2[:, 0:n], op=ALU.subtract), [b_x, b_m2], [b_x])
            S.op('pool', lambda c=c: nc.gpsimd.tensor_tensor(out=x[:, c, 0:n], in0=x[:, c, 0:n], in1=rs[:, 0:n], op=ALU.mult), [b_x, b_rs], [b_x])
            S.op('dve', lambda c=c: nc.vector.tensor_scalar(out=x[:, c, 0:n], in0=x[:, c, 0:n], scalar1=lngb[:, lnidx * 8 + c:lnidx * 8 + c + 1],
                                                            scalar2=lngb[:, 32 + lnidx * 8 + c:32 + lnidx * 8 + c + 1], op0=ALU.mult, op1=ALU.add), [b_x, b_lngb], [b_x])
            evac_act(xb[:, c, 0:n], x[:, c, 0:n], [b_x], [b_xb])

    def proj_fm(wsrc, ncols, xb, b_xb, n, kc, consume):
        done = 0
        pi = 0
        while done < ncols:
            cw = min(512, ncols - done)
            wv, b_w = wload(wsrc[:, done:done + cw], kc, cw)
            for j in range(0, cw, 128):
                m = min(128, cw - j)
                ps, b_ps = psb[2 + (pi % 2)]
                pi += 1

                def f(j=j, m=m, ps=ps, wv=wv):
                    last = None
                    for k in range(kc):
                        last = nc.tensor.matmul(ps[0:m, 0:n], lhsT=wv[:, k, j:j + m], rhs=xb[:, k, 0:n], start=(k == 0), stop=(k == kc - 1))
                    return last
                S.op('pe', f, [b_w, b_xb], [b_ps])
                consume((done + j) // 128, m, ps, b_ps)
            done += cw

    def mlp(l, x, b_x, xb, b_xb, n, hT, b_hT):
        for fc in range(8):
            wv, b_w = wload(w_mlp1[l][:, fc * 512:(fc + 1) * 512], 8, 512)
            for sub in range(4):
                ps, b_ps = psb[2 + (sub % 2)]

                def f(sub=sub, ps=ps, wv=wv):
                    last = None
                    for k in range(8):
                        last = nc.tensor.matmul(ps[:, 0:n], lhsT=wv[:, k, sub * 128:(sub + 1) * 128], rhs=xb[:, k, 0:n], start=(k == 0), stop=(k == 7))
                    return last
                S.op('pe', f, [b_w, b_xb], [b_ps])
                fi = fc * 4 + sub
                evac_act(hT[:, fi, 0:n], ps[:, 0:n], [b_ps], [b_hT], func=AF.Relu)
                S.op('pool', lambda fi=fi: nc.gpsimd.tensor_tensor(out=hT[:, fi, 0:n], in0=hT[:, fi, 0:n], in1=hT[:, fi, 0:n], op=ALU.mult), [b_hT], [b_hT])
        for oc in range(8):
            wv, b_w = wload(w_mlp2[l][:, oc * 128:(oc + 1) * 128], 32, 128)
            ps, b_ps = psb[2 + (oc % 2)]

            def f(ps=ps, wv=wv):
                last = None
                for k in range(32):
                    last = nc.tensor.matmul(ps[:, 0:n], lhsT=wv[:, k, :], rhs=hT[:, k, 0:n], start=(k == 0), stop=(k == 31))
                return last
            S.op('pe', f, [b_w, b_hT], [b_ps])
            S.op('dve', lambda oc=oc, ps=ps: nc.vector.scalar_tensor_tensor(out=x[:, oc, 0:n], in0=x[:, oc, 0:n], scalar=ALPHA, in1=ps[:, 0:n], op0=ALU.mult, op1=ALU.add), [b_x, b_ps], [b_x])

    def outproj(wsrc, aT, b_aT, x, b_x, n):
        for half in range(2):
            wv, b_w = wload(wsrc[:, half * 512:(half + 1) * 512], 8, 512)
            for sub in range(4):
                oc = half * 4 + sub
                ps, b_ps = psb[2 + (sub % 2)]

                def f(sub=sub, ps=ps, wv=wv):
                    last = None
                    for k in range(8):
                        last = nc.tensor.matmul(ps[:, 0:n], lhsT=wv[:, k, sub * 128:(sub + 1) * 128], rhs=aT[:, k, 0:n], start=(k == 0), stop=(k == 7))
                    return last
                S.op('pe', f, [b_w, b_aT], [b_ps])
                S.op('dve', lambda oc=oc, ps=ps: nc.vector.scalar_tensor_tensor(out=x[:, oc, 0:n], in0=x[:, oc, 0:n], scalar=ALPHA, in1=ps[:, 0:n], op0=ALU.mult, op1=ALU.add), [b_x, b_ps], [b_x])

    PRECAST = cfg.get('PRECAST', 1)
    if PRECAST:
        bf_in_e = precast("w_in_e", w_in_e, 1024, 2888)
        bf_out_e = precast("w_out_e", w_out_e, 1024, 1024)
        bf_m1 = [None, None]
        bf_m2 = [None, None]
        bf_m1[0] = precast("w_mlp1_0", w_mlp1[0], 1024, 4096)
        bf_m2[0] = precast("w_mlp2_0", w_mlp2[0], 4096, 1024)
        bf_in_o = precast("w_in_o", w_in_o, 1024, 1536)
        bf_out_o = precast("w_out_o", w_out_o, 1024, 1024)
        bf_m1[1] = precast("w_mlp1_1", w_mlp1[1], 1024, 4096)
        bf_m2[1] = precast("w_mlp2_1", w_mlp2[1], 4096, 1024)
    mk = A.top
    wkv, b_wkv = A.alloc("wkv", [8, 1344], BF16)
    for (dst, src, w) in [(0, KA, 512), (512, KB, 128), (640, KI, 64), (704, VA, 512), (1216, VB, 128)]:
        dma('pool', wkv[:, :, dst:dst + w], w_in_e[:, src:src + w].rearrange("(k p) n -> p k n", p=128), [], [b_wkv], "ld1")
    xsb = [A.alloc("xsb%d" % i, [8, 512], BF16) for i in range(2)]
    stg = [A.alloc("stg%d" % i, [704], F32) for i in range(2)]
    stgk = [A.alloc("stgk%d" % i, [512], F32) for i in range(2)]
    sti = 0
    for gi in range(min(SEQ // 512, cfg.get('P1G', 99))):
        xs, b_xs = xsb[gi % 2]
        dma('pool', xs, xT_seq[:, gi * 512:(gi + 1) * 512].rearrange("(k p) n -> p k n", p=128), [], [b_xs], "ld%d" % (2 + gi % 2))
        P1V = cfg.get('P1V', 9)
        for j in range(min(6, cfg.get('P1J', 6)) if P1V >= 2 else 0):
            m = 128 if j < 5 else 64
            ps, b_ps = psb[j % 2]

            def f(j=j, m=m, ps=ps):
                last = None
                for k in range(8):
                    last = nc.tensor.matmul(ps[0:m, 0:512], lhsT=wkv[:, k, j * 128:j * 128 + m], rhs=xs[:, k, :], start=(k == 0), stop=(k == 7))
                return last
            S.op('pe', f, [b_wkv, b_xs], [b_ps])
            if j < 4:
                dst, b_dst = kaT[:, j, gi * 512:(gi + 1) * 512], b_kaT
            elif j == 4:
                dst, b_dst = kbT[:, gi * 512:(gi + 1) * 512], b_kbT
            else:
                dst, b_dst = kiT[0:64, gi * 512:(gi + 1) * 512], b_kiT
            evac_act(dst, ps[0:m, 0:512], [b_ps], [b_dst])
            if P1V < 3:
                continue
            sk, b_sk = stgk[sti % 2]
            sti += 1
            if cfg.get('P1X', 3) == 3:
                evac_act(sk[0:m, :], ps[0:m, 0:512], [b_ps], [b_sk])
            elif cfg.get('P1X', 0) == 4:
                S.op('dve', lambda m=m, ps=ps, sk=sk: nc.vector.tensor_scalar(out=sk[0:m, :], in0=ps[0:m, 0:512], scalar1=1.0, scalar2=None, op0=ALU.mult), [b_ps], [b_sk])
            elif cfg.get('P1X', 0) != 2:
                S.op('dve', lambda m=m, ps=ps, sk=sk: nc.vector.tensor_copy(out=sk[0:m, :], in_=ps[0:m, 0:512]), [b_ps], [b_sk])
            if cfg.get('P1X', 0) != 1:
                dma('sp', kT_all[j * 128:j * 128 + m, gi * 512:(gi + 1) * 512], sk[0:m, :], [b_sk], [OUTB], "st0")
        for t in range(4 if P1V >= 4 else 0):
            kt = gi * 4 + t
            ps, b_ps = psb[2 + (t % 2)]

            def f(t=t, ps=ps):
                last = None
                for k in range(8):
                    nc.tensor.matmul(ps[:, 0:512], lhsT=xs[:, k, t * 128:(t + 1) * 128], rhs=wkv[:, k, 704:1216], start=(k == 0), stop=(k == 7))
                    last = nc.tensor.matmul(ps[:, 512:640], lhsT=xs[:, k, t * 128:(t + 1) * 128], rhs=wkv[:, k, 1216:1344], start=(k == 0), stop=(k == 7))
                return last
            S.op('pe', f, [b_wkv, b_xs], [b_ps])
            evac_act(va[:, kt, :], ps[:, 0:512], [b_ps], [b_va])
            evac_act(vb[:, kt, :], ps[:, 512:640], [b_ps], [b_vb])
            sg, b_sg = stg[t % 2]
            evac_act(sg[:, 0:512], ps[:, 0:512], [b_ps], [b_sg])
            evac_act(sg[:, 512:640], ps[:, 512:640], [b_ps, b_sg], [b_sg])
            dma('sp', v_all[kt * 128:(kt + 1) * 128, :], sg[:, 0:640], [b_sg], [OUTB], "st1")
    A.top = mk

    class _Stop(Exception):
        pass
    SUB = cfg.get('SUB', 99)

    def chk(level):
        if SUB <= level:
            raise _Stop()

    def finish_all():
        for sname, v in S.cnt.items():
            if v > 0 and S.seen['sp'].get(sname, 0) < v:
                nc.sync.wait_ge(S.sems[sname], v)
        return nc, dict(peak=A.peak)
    STOP = cfg.get('STOP', 9)
    if STOP <= 1:
        return finish_all()
    if PRECAST:
        w_in_e, w_out_e, w_in_o, w_out_o, w_mlp1, w_mlp2 = bf_in_e, bf_out_e, bf_in_o, bf_out_o, bf_m1, bf_m2
    xb, b_xb = A.alloc("xb", [8, GQ], BF16)
    aT, b_aT = A.alloc("aT", [8, GQ], BF16)
    pqr, b_pqr = A.alloc("pqr", [GQ], F32)
    mkg = A.top
    qT, b_qT = A.alloc("qT", [8, GQ], BF16)
    qiT, b_qiT = A.alloc("qiT", [8, GQ], BF16)
    wtok, b_wtok = A.alloc("wtok", [4, 8], F32)
    sc0 = A.top
    score, b_score = A.alloc("score", [max(LCAP, 4096)], F32)
    sc1 = A.top
    A.top = sc0
    PTs = [A.alloc("PT%d" % i, [1024], BF16) for i in range(2)]
    f1, b_f1 = A.alloc("f1", [1024], F32)
    f2, b_f2 = A.alloc("f2", [1024], F32)
    f3, b_f3 = A.alloc("f3", [512], F32)
    f4, b_f4 = A.alloc("f4", [512], BF16)
    assert A.top <= sc1
    A.top = sc1
    maskb, b_maskb = A.alloc("maskb", [LCAP], BF16)
    maskT, b_maskT = A.alloc("maskT", [LCAP // 128, 128], BF16)
    cm = [A.alloc("cm%d" % i, [128], BF16) for i in range(2)]
    rtmp = [A.alloc("rtmp%d" % i, [512], F32) for i in range(2)]
    bis, b_bis = A.alloc("bis", [64], F32)
    pqb, b_pqb = A.alloc("pqb", [64], F32)
    attn_top = A.top
    A.top = mkg
    hT, b_hT = A.alloc("hT", [32, GQ], BF16)
    lnscr = (A.alloc("lnsq", [8, GQ], BF16), A.alloc("lnm2", [GQ], F32), A.alloc("lnrs", [GQ], F32))
    xf, b_xf = A.alloc("xf", [8, GQ], F32)
    post_top = A.top
    A.top = mkg
    xce, b_xce = A.alloc("xce", [4, 16 + GQ], F32)
    lv = [A.alloc("lv%d" % i, [16 + GQ], F32) for i in range(2)]
    poolb, b_poolb = A.alloc("poolb", [4, GQ], BF16)
    uT, b_uT = A.alloc("uT", [4, GQ], BF16)
    vtok, b_vtok = A.alloc("vtok", [4, 512], F32)
    vnb, b_vnb = A.alloc("vnb", [4, 512], BF16)
    vst, b_vst = A.alloc("vst", [4, 8], F32)
    l1_top = A.top
    A.top = max(attn_top, post_top, l1_top)

    def attention_tile(q0, nq, ktiles, krow_len, pqcol, Ktop, nvis=0):
        nkt = len(ktiles)
        (SA0, b_SA0), (SA1, b_SA1), (O, b_O), (L, b_L) = psb
        SAs = [(SA0, b_SA0), (SA1, b_SA1)]
        W4a = 4 * nq
        for ti, kt in enumerate(ktiles):
            nk = kt['nk']
            SA, b_SA = SAs[ti % 2]
            PT, b_PT = PTs[ti % 2]
            cmk, b_cmk = cm[ti % 2]

            def f(kt=kt, nk=nk, SA=SA):
                last = None
                for h in range(4):
                    for s in range(2):
                        last = nc.tensor.matmul(SA[0:nk, s * 512 + h * nq:s * 512 + (h + 1) * nq], lhsT=kt['kaT'][h][64 * s:64 * s + 64, :],
                                                rhs=qT[64 * s:64 * s + 64, h, q0:q0 + nq], start=True, stop=True)
                return last
            S.op('pe', f, kt['bufs'] + [b_qT], [b_SA])
            for s_ in range(2):
                c0 = s_ * 512
                S.op('act', lambda nk=nk, SA=SA, PT=PT, c0=c0: nc.scalar.activation(out=PT[0:nk, c0:c0 + W4a], in_=SA[0:nk, c0:c0 + W4a], func=AF.Exp, scale=0.125), [b_SA], [b_PT])
            kcol1 = sum(k_['nk'] for k_ in ktiles[:ti + 1])
            need_mask = kcol1 > nvis
            if need_mask:
                S.op('dve', lambda nk=nk, kt=kt, cmk=cmk: nc.vector.tensor_scalar(out=cmk[0:nk, 0:nq], in0=pqr[0:nk, q0:q0 + nq], scalar1=kt['pkc'], scalar2=None, op0=ALU.is_ge),
                     [b_pqr, b_cst], [b_cmk])
            for s_ in (range(2) if need_mask else []):
                c0 = s_ * 512
                S.op('pool', lambda nk=nk, PT=PT, cmk=cmk, c0=c0: nc.gpsimd.tensor_tensor(out=PT[0:nk, c0:c0 + W4a].rearrange("p (a q) -> p a q", a=4), in0=PT[0:nk, c0:c0 + W4a].rearrange("p (a q) -> p a q", a=4),
                                                                                        in1=cmk[0:nk, 0:nq].unsqueeze(1).to_broadcast([nk, 4, nq]), op=ALU.mult), [b_PT, b_cmk], [b_PT])

            def g(kt=kt, nk=nk, PT=PT, ti=ti):
                last = None
                for s in range(2):
                    for h in range(4):
                        c = s * 512 + h * nq
                        nc.tensor.matmul(O[:, c:c + nq], lhsT=kt['va'][h], rhs=PT[0:nk, c:c + nq],
                                         start=(ti == 0 and h == 0), stop=(ti == nkt - 1), skip_group_check=True)
                for s in range(2):
                    c0 = s * 512
                    last = nc.tensor.matmul(L[:, c0:c0 + W4a], lhsT=onesb[0:nk, :], rhs=PT[0:nk, c0:c0 + W4a], start=(ti == 0), stop=(ti == nkt - 1))
                return last
            S.op('pe', g, kt['bufs'] + [b_PT, b_onesb], [b_O, b_L])
        chk(2)
        for s_ in range(2):
            c0 = s_ * 512
            S.op('dve', lambda c0=c0: nc.vector.reciprocal(out=f1[:, c0:c0 + W4a], in_=L[:, c0:c0 + W4a]), [b_L], [b_f1])
            S.op('dve', lambda c0=c0: nc.vector.tensor_tensor(out=f2[:, c0:c0 + W4a], in0=O[:, c0:c0 + W4a], in1=f1[:, c0:c0 + W4a], op=ALU.mult), [b_O, b_f1], [b_f2])
        f2s0 = f2[:, 0:W4a].rearrange("p (h q) -> p h q", h=4)
        f2s1 = f2[:, 512:512 + W4a].rearrange("p (h q) -> p h q", h=4)
        f3v = f3[:, 0:4 * nq].rearrange("p (h q) -> p h q", h=4)
        S.op('dve', lambda: nc.vector.scalar_tensor_tensor(out=f3v, in0=f2s1, scalar=small[:, 0:1], in1=f2s0, op0=ALU.mult, op1=ALU.add), [b_f2, b_small], [b_f3])
        S.op('act', lambda: nc.scalar.activation(out=f4[:, 0:4 * nq], in_=f3[:, 0:4 * nq], func=AF.Square), [b_f3], [b_f4])
        S.op('pe', lambda: nc.tensor.matmul(SA0[:, 0:4 * nq], lhsT=ones128, rhs=f4[:, 0:4 * nq], start=True, stop=True), [b_f4, b_ones128], [b_SA0])
        S.op('dve', lambda: nc.vector.tensor_scalar(out=f1[:, 0:4 * nq], in0=SA0[:, 0:4 * nq], scalar1=EPS, scalar2=None, op0=ALU.add), [b_SA0], [b_f1])
        S.op('act', lambda: nc.scalar.activation(out=f1[:, 0:4 * nq], in_=f1[:, 0:4 * nq], func=AF.Sqrt), [b_f1], [b_f1])
        S.op('dve', lambda: nc.vector.reciprocal(out=f1[:, 0:4 * nq], in_=f1[:, 0:4 * nq]), [b_f1], [b_f1])
        S.op('dve', lambda: nc.vector.tensor_tensor(out=f3[:, 0:4 * nq], in0=f3[:, 0:4 * nq], in1=f1[:, 0:4 * nq], op=ALU.mult), [b_f3, b_f1], [b_f3])
        S.op('dve', lambda: nc.vector.tensor_scalar(out=aT[:, 0:4, q0:q0 + nq], in0=f3v, scalar1=small[:, 1:2], scalar2=None, op0=ALU.mult), [b_f3, b_small], [b_aT])

        chk(3)
        (D0, b_D0), (D1, b_D1), (TPp, b_TP), (OL, b_OL) = psb
        Ds = [(D0, b_D0), (D1, b_D1)]
        di = 0
        col = 0
        blocks = []
        for kt in ktiles:
            if blocks and blocks[-1][1] + kt['nk'] <= 512 and blocks[-1][3] is kt['kiT_base']:
                blocks[-1][1] += kt['nk']
                blocks[-1][4] += kt['bufs']
            else:
                blocks.append([col, kt['nk'], kt['kiT_off'], kt['kiT_base'], list(kt['bufs'])])
            col += kt['nk']
        Ltot = col
        for (c0, bw, koff, kbase, kbufs) in blocks:
            for h in range(8):
                D, b_D = Ds[di % 2]
                rt, b_rt = rtmp[di % 2]
                di += 1
                S.op('pe', lambda D=D, h=h, bw=bw, koff=koff, kbase=kbase: nc.tensor.matmul(D[0:nq, 0:bw], lhsT=qiT[0:64, h, q0:q0 + nq], rhs=kbase[0:64, koff:koff + bw], start=True, stop=True),
                     kbufs + [b_qiT], [b_D])
                S.op('act', lambda D=D, rt=rt, bw=bw: nc.scalar.activation(out=rt[0:nq, 0:bw], in_=D[0:nq, 0:bw], func=AF.Relu), [b_D], [b_rt])
                if h == 0:
                    S.op('dve', lambda rt=rt, c0=c0, bw=bw: nc.vector.tensor_scalar(out=score[0:nq, c0:c0 + bw], in0=rt[0:nq, 0:bw], scalar1=wtokv[0:nq, 0:1], scalar2=None, op0=ALU.mult),
                         [b_rt, b_wtok, b_wtok_s], [b_score])
                else:
                    S.op('dve', lambda rt=rt, c0=c0, bw=bw, h=h: nc.vector.scalar_tensor_tensor(out=score[0:nq, c0:c0 + bw], in0=rt[0:nq, 0:bw], scalar=wtokv[0:nq, h:h + 1],
                                                                                          in1=score[0:nq, c0:c0 + bw], op0=ALU.mult, op1=ALU.add), [b_rt, b_wtok, b_wtok_s, b_score], [b_score])
        chk(3.3)
        S.op('dve', lambda: nc.vector.tensor_reduce(out=bis[0:nq, 0:1], in_=score[0:nq, 0:Ltot], axis=AX.X, op=ALU.max), [b_score], [b_bis])
        S.op('dve', lambda: nc.vector.tensor_reduce(out=bis[0:nq, 1:2], in_=score[0:nq, 0:Ltot], axis=AX.X, op=ALU.min), [b_score, b_bis], [b_bis])
        S.op('dve', lambda: nc.vector.tensor_tensor(out=bis[0:nq, 2:3], in0=bis[0:nq, 0:1], in1=bis[0:nq, 1:2], op=ALU.subtract), [b_bis], [b_bis])
        S.op('dve', lambda: nc.vector.tensor_scalar(out=bis[0:nq, 2:3], in0=bis[0:nq, 2:3], scalar1=2.0, scalar2=None, op0=ALU.add), [b_bis], [b_bis])
        S.op('dve', lambda: nc.vector.tensor_scalar(out=bis[0:nq, 8:8 + NBIS + 2], in0=pow2[0:nq, 0:NBIS + 2], scalar1=bis[0:nq, 2:3], scalar2=None, op0=ALU.mult), [b_bis, b_cst], [b_bis])
        S.op('dve', lambda: nc.vector.scalar_tensor_tensor(out=bis[0:nq, 3:4], in0=bis[0:nq, 1:2], scalar=-1.0, in1=bis[0:nq, 8:9], op0=ALU.add, op1=ALU.add), [b_bis], [b_bis])
        S.op('dve', lambda: nc.vector.tensor_scalar(out=pqb[0:nq, 0:16], in0=blkoff[0:nq, 0:16], scalar1=-1.0, scalar2=pqcol, op0=ALU.mult, op1=ALU.add), [b_cst, b_posq_col], [b_pqb])
        for c0 in range(0, Ltot, 512):
            bw = min(512, Ltot - c0)
            if c0 + bw <= nvis:
                continue
            bi = c0 // 512
            rt, b_rt = rtmp[bi % 2]
            S.op('pool', lambda rt=rt, bw=bw, bi=bi: nc.gpsimd.tensor_scalar(out=rt[0:nq, 0:bw], in0=ramp512[0:nq, 0:bw], scalar1=pqb[0:nq, bi:bi + 1], scalar2=0.0, op0=ALU.subtract, op1=ALU.max),
                 [b_cst, b_pqb], [b_rt])
            S.op('dve', lambda rt=rt, bw=bw, c0=c0: nc.vector.scalar_tensor_tensor(out=score[0:nq, c0:c0 + bw], in0=rt[0:nq, 0:bw], scalar=NEG, in1=score[0:nq, c0:c0 + bw], op0=ALU.mult, op1=ALU.add),
                 [b_rt, b_score], [b_score])
        chk(3.5)
        for it in range(NBIS):
            S.op('dve', lambda: nc.vector.tensor_scalar(out=maskb[0:nq, 0:Ltot], in0=score[0:nq, 0:Ltot], scalar1=bis[0:nq, 3:4], scalar2=None, op0=ALU.is_ge, op1=ALU.add, accum_out=bis[0:nq, 4:5]),
                 [b_score, b_bis], [b_maskb, b_bis])
            S.op('dve', lambda it=it: nc.vector.tensor_scalar(out=bis[0:nq, 5:6], in0=bis[0:nq, 4:5], scalar1=float(Ktop) - 0.5, scalar2=bis[0:nq, 8 + it:9 + it], op0=ALU.is_ge, op1=ALU.mult), [b_bis], [b_bis])
            S.op('dve', lambda it=it: nc.vector.scalar_tensor_tensor(out=bis[0:nq, 3:4], in0=bis[0:nq, 5:6], scalar=bis[0:nq, 9 + it:10 + it], in1=bis[0:nq, 3:4], op0=ALU.subtract, op1=ALU.add), [b_bis], [b_bis])
        S.op('dve', lambda: nc.vector.tensor_tensor(out=bis[0:nq, 6:7], in0=bis[0:nq, 3:4], in1=bis[0:nq, 8 + NBIS:9 + NBIS], op=ALU.subtract), [b_bis], [b_bis])
        S.op('dve', lambda: nc.vector.tensor_scalar(out=maskb[0:nq, 0:Ltot], in0=score[0:nq, 0:Ltot], scalar1=bis[0:nq, 6:7], scalar2=None, op0=ALU.is_ge), [b_score, b_bis], [b_maskb])
        chk(3.7)
        TPb = TPp.bitcast(BF16)
        col = 0
        for g0 in range(0, nkt, 8):
            g1 = min(nkt, g0 + 8)

            def f(g0=g0, g1=g1, col=col):
                last = None
                c = col
                for ti in range(g0, g1):
                    nk = ktiles[ti]['nk']
                    last = nc.tensor.transpose(TPb[0:nk, (ti - g0) * 128:(ti - g0) * 128 + nq], maskb[0:nq, c:c + nk], identb[0:nq, 0:nq])
                    c += nk
                return last
            S.op('pe', f, [b_maskb, b_identb], [b_TP])
            for ti in range(g0, g1):
                col += ktiles[ti]['nk']
            S.op('act', lambda g0=g0, g1=g1: nc.scalar.activation(out=maskT[:, g0:g1, 0:nq], in_=TPb[:, 0:(g1 - g0) * 128].rearrange("p (t q) -> p t q", q=128)[:, :, 0:nq], func=AF.Copy), [b_TP], [b_maskT])
        chk(4)
        W4 = 4 * nq
        for ti, kt in enumerate(ktiles):
            nk = kt['nk']
            D, b_D = Ds[ti % 2]
            PT, b_PT = PTs[ti % 2]

            def f(kt=kt, nk=nk, D=D):
                last = None
                for h in range(4):
                    last = nc.tensor.matmul(D[0:nk, h * nq:(h + 1) * nq], lhsT=kt['kbT'], rhs=qT[:, 4 + h, q0:q0 + nq], start=True, stop=True)
                return last
            S.op('pe', f, kt['bufs'] + [b_qT], [b_D])
            S.op('act', lambda nk=nk, D=D, PT=PT: nc.scalar.activation(out=PT[0:nk, 0:W4], in_=D[0:nk, 0:W4], func=AF.Exp, scale=128 ** -0.5), [b_D], [b_PT])
            S.op('pool', lambda nk=nk, PT=PT, ti=ti: nc.gpsimd.tensor_tensor(out=PT[0:nk, 0:W4].rearrange("p (a q) -> p a q", a=4), in0=PT[0:nk, 0:W4].rearrange("p (a q) -> p a q", a=4),
                                                                              in1=maskT[0:nk, ti, 0:nq].unsqueeze(1).to_broadcast([nk, 4, nq]), op=ALU.mult), [b_PT, b_maskT], [b_PT])

            def g(kt=kt, nk=nk, PT=PT, ti=ti):
                nc.tensor.matmul(OL[:, 0:W4], lhsT=kt['vb'], rhs=PT[0:nk, 0:W4], start=(ti == 0), stop=(ti == nkt - 1))
                return nc.tensor.matmul(OL[:, 512:512 + W4], lhsT=onesb[0:nk, :], rhs=PT[0:nk, 0:W4], start=(ti == 0), stop=(ti == nkt - 1))
            S.op('pe', g, kt['bufs'] + [b_PT, b_onesb], [b_OL])
        S.op('dve', lambda: nc.vector.reciprocal(out=f1[:, 0:W4], in_=OL[:, 512:512 + W4]), [b_OL], [b_f1])
        S.op('dve', lambda: nc.vector.tensor_tensor(out=aT[:, 4:8, q0:q0 + nq], in0=OL[:, 0:W4].rearrange("p (h q) -> p h q", h=4), in1=f1[:, 0:W4].rearrange("p (h q) -> p h q", h=4), op=ALU.mult),
             [b_OL, b_f1], [b_aT])

    wtokv = None

    def prompt_ktiles(n):
        kts = []
        for t in range(n):
            kts.append(dict(nk=128, kaT=[kaT[:, h, t * 128:(t + 1) * 128] for h in range(4)], kbT=kbT[:, t * 128:(t + 1) * 128],
                            kiT_base=kiT, kiT_off=t * 128, va=[va[:, t, h * 128:(h + 1) * 128] for h in range(4)], vb=vb[:, t, :],
                            pkc=poskc[:, t:t + 1], bufs=[b_kaT, b_kbT, b_kiT, b_va, b_vb]))
        return kts

    def run_group(kind, gi):
        nonlocal wtokv
        if kind == 'halo':
            n, c0, xsrc, tile0 = NH, 0, xT_q, 0
        elif kind == 'own':
            n, c0, xsrc, tile0 = GQ, NH + gi * GQ, xT_q, 1 + gi * 4
        else:
            n, c0, xsrc, tile0 = NS, 0, xT_s, NTQ - 1
        pc0 = c0 if kind != 'sample' else NQ
        dma('pool', xb[:, :, 0:n], xsrc[:, c0:c0 + n].rearrange("(k p) n -> p k n", p=128), [], [b_xb], "ld0")
        dma('sp', pqr[:, 0:n], posq_row_d[:, pc0:pc0 + n], [], [b_pqr], "ld1")
        def cons_q(base):
            def c(j, m, ps, b_ps):
                evac_act(qT[:, base + j, 0:n], ps[:, 0:n], [b_ps], [b_qT])
            return c
        proj_fm(w_in_e[:, QA:QA + 512], 512, xb, b_xb, n, 8, cons_q(0))
        proj_fm(w_in_e[:, QB:QB + 512], 512, xb, b_xb, n, 8, cons_q(4))
        wv, b_w = wload(w_in_e[:, QI:QI + 512], 8, 512)
        for h in range(8):
            ps, b_ps = psb[2 + (h % 2)]

            def f(h=h, ps=ps, wv=wv):
                last = None
                for k in range(8):
                    last = nc.tensor.matmul(ps[0:64, 0:n], lhsT=wv[:, k, h * 64:(h + 1) * 64], rhs=xb[:, k, 0:n], start=(k == 0), stop=(k == 7))
                return last
            S.op('pe', f, [b_w, b_xb], [b_ps])
            evac_act(qiT[0:64, h, 0:n], ps[0:64, 0:n], [b_ps], [b_qiT])
        ntile = (n + 127) // 128
        for t in range(ntile if kind != 'sample' else 0):
            tn = min(128, n - t * 128)
            ps, b_ps = psb[2 + (t % 2)]

            def f(t=t, tn=tn, ps=ps):
                last = None
                for k in range(8):
                    last = nc.tensor.matmul(ps[0:tn, 0:8], lhsT=xb[:, k, t * 128:t * 128 + tn], rhs=wwi[:, k, :], start=(k == 0), stop=(k == 7))
                return last
            S.op('pe', f, [b_wwi, b_xb], [b_ps])
            evac_act(wtok[0:tn, t, :], ps[0:tn, 0:8], [b_ps], [b_wtok], scale=(8 ** -0.5) / 8.0)
        chk(1)
        if kind != 'sample':
            for t in range(ntile):
                tn = min(128, n - t * 128)
                wtokv = wtok[:, t, :]
                if kind == 'halo':
                    nkeys = NKT
                else:
                    nkeys = min(NKT, 8 * (gi + 1))
                attention_tile(t * 128, tn, prompt_ktiles(nkeys), nkeys * 128, posq_col[0:tn, tile0 + t:tile0 + t + 1], KP, nvis=(0 if kind == 'halo' else 1024 * gi))
        else:
            sample_attention(n)
        chk(5)
        dma('sp', xf[:, :, 0:n], xsrc[:, c0:c0 + n].rearrange("(k p) n -> p k n", p=128), [], [b_xf], "ld2")
        outproj(w_out_e, aT, b_aT, xf, b_xf, n)
        chk(6)
        layernorm(xf, b_xf, xb, b_xb, n, 0, lnscr)
        chk(7)
        mlp(0, xf, b_xf, xb, b_xb, n, hT, b_hT)
        chk(8)
        layernorm(xf, b_xf, xb, b_xb, n, 1, lnscr)
        chk(9)
        layer1_mixer(kind, gi, n)
        if kind == 'halo':
            return
        outproj(w_out_o, aT, b_aT, xf, b_xf, n)
        layernorm(xf, b_xf, xb, b_xb, n, 2, lnscr)
        mlp(1, xf, b_xf, xb, b_xb, n, hT, b_hT)
        layernorm(xf, b_xf, xb, b_xb, n, 3, lnscr)
        if kind == 'own':
            dma('sp', yT_q[:, gi * GQ:(gi + 1) * GQ].rearrange("(k p) n -> p k n", p=128), xf[:, :, 0:n], [b_xf], [OUTB], "st0")
        else:
            dma('sp', yT_s.rearrange("(k p) n -> p k n", p=128), xf[:, :, 0:n], [b_xf], [OUTB], "st0")

    def layer1_mixer(kind, gi, n):
        samp = (kind == 'sample')
        HIST = 16
        def cons_xc(j, m, ps, b_ps):
            if samp:
                for g in [j]:
                    S.op('act', lambda: nc.scalar.activation(out=xce[:, j, 0:NB * 20].rearrange("p (b t) -> p b t", t=20)[:, :, 16:20], in_=ps[:, 0:n].rearrange("p (b t) -> p b t", t=4), func=AF.Copy), [b_ps], [b_xce])
            else:
                evac_act(xce[:, j, HIST:HIST + n], ps[:, 0:n], [b_ps], [b_xce])
        proj_fm(w_in_o[:, 0:512], 512, xb, b_xb, n, 8, cons_xc)
        if kind == 'halo':
            for g in range(4):
                S.op('dve', lambda g=g: nc.vector.tensor_tensor(out=xch[:, g, 0:n], in0=xce[:, g, HIST:HIST + n], in1=hval[:, 0:n], op=ALU.mult), [b_xce, b_hval], [b_xch])
            return
        if samp:
            for g in range(4):
                dma('sp', xce[:, g, 0:NB * 20].rearrange("p (b t) -> p b t", t=20)[:, :, 1:16], stateT[g * 128:(g + 1) * 128, :, :], [], [b_xce], "ld3")
            for g in range(4):
                dma('sp', poolT_s[g * 128:(g + 1) * 128, :, :], xce[:, g, 0:NB * 20].rearrange("p (b t) -> p b t", t=20)[:, :, 5:20], [b_xce], [OUTB], "st1")
        else:
            for g in range(4):
                S.op('dve', lambda g=g: nc.vector.tensor_copy(out=xce[:, g, 0:HIST], in_=xch[:, g, gi * 16:(gi + 1) * 16]), [b_xch], [b_xce])
            if gi == NG - 1:
                for g in range(4):
                    dma('sp', xc_tail[g * 128:(g + 1) * 128, :], xce[:, g, HIST + n - 16:HIST + n], [b_xce], [OUTB], "st1")
        for g in range(4):
            w = 2 << g
            if samp:
                E = NB * 20
                src = xce[:, g, 0:E].rearrange("p (b t) -> p b t", t=20)
                cur, b_cur = src, b_xce
                sh = 1
                lo = 0
                for lvl in range(g + 1):
                    dst, b_dst = lv[lvl % 2]
                    dstv = dst[:, 0:E].rearrange("p (b t) -> p b t", t=20)
                    lo += sh
                    S.op('dve', lambda cur=cur, dstv=dstv, lo=lo, sh=sh: nc.vector.tensor_tensor(out=dstv[:, :, lo:20], in0=cur[:, :, lo:20], in1=cur[:, :, lo - sh:20 - sh], op=ALU.add), [b_cur], [b_dst])
                    cur, b_cur = dstv, b_dst
                    sh *= 2
                pv = poolb[:, g, 0:n].rearrange("p (b t) -> p b t", t=4)
                S.op('dve', lambda cur=cur, pv=pv, src=src, w=w: nc.vector.scalar_tensor_tensor(out=pv, in0=cur[:, :, 16:20], scalar=1.0 / w, in1=src[:, :, 16:20], op0=ALU.mult, op1=ALU.subtract), [b_cur, b_xce], [b_poolb])
            else:
                E = HIST + n
                src = xce[:, g, 0:E]
                cur, b_cur = src, b_xce
                sh = 1
                lo = 0
                for lvl in range(g + 1):
                    dst, b_dst = lv[lvl % 2]
                    lo += sh
                    S.op('dve', lambda cur=cur, dst=dst, lo=lo, sh=sh, E=E: nc.vector.tensor_tensor(out=dst[:, lo:E], in0=cur[:, lo:E], in1=cur[:, lo - sh:E - sh], op=ALU.add), [b_cur], [b_dst])
                    cur, b_cur = dst[:, 0:E], b_dst
                    sh *= 2
                S.op('dve', lambda cur=cur, src=src, w=w, g=g: nc.vector.scalar_tensor_tensor(out=poolb[:, g, 0:n], in0=cur[:, HIST:HIST + n], scalar=1.0 / w, in1=src[:, HIST:HIST + n], op0=ALU.mult, op1=ALU.subtract),
                     [b_cur, b_xce], [b_poolb])
                if gi == 0:
                    dstl, b_dl = lv[(g + 1) % 2]
                    S.op('dve', lambda cur=cur, dstl=dstl, g=g: nc.vector.tensor_tensor(out=dstl[:, 0:16], in0=cur[:, HIST:HIST + 16], in1=invc[:, g, :], op=ALU.mult), [b_cur, b_invc], [b_dl])
                    S.op('dve', lambda dstl=dstl, src=src, g=g: nc.vector.tensor_tensor(out=poolb[:, g, 0:16], in0=dstl[:, 0:16], in1=src[:, HIST:HIST + 16], op=ALU.subtract), [b_dl, b_xce], [b_poolb])
        for g in range(4):
            ps, b_ps = psb[2 + (g % 2)]
            S.op('pe', lambda g=g, ps=ps: nc.tensor.matmul(ps[:, 0:n], lhsT=wpool_b[:, g, :], rhs=poolb[:, g, 0:n], start=True, stop=True), [b_wpool, b_poolb], [b_ps])
            S.op('dve', lambda g=g, ps=ps: nc.vector.tensor_scalar(out=aT[:, g, 0:n], in0=ps[:, 0:n], scalar1=pscT[:, g:g + 1], scalar2=None, op0=ALU.mult), [b_ps, b_pscT], [b_aT])
        def cons_u(j, m, ps, b_ps):
            evac_act(uT[:, j, 0:n], ps[:, 0:n], [b_ps], [b_uT], func=AF.Gelu)
        proj_fm(w_in_o[:, 512:1024], 512, xb, b_xb, n, 8, cons_u)
        wv, b_w = wload(w_in_o[:, 1024:1536], 8, 512)
        ntile = (n + 127) // 128
        for t in range(ntile):
            tn = min(128, n - t * 128)
            ps, b_ps = psb[2 + (t % 2)]

            def f(t=t, tn=tn, ps=ps, wv=wv):
                last = None
                for k in range(8):
                    last = nc.tensor.matmul(ps[0:tn, 0:512], lhsT=xb[:, k, t * 128:t * 128 + tn], rhs=wv[:, k, :], start=(k == 0), stop=(k == 7))
                return last
            S.op('pe', f, [b_w, b_xb], [b_ps])
            vt = vtok[0:tn, t, :]
            st = vst[0:tn, t, :]
            S.op('act', lambda ps=ps, tn=tn, vt=vt, st=st: nc.scalar.activation(out=vt, in_=ps[0:tn, 0:512], func=AF.Gelu, accum_out=st[:, 0:1]), [b_ps], [b_vtok, b_vst])
            S.op('dve', lambda st=st: nc.vector.tensor_scalar(out=st[:, 1:2], in0=st[:, 0:1], scalar1=1.0 / 512, scalar2=None, op0=ALU.mult), [b_vst], [b_vst])
            S.op('dve', lambda vt=vt, st=st: nc.vector.tensor_scalar(out=vt, in0=vt, scalar1=st[:, 1:2], scalar2=None, op0=ALU.subtract), [b_vtok, b_vst], [b_vtok])
            lvt, b_lvt = lv[t % 2]
            S.op('act', lambda vt=vt, tn=tn, lvt=lvt, st=st: nc.scalar.activation(out=lvt[0:tn, 0:512], in_=vt, func=AF.Square, accum_out=st[:, 2:3]), [b_vtok], [b_lvt, b_vst])
            S.op('dve', lambda st=st: nc.vector.tensor_scalar(out=st[:, 3:4], in0=st[:, 2:3], scalar1=1.0 / 512, scalar2=EPS, op0=ALU.mult, op1=ALU.add), [b_vst], [b_vst])
            S.op('act', lambda st=st: nc.scalar.activation(out=st[:, 4:5], in_=st[:, 3:4], func=AF.Sqrt), [b_vst], [b_vst])
            S.op('dve', lambda st=st: nc.vector.reciprocal(out=st[:, 5:6], in_=st[:, 4:5]), [b_vst], [b_vst])
            S.op('dve', lambda vt=vt, st=st, tn=tn: nc.vector.scalar_tensor_tensor(out=vt, in0=vt, scalar=st[:, 5:6], in1=sgugb[0:tn, 0:512], op0=ALU.mult, op1=ALU.mult), [b_vtok, b_vst, b_sgugb], [b_vtok])
            S.op('dve', lambda vt=vt, tn=tn: nc.vector.tensor_tensor(out=vt, in0=vt, in1=sgugb[0:tn, 512:1024], op=ALU.add), [b_vtok, b_sgugb], [b_vtok])
            S.op('act', lambda vt=vt, tn=tn, t=t: nc.scalar.activation(out=vnb[0:tn, t, :], in_=vt, func=AF.Copy), [b_vtok], [b_vnb])
            if samp:
                dma('sp', vn_s_o[:, :], vt, [b_vtok], [OUTB], "st1")
            for g in range(4):
                ps2, b_ps2 = psb[g % 2]
                if samp:
                    S.op('pe', lambda g=g, ps2=ps2, tn=tn, t=t: nc.tensor.matmul(ps2[:, 0:tn], lhsT=vnb[0:tn, t, g * 128:(g + 1) * 128], rhs=wsTs_b[0:tn, g, 0:tn], start=True, stop=True), [b_vnb, b_wsTs], [b_ps2])
                    bias = bsrs[:, g, 0:tn]
                    b_bias = b_bsrs
                else:
                    S.op('pe', lambda g=g, ps2=ps2, tn=tn, t=t: nc.tensor.matmul(ps2[:, 0:tn], lhsT=vnb[0:tn, t, g * 128:(g + 1) * 128], rhs=wsT_b[0:tn, g, 0:tn], start=True, stop=True), [b_vnb, b_wsT], [b_ps2])
                    bias = bsr[:, g, 0:tn]
                    b_bias = b_bsr
                lvt2, b_lvt2 = lv[g % 2]
                S.op('dve', lambda ps2=ps2, tn=tn, bias=bias, lvt2=lvt2: nc.vector.tensor_tensor(out=lvt2[:, 0:tn], in0=ps2[:, 0:tn], in1=bias, op=ALU.add), [b_ps2, b_bias], [b_lvt2])
                S.op('dve', lambda g=g, tn=tn, t=t, lvt2=lvt2: nc.vector.tensor_tensor(out=aT[:, 4 + g, t * 128:t * 128 + tn], in0=lvt2[:, 0:tn], in1=uT[:, g, t * 128:t * 128 + tn], op=ALU.mult), [b_lvt2, b_uT], [b_aT])

    def sample_attention(n):
        nonlocal wtokv
        kn, b_kn = f2[:, 0:6 * NS].rearrange("p (j t) -> p j t", j=6), b_f2
        knb, b_knb = A_knb
        wv, b_w = wload(w_in_e[:, KA:KA + 512], 8, 512)
        for j in range(4):
            ps, b_ps = psb[2 + (j % 2)]

            def f(j=j, ps=ps, wv=wv):
                last = None
                for k in range(8):
                    last = nc.tensor.matmul(ps[:, 0:n], lhsT=wv[:, k, j * 128:(j + 1) * 128], rhs=xb[:, k, 0:n], start=(k == 0), stop=(k == 7))
                return last
            S.op('pe', f, [b_w, b_xb], [b_ps])
            evac_act(knb[:, j, 0:n], ps[:, 0:n], [b_ps], [b_knb])
            S.op('dve', lambda j=j, ps=ps: nc.vector.tensor_copy(out=kn[:, j, 0:n], in_=ps[:, 0:n]), [b_ps], [b_kn])
        wv, b_w = wload(w_in_e[:, KB:KB + 128], 8, 128)
        ps, b_ps = psb[2]

        def f(ps=ps, wv=wv):
            last = None
            for k in range(8):
                last = nc.tensor.matmul(ps[:, 0:n], lhsT=wv[:, k, :], rhs=xb[:, k, 0:n], start=(k == 0), stop=(k == 7))
            return last
        S.op('pe', f, [b_w, b_xb], [b_ps])
        evac_act(knb[:, 4, 0:n], ps[:, 0:n], [b_ps], [b_knb])
        S.op('dve', lambda ps=ps: nc.vector.tensor_copy(out=kn[:, 4, 0:n], in_=ps[:, 0:n]), [b_ps], [b_kn])
        wv, b_w = wload(w_in_e[:, KI:KI + 64], 8, 64)
        ps, b_ps = psb[3]

        def f(ps=ps, wv=wv):
            last = None
            for k in range(8):
                last = nc.tensor.matmul(ps[0:64, 0:n], lhsT=wv[:, k, :], rhs=xb[:, k, 0:n], start=(k == 0), stop=(k == 7))
            return last
        S.op('pe', f, [b_w, b_xb], [b_ps])
        evac_act(knb[0:64, 5, 0:n], ps[0:64, 0:n], [b_ps], [b_knb])
        S.op('dve', lambda ps=ps: nc.vector.tensor_copy(out=kn[0:64, 5, 0:n], in_=ps[0:64, 0:n]), [b_ps], [b_kn])
        for j in range(6):
            m = 128 if j < 5 else 64
            dma('sp', kT_s_o[j * 128:j * 128 + m, :], kn[0:m, j, 0:n], [b_kn], [OUTB], "st0")
        wvv, b_wvv = wload(w_in_e[:, VA:VA + 512], 8, 512)
        wvb, b_wvb = wload(w_in_e[:, VB:VB + 128], 8, 128)
        idx, b_idx = A_idx
        pa, b_pa = A_pa
        pb, b_pb = A_pb
        vnew, b_vnew = A_vnew
        vnf, b_vnf = A_vnf
        TPb = psb[2][0].bitcast(BF16)
        b_TP = psb[2][1]
        for b in range(NB):
            def gat(b=b):
                out = []
                for j in range(NPAGE):
                    bj = b * NPAGE + j
                    out.append(nc.gpsimd.indirect_dma_start(out=pa[:, j, :], out_offset=None, in_=cache_a, in_offset=bass.IndirectOffsetOnAxis(ap=idx[:, bj:bj + 1], axis=0)))
                    out.append(nc.gpsimd.indirect_dma_start(out=pb[:, j, :], out_offset=None, in_=cache_b, in_offset=bass.IndirectOffsetOnAxis(ap=idx[:, bj:bj + 1], axis=0)))
                return out
            S.op('pool', gat, [b_idx], [b_pa, b_pb], sem="gath", inc=16, multi=True)
            for j in range(NPAGE):
                def tr(j=j):
                    last = None
                    for h in range(4):
                        nc.tensor.transpose(TPb[:, h * 128:(h + 1) * 128], pa[:, j, h * 256:h * 256 + 128], identb)
                    nc.tensor.transpose(TPb[:, 512:640], pb[:, j, 0:128], identb)
                    last = nc.tensor.transpose(TPb[0:64, 640:768], pb[:, j, 256:320], identb)
                    return last
                S.op('pe', tr, [b_pa, b_pb, b_identb], [b_TP])
                S.op('act', lambda j=j: nc.scalar.activation(out=kaTs[:, :, j * 128:(j + 1) * 128], in_=TPb[:, 0:512].rearrange("p (h k) -> p h k", h=4), func=AF.Copy), [b_TP], [b_kaTs])
                S.op('dve', lambda j=j: nc.vector.tensor_copy(out=kbTs[:, j * 128:(j + 1) * 128], in_=TPb[:, 512:640]), [b_TP], [b_kbTs])
                S.op('dve', lambda j=j: nc.vector.tensor_copy(out=kiTs[0:64, j * 128:(j + 1) * 128], in_=TPb[0:64, 640:768]), [b_TP], [b_kiTs])
            S.op('act', lambda b=b: nc.scalar.activation(out=kaTs[:, :, PAST:PAST + 4], in_=knb[:, 0:4, 4 * b:4 * b + 4], func=AF.Copy), [b_knb], [b_kaTs])
            S.op('dve', lambda b=b: nc.vector.tensor_copy(out=kbTs[:, PAST:PAST + 4], in_=knb[:, 4, 4 * b:4 * b + 4]), [b_knb], [b_kbTs])
            S.op('dve', lambda b=b: nc.vector.tensor_copy(out=kiTs[0:64, PAST:PAST + 4], in_=knb[0:64, 5, 4 * b:4 * b + 4]), [b_knb], [b_kiTs])
            ps, b_ps = psb[3]

            def fv(b=b, ps=ps):
                last = None
                for k in range(8):
                    nc.tensor.matmul(ps[0:4, 0:512], lhsT=xb[:, k, 4 * b:4 * b + 4], rhs=wvv[:, k, :], start=(k == 0), stop=(k == 7))
                for k in range(8):
                    nc.tensor.matmul(ps[0:4, 512:640], lhsT=xb[:, k, 4 * b:4 * b + 4], rhs=wvb[:, k, :], start=(k == 0), stop=(k == 7))
                for k in range(8):
                    last = nc.tensor.matmul(ps[0:4, 640:648], lhsT=xb[:, k, 4 * b:4 * b + 4], rhs=wwi[:, k, :], start=(k == 0), stop=(k == 7))
                return last
            S.op('pe', fv, [b_wvv, b_wvb, b_wwi, b_xb], [b_ps])
            evac_act(wtok_s[0:4, b, :], ps[0:4, 640:648], [b_ps], [b_wtok_s], scale=(8 ** -0.5) / 8.0)
            evac_act(vnew[0:4, 0:512], ps[0:4, 0:512], [b_ps], [b_vnew])
            evac_act(vnew[0:4, 512:640], ps[0:4, 512:640], [b_ps, b_vnew], [b_vnew])
            S.op('dve', lambda ps=ps: nc.vector.tensor_copy(out=vnf[0:4, 0:512], in_=ps[0:4, 0:512]), [b_ps], [b_vnf])
            S.op('dve', lambda ps=ps: nc.vector.tensor_copy(out=vnf[0:4, 512:640], in_=ps[0:4, 512:640]), [b_ps, b_vnf], [b_vnf])
            dma('sp', v_s_o[b, :, :], vnf[0:4, :], [b_vnf], [OUTB], "st1")
            kts = []
            for j in range(NPAGE):
                kts.append(dict(nk=128, kaT=[kaTs[:, h, j * 128:(j + 1) * 128] for h in range(4)], kbT=kbTs[:, j * 128:(j + 1) * 128],
                                kiT_base=kiTs, kiT_off=j * 128, va=[pa[:, j, h * 256 + 128:h * 256 + 256] for h in range(4)], vb=pb[:, j, 128:256],
                                pkc=poskc[:, j:j + 1], bufs=[b_kaTs, b_kbTs, b_kiTs, b_pa, b_pb]))
            kts.append(dict(nk=4, kaT=[kaTs[:, h, PAST:PAST + 4] for h in range(4)], kbT=kbTs[:, PAST:PAST + 4], kiT_base=kiTs, kiT_off=PAST,
                            va=[vnew[0:4, h * 128:(h + 1) * 128] for h in range(4)], vb=vnew[0:4, 512:640], pkc=poskc[0:4, NPAGE:NPAGE + 1],
                            bufs=[b_kaTs, b_kbTs, b_kiTs, b_vnew]))
            wtokv = wtok_s[:, b, :]
            attention_tile(4 * b, 4, kts, PAST + 4, posq_col[0:4, NTQ - 1:NTQ], KS, nvis=PAST)

    sv = A.top
    A.top = 0
    LS = PAST + 128
    kaTs, b_kaTs = A.alloc("kaTs", [4, LS], BF16)
    kbTs, b_kbTs = A.alloc("kbTs", [LS], BF16)
    kiTs, b_kiTs = A.alloc("kiTs", [LS], BF16)
    A_pa = A.alloc("pa", [NPAGE, 1024], BF16)
    A_pb = A.alloc("pb", [NPAGE, 320], BF16)
    assert A.top <= (4 * LCAP + 2 * LCAP + (LCAP // 128) * 640) * 2, "sample buffers exceed dead prompt K/V region"
    A.top = sv
    A_knb = A.alloc("knb", [6, 64], BF16)
    A_vnew = A.alloc("vnew", [640], BF16)
    A_vnf = A.alloc("vnf", [640], F32)
    wtok_s, b_wtok_s = A.alloc("wtok_s", [NB, 8], F32)

    GROUPS = cfg.get('GROUPS', 'hos')
    try:
        if 'h' in GROUPS:
            run_group('halo', 0)
    except _Stop:
        return finish_all()
    if STOP <= 2:
        return finish_all()
    for gi in cfg.get('GILIST', range(NG if 'o' in GROUPS else 0)):
        run_group('own', gi)
    if STOP <= 3:
        return finish_all()
    if 's' in GROUPS:
        run_group('sample', 0)
    for sname, v in S.cnt.items():
        if v > 0 and S.seen['sp'].get(sname, 0) < v:
            nc.sync.wait_ge(S.sems[sname], v)
    return nc, dict(peak=A.peak)


def _consts():
    c = np.zeros((128, 992), np.float32)
    c[:, 0:512] = np.arange(512, dtype=np.float32)[None, :]
    c[:, 512:576] = (512.0 * np.arange(64, dtype=np.float32))[None, :]
    c[:, 576:608] = (0.5 ** (np.arange(32) + 1)).astype(np.float32)[None, :]
    c[:, 608:736] = np.eye(128, dtype=np.float32)
    c[:, 736:864] = (np.arange(128)[:, None] <= np.arange(128)[None, :]).astype(np.float32)
    c[:, 864:992] = 128.0 * np.arange(128, dtype=np.float32)[None, :] + np.arange(128, dtype=np.float32)[:, None]
    return c


def make_in_maps(cfg, inp):
    SEQ, NB, NPAGE, NPOOL = cfg['SEQ'], cfg['NB'], cfg['NPAGE'], cfg['NPOOL']
    NCORE = cfg['NCORE']
    PAST = NPAGE * 128
    NG = SEQ // 1024
    TOWN, NH, NS = NG * 512, 16 * NG, NB * 4
    NQ = NH + TOWN
    NTQ = 1 + TOWN // 128 + 1
    f = lambda a: np.ascontiguousarray(a, dtype=np.float32)
    xp = np.asarray(inp['x_prompt'])
    xs = np.asarray(inp['x_sample'])
    ca = np.asarray(inp['cache_a'])[0].reshape(NPOOL * 128, 1024)
    cb = np.asarray(inp['cache_b'])[0].reshape(NPOOL * 128, 320)
    pt = np.asarray(inp['page_table']).astype(np.int32)
    st = np.asarray(inp['state_pool'])[0]
    w_s = np.asarray(inp['w_s'])[0]
    b_s = np.asarray(inp['b_s'])[0]
    shared = dict(
        consts=_consts(), cache_a=f(ca), cache_b=f(cb),
        w_in_e=f(inp['w_in_e'][0]), lam_rep=f(np.broadcast_to(np.asarray(inp['lam_e'])[0].reshape(1, 256), (128, 256))),
        sublng=f(np.asarray(inp['subln_g'])[0].reshape(128, 1)), w_out_e=f(inp['w_out_e'][0]), w_in_o=f(inp['w_in_o'][0]), w_out_o=f(inp['w_out_o'][0]),
        w_pool=f(inp['w_pool'][0]), pool_scaleT=f(np.asarray(inp['pool_scale'])[0].reshape(4, 128).T),
        sgu_gb=f(np.broadcast_to(np.concatenate([np.asarray(inp['sgu_g'])[0], np.asarray(inp['sgu_b'])[0]])[None, :], (128, 1024))),
        w_sT=f(np.transpose(w_s, (2, 0, 1))),
        w_sT_s=f(np.tile(np.transpose(w_s[:, :4, :4], (2, 0, 1)), (16, 1, 16))),
        mask_s=f(np.kron(np.eye(16), (np.arange(4)[:, None] <= np.arange(4)[None, :]).astype(np.float32))),
        bs_rep=f(np.broadcast_to(b_s[None, :, :], (128, 4, 128))),
        bs_rep_s=f(np.broadcast_to(np.tile(b_s[:, :4], (1, 16))[None, :, :], (128, 4, 64))),
        ln_gb=f(np.concatenate([np.asarray(inp['ln_g']).reshape(4, 8, 128).transpose(2, 0, 1).reshape(128, 32),
                                np.asarray(inp['ln_b']).reshape(4, 8, 128).transpose(2, 0, 1).reshape(128, 32)], 1)),
        w_mlp1=f(inp['w_mlp1']), w_mlp2=f(inp['w_mlp2']),
    )
    maps = []
    for c in range(NCORE):
        s, h = c // 2, c % 2
        xT = xp[s].T
        own_pos = np.concatenate([np.arange(1024 * i + 512 * h, 1024 * i + 512 * h + 512) for i in range(NG)])
        halo_pos = np.concatenate([np.arange(1024 * i + 512 * h - 16, 1024 * i + 512 * h) for i in range(NG)])
        hvalid = (halo_pos >= 0).astype(np.float32)
        halo_idx = np.maximum(halo_pos, 0)
        qpos = np.concatenate([halo_idx, own_pos])
        xT_q = xT[:, qpos]
        bsl = slice(NB * c, NB * (c + 1))
        xT_s = xs[bsl].reshape(NS, 1024).T
        spos = (PAST + np.tile(np.arange(4), NB)).astype(np.float32)
        posq_row = np.broadcast_to(np.concatenate([qpos.astype(np.float32), spos])[None, :], (128, NQ + NS))
        posq_col = np.zeros((128, NTQ), np.float32)
        posq_col[:NH, 0] = halo_idx
        posq_col[:, 1:1 + TOWN // 128] = own_pos.reshape(TOWN // 128, 128).T
        posq_col[:, NTQ - 1] = PAST + np.arange(128)
        hv = np.zeros((128, 64), np.float32)
        hv[:, :NH] = hvalid[None, :]
        invc = np.zeros((128, 4, 16), np.float32)
        for g in range(4):
            w = 2 << g
            p0 = own_pos[0] + np.arange(16)
            invc[:, g, :] = (1.0 / np.minimum(w, p0 + 1))[None, :]
        m = dict(shared)
        m.update(xT_seq=f(xT), xT_q=f(xT_q), xT_s=f(xT_s), posq_row=f(posq_row), posq_col=posq_col,
                 pt=np.ascontiguousarray(np.broadcast_to(pt[bsl].reshape(1, NB * NPAGE), (128, NB * NPAGE)).astype(np.int32)),
                 stateT=f(np.transpose(st[bsl], (2, 0, 1))), halo_valid=hv, invc=invc.reshape(128, 64))
        maps.append(m)
    return maps


def assemble(cfg, res, nbatch):
    SEQ, NB, NPAGE = cfg['SEQ'], cfg['NB'], cfg['NPAGE']
    NCORE = cfg['NCORE']
    NG = SEQ // 1024
    NS = NB * 4
    DB = NB * NCORE
    y_p = np.zeros((nbatch, SEQ, 1024), np.float32)
    y_s = np.zeros((DB, 4, 1024), np.float32)
    na_p = np.zeros((1, nbatch, SEQ, 4, 256), np.float32)
    nb_p = np.zeros((1, nbatch, SEQ, 320), np.float32)
    pool_p = np.zeros((1, nbatch, 15, 512), np.float32)
    na_s = np.zeros((1, DB, 4, 4, 256), np.float32)
    nb_s = np.zeros((1, DB, 4, 320), np.float32)
    pool_s = np.zeros((1, DB, 15, 512), np.float32)
    v_s = np.zeros((1, DB, 4, 512), np.float32)
    for c in range(NCORE):
        r = res[c]
        s, h = c // 2, c % 2
        for i in range(NG):
            y_p[s, 1024 * i + 512 * h:1024 * i + 512 * h + 512] = r['yT_q'][:, i * 512:(i + 1) * 512].T
        if h == 0:
            kT = r['kT_all']
            v = r['v_all']
            na_p[0, s, :, :, 0:128] = kT[0:512].T.reshape(SEQ, 4, 128)
            na_p[0, s, :, :, 128:256] = v[:, 0:512].reshape(SEQ, 4, 128)
            nb_p[0, s, :, 0:128] = kT[512:640].T
            nb_p[0, s, :, 128:256] = v[:, 512:640]
            nb_p[0, s, :, 256:320] = kT[640:704].T
        else:
            pool_p[0, s] = r['xc_tail'][:, 1:16].T
        bsl = slice(NB * c, NB * (c + 1))
        y_s[bsl] = r['yT_s'].T.reshape(NB, 4, 1024)
        kTs = r['kT_s']
        vs = r['v_s']
        na_s[0, bsl, :, :, 0:128] = kTs[0:512].T.reshape(NB, 4, 4, 128)
        na_s[0, bsl, :, :, 128:256] = vs[:, :, 0:512].reshape(NB, 4, 4, 128)
        nb_s[0, bsl, :, 0:128] = kTs[512:640].T.reshape(NB, 4, 128)
        nb_s[0, bsl, :, 128:256] = vs[:, :, 512:640]
        nb_s[0, bsl, :, 256:320] = kTs[640:704].T.reshape(NB, 4, 64)
        pool_s[0, bsl] = np.transpose(r['poolT_s'], (1, 2, 0))
        v_s[0, bsl] = r['vn_s'].reshape(NB, 4, 512)
    return (y_p, y_s, na_p, nb_p, pool_p, na_s, nb_s, pool_s, v_s)


_CACHE = {}


def run_cfg(cfg, inp, nbatch):
    key = tuple(sorted(cfg.items()))
    if key not in _CACHE:
        _CACHE[key] = build(cfg)[0]
    nc = _CACHE[key]
    maps = make_in_maps(cfg, inp)
    res = run_bass_kernel_spmd(nc, maps, core_ids=list(range(cfg['NCORE'])))
    return assemble(cfg, res.results, nbatch)


def kernel(**inputs):
    cfg = dict(SEQ=4096, NB=16, NPAGE=16, NPOOL=int(np.asarray(inputs['cache_a']).shape[1]), NCORE=8)
    return run_cfg(cfg, inputs, 4)
```

```python
import numpy as np
import concourse.bass as bass
import concourse.mybir as mybir
from concourse.bass_utils import run_bass_kernel_spmd

F32 = mybir.dt.float32
BF16 = mybir.dt.bfloat16
I32 = mybir.dt.int32
U32 = mybir.dt.uint32
ALU = mybir.AluOpType
AF = mybir.ActivationFunctionType
AX = mybir.AxisListType

QA, KA, VA, QB, KB, VB, QI, KI, WI = 0, 512, 1024, 1536, 2048, 2176, 2304, 2816, 2880
ALPHA = 4 ** 0.25
EPS = 1e-5
LAM_INIT0 = 0.8 - 0.6
NBIS = 16


class Buf:
    def __init__(self, name, lo=0, hi=0):
        self.name = name
        self.w = None
        self.r = {}
        self.lo, self.hi = lo, hi
        self.overlaps = []
        self.psum = False


class Sched:
    def __init__(self, nc):
        self.nc = nc
        self.eng = {'pe': nc.tensor, 'act': nc.scalar, 'dve': nc.vector, 'pool': nc.gpsimd, 'sp': nc.sync}
        self.sems = {}
        self.cnt = {}
        self.seen = {e: {} for e in self.eng}
        self.swdge_tag = None

    def newsem(self, name):
        if name not in self.sems:
            self.sems[name] = self.nc.alloc_semaphore(name)
            self.cnt[name] = 0
        return name

    def _wait(self, e, deps):
        best = {}
        for d in deps:
            if d is None:
                continue
            s, v = d
            if best.get(s, 0) < v:
                best[s] = v
        for s, v in best.items():
            if self.seen[e].get(s, 0) >= v:
                continue
            self.eng[e].wait_ge(self.sems[s], v)
            self.seen[e][s] = v

    def op(self, e, fn, reads=(), writes=(), sem=None, inc=1, multi=False):
        deps = set()
        for b in reads:
            deps.add(b.w)
            for o in b.overlaps:
                deps.add(o.w)
            if b.psum:
                deps.update(b.r.items())
        for b in writes:
            deps.add(b.w)
            deps.update(b.r.items())
            for o in b.overlaps:
                deps.add(o.w)
                deps.update(o.r.items())
        self._wait(e, deps)
        if sem is None:
            sem = self.newsem('c_' + e)
        if multi:
            inss = fn()
            for ins in inss:
                ins.then_inc(self.sems[sem], inc)
                self.cnt[sem] += inc
        else:
            ins = fn()
            ins.then_inc(self.sems[sem], inc)
            self.cnt[sem] += inc
        tag = (sem, self.cnt[sem])
        if e == 'pool' and inc == 16:
            self.swdge_tag = tag
        for b in writes:
            b.w = tag
            b.r = {}
        for b in reads:
            if b.r.get(sem, 0) < self.cnt[sem]:
                b.r[sem] = self.cnt[sem]

    def finish(self, e, bufs):
        deps = set()
        for b in bufs:
            deps.add(b.w)
            deps.update(b.r.items())
        self._wait(e, deps)


class WRef:
    def __init__(self, ap, bufs):
        self.ap, self.bufs = ap, bufs

    def __getitem__(self, key):
        return WRef(self.ap[key], self.bufs)


class Arena:
    def __init__(self, nc, nbytes):
        self.nbytes = nbytes
        self.t = nc.alloc_sbuf_tensor("arena", [128, nbytes // 2], BF16).ap()
        self.top = 0
        self.all = []
        self.peak = 0

    def alloc(self, name, free_shape, dtype):
        esz = 4 if dtype in (F32, I32, U32) else 2
        n = int(np.prod(free_shape))
        nb = (n * esz + 63) // 64 * 64
        off = self.top
        self.top += nb
        self.peak = max(self.peak, self.top)
        assert self.top <= self.nbytes, (name, self.top, self.nbytes)
        ap = self.t[:, off // 2:(off + n * esz) // 2]
        if esz == 4:
            ap = ap.bitcast(dtype)
        elif dtype != BF16:
            ap = ap.bitcast(dtype)
        if len(free_shape) == 2:
            ap = ap.rearrange("p (a b) -> p a b", a=free_shape[0])
        elif len(free_shape) == 3:
            ap = ap.rearrange("p (a b c) -> p a b c", a=free_shape[0], b=free_shape[1])
        b = Buf(name, off, off + nb)
        for o in self.all:
            if o.lo < b.hi and b.lo < o.hi:
                o.overlaps.append(b)
                b.overlaps.append(o)
        self.all.append(b)
        return ap, b


def build(cfg):
    SEQ, NB, NPAGE, NPOOL = cfg['SEQ'], cfg['NB'], cfg['NPAGE'], cfg['NPOOL']
    PAST = NPAGE * 128
    NG = SEQ // 1024
    GQ = 512
    TOWN = NG * GQ
    NH = 16 * NG
    NS = NB * 4
    NQ = NH + TOWN
    NKT = SEQ // 128
    LCAP = max(SEQ, PAST + 128)
    KP = min(256, SEQ // 4)
    KS = min(256, (PAST + 4) // 4)
    NTQ = 1 + TOWN // 128 + 1

    nc = bass.Bass("TRN2", target_bir_lowering=False)
    S = Sched(nc)

    def din(name, shape, dt=F32):
        return nc.dram_tensor(name, list(shape), dt, kind="ExternalInput").ap()

    def dout(name, shape, dt=F32):
        return nc.dram_tensor(name, list(shape), dt, kind="ExternalOutput").ap()

    xT_seq = din("xT_seq", [1024, SEQ])
    xT_q = din("xT_q", [1024, NQ])
    xT_s = din("xT_s", [1024, NS])
    posq_row_d = din("posq_row", [128, NQ + NS])
    posq_col_d = din("posq_col", [128, NTQ])
    consts_d = din("consts", [128, 512 + 64 + 32 + 128 + 128 + 128])
    cache_a = din("cache_a", [NPOOL * 128, 1024])
    cache_b = din("cache_b", [NPOOL * 128, 320])
    pt_d = din("pt", [128, NB * NPAGE], I32)
    stateT = din("stateT", [512, NB, 15])
    w_in_e = din("w_in_e", [1024, 2888])
    lam_rep = din("lam_rep", [128, 256])
    sublng = din("sublng", [128, 1])
    w_out_e = din("w_out_e", [1024, 1024])
    w_in_o = din("w_in_o", [1024, 1536])
    w_out_o = din("w_out_o", [1024, 1024])
    w_pool = din("w_pool", [4, 128, 128])
    pool_scaleT = din("pool_scaleT", [128, 4])
    sgu_gb = din("sgu_gb", [128, 1024])
    w_sT = din("w_sT", [128, 4, 128])
    w_sT_s = din("w_sT_s", [64, 4, 64])
    mask_s_d = din("mask_s", [64, 64])
    bs_rep = din("bs_rep", [128, 4, 128])
    bs_rep_s = din("bs_rep_s", [128, 4, 64])
    ln_gb = din("ln_gb", [128, 64])
    w_mlp1 = din("w_mlp1", [2, 1024, 4096])
    w_mlp2 = din("w_mlp2", [2, 4096, 1024])
    halo_valid = din("halo_valid", [128, 64])
    invc_d = din("invc", [128, 64])

    yT_q = dout("yT_q", [1024, TOWN])
    yT_s = dout("yT_s", [1024, NS])
    kT_all = dout("kT_all", [704, SEQ])
    v_all = dout("v_all", [SEQ, 640])
    xc_tail = dout("xc_tail", [512, 16])
    kT_s_o = dout("kT_s", [704, NS])
    v_s_o = dout("v_s", [NB, 4, 640])
    poolT_s = dout("poolT_s", [512, NB, 15])
    vn_s_o = dout("vn_s", [NS, 512])
    OUTB = Buf("outputs")

    A = Arena(nc, 207 * 1024)
    kaT, b_kaT = A.alloc("kaT", [4, LCAP], BF16)
    kbT, b_kbT = A.alloc("kbT", [LCAP], BF16)
    kiT, b_kiT = A.alloc("kiT", [LCAP], BF16)
    va, b_va = A.alloc("va", [LCAP // 128, 512], BF16)
    vb, b_vb = A.alloc("vb", [LCAP // 128, 128], BF16)
    cst, b_cst = A.alloc("cst", [512 + 64 + 32 + 128 + 128 + 128], F32)
    ramp512 = cst[:, 0:512]
    blkoff = cst[:, 512:576]
    pow2 = cst[:, 576:608]
    tril_f = cst[:, 736:864]
    poskc = cst[:, 864:992]
    identb, b_identb = A.alloc("identb", [128], BF16)
    onesb, b_onesb = A.alloc("onesb", [128], BF16)
    onesd, b_onesd = A.alloc("onesd", [128], BF16)
    ones128, b_ones128 = A.alloc("ones128", [128], BF16)
    posq_col, b_posq_col = A.alloc("posq_col", [NTQ], F32)
    lngb, b_lngb = A.alloc("lngb", [64], F32)
    small, b_small = A.alloc("small", [16], F32)
    pscT, b_pscT = A.alloc("pscT", [4], F32)
    wpool_b, b_wpool = A.alloc("wpool", [4, 128], BF16)
    wsT_b, b_wsT = A.alloc("wsT", [4, 128], BF16)
    wsTs_b, b_wsTs = A.alloc("wsTs", [4, 64], BF16)
    bsr, b_bsr = A.alloc("bsr", [4, 128], F32)
    bsrs, b_bsrs = A.alloc("bsrs", [4, 64], F32)
    sgugb, b_sgugb = A.alloc("sgugb", [1024], F32)
    hval, b_hval = A.alloc("hval", [64], F32)
    invc, b_invc = A.alloc("invc", [4, 16], F32)
    xch, b_xch = A.alloc("xch", [4, 64], F32)
    wwi, b_wwi = A.alloc("wwi", [8, 8], BF16)
    A_pti = A.alloc("pti", [NB * NPAGE], I32)
    A_idx = A.alloc("idx", [NB * NPAGE], I32)
    NSLOT = 2
    slots = [A.alloc("wslot%d" % i, [4096], BF16) for i in range(NSLOT)]
    for i in range(NSLOT):
        S.newsem("ws%d" % i)
    slot_i = [0]

    psb = []
    for i in range(4):
        t = nc.alloc_psum_tensor("ps%d" % i, [128, 1024], F32).ap()
        psb.append((t, Buf("ps%d" % i)))
        psb[-1][1].psum = True

    for s in ["ld0", "ld1", "ld2", "ld3", "st0", "st1", "gath", "cst"]:
        S.newsem(s)

    def dma(e, out, in_, reads, writes, sem=None):
        writes = [w for w in writes if w is not OUTB]
        bb = (writes + reads)[0]
        sem = S.newsem("d_" + bb.name)
        S.op(e, lambda: S.eng[e].dma_start(out=out, in_=in_), reads, writes, sem=sem, inc=16)

    def wload(src2d, kc, ncols):
        i = slot_i[0] % NSLOT
        slot_i[0] += 1
        ap, b = slots[i]
        v = ap[:, 0:kc * ncols].rearrange("p (k n) -> p k n", k=kc)
        if isinstance(src2d, WRef):
            dma('sp', v, src2d.ap.rearrange("(k p) n -> p k n", p=128), list(src2d.bufs), [b])
        else:
            dma('pool', v, src2d.rearrange("(k p) n -> p k n", p=128), [], [b], "ws%d" % i)
        return v, b

    def precast(name, src, R, C):
        dst = nc.dram_tensor("bf_" + name, [R, C], BF16, kind="Internal").ap()
        bufs = []
        for c0 in range(0, C, 2048):
            c1 = min(C, c0 + 2048)
            bb = Buf("bf_%s_%d" % (name, c0))
            bufs.append(bb)
            dma('pool', dst[:, c0:c1], src[:, c0:c1], [], [bb])
        return WRef(dst, bufs)

    _op_real = S.op
    NOC = cfg.get('NOCONST', 0)
    if NOC == 1:
        S.op = lambda *a, **k: None
    elif NOC == 2:
        S.op = lambda e, fn, reads=(), writes=(), sem=None, inc=1, multi=False: (_op_real(e, fn, reads, writes, sem=sem, inc=inc, multi=multi) if inc == 16 else None)
    elif NOC == 3:
        S.op = lambda e, fn, reads=(), writes=(), sem=None, inc=1, multi=False: (None if (inc == 16 and e == 'pool') else _op_real(e, fn, reads, writes, sem=sem, inc=inc, multi=multi))
    dma('sp', cst, consts_d, [], [b_cst], "cst")
    dma('sp', posq_col, posq_col_d, [], [b_posq_col], "cst")
    dma('sp', lngb, ln_gb, [], [b_lngb], "cst")
    dma('sp', pscT, pool_scaleT, [], [b_pscT], "cst")
    dma('sp', bsr, bs_rep, [], [b_bsr], "cst")
    dma('sp', bsrs, bs_rep_s, [], [b_bsrs], "cst")
    dma('sp', sgugb, sgu_gb, [], [b_sgugb], "cst")
    dma('sp', hval, halo_valid, [], [b_hval], "cst")
    dma('sp', invc.rearrange("p a b -> p (a b)"), invc_d, [], [b_invc], "cst")
    dma('pool', identb, consts_d[:, 608:736], [], [b_identb], "ld0")
    dma('pool', wwi, w_in_e[:, WI:WI + 8].rearrange("(k p) n -> p k n", p=128), [], [b_wwi])
    dma('pool', wpool_b, w_pool.rearrange("g c d -> c g d"), [], [b_wpool], "ld0")
    KEEPC = cfg.get('KEEPC', 'ABCDEFGH')
    if 'A' in KEEPC:
        S.op('dve', lambda: nc.vector.memset(onesb, 1.0), [], [b_onesb])
    if 'A' in KEEPC:
        S.op('dve', lambda: nc.vector.memset(onesd, 1.0 / 1024), [], [b_onesd])
    if 'A' in KEEPC:
        S.op('dve', lambda: nc.vector.memset(ones128, 1.0 / 128), [], [b_ones128])

    mk = A.top
    t0, b_t0 = A.alloc("t0", [256], F32)
    t1, b_t1 = A.alloc("t1", [4, 128], F32)
    t2, b_t2 = A.alloc("t2", [64, 64], F32)
    t3, b_t3 = A.alloc("t3", [64], F32)
    if 'B' in KEEPC and 'E' in KEEPC:
        dma('sp', t0, lam_rep, [], [b_t0], "ld1")
    if 'B' in KEEPC and 'E' in KEEPC:
        S.op('dve', lambda: nc.vector.tensor_tensor(out=t0[:, 0:64], in0=t0[:, 0:64], in1=t0[:, 64:128], op=ALU.mult), [b_t0], [b_t0])
    if 'B' in KEEPC and 'E' in KEEPC:
        S.op('dve', lambda: nc.vector.tensor_tensor(out=t0[:, 128:192], in0=t0[:, 128:192], in1=t0[:, 192:256], op=ALU.mult), [b_t0], [b_t0])
    if 'B' in KEEPC and 'E' in KEEPC:
        S.op('dve', lambda: nc.vector.reduce_sum(out=small[:, 2:3], in_=t0[:, 0:64], axis=AX.X), [b_t0], [b_small])
    if 'B' in KEEPC and 'E' in KEEPC:
        S.op('dve', lambda: nc.vector.reduce_sum(out=small[:, 3:4], in_=t0[:, 128:192], axis=AX.X), [b_t0, b_small], [b_small])
    if 'B' in KEEPC and 'F' in KEEPC:
        S.op('act', lambda: nc.scalar.activation(out=small[:, 4:6], in_=small[:, 2:4], func=AF.Exp), [b_small], [b_small])
    if 'B' in KEEPC and 'G' in KEEPC:
        S.op('dve', lambda: nc.vector.tensor_tensor(out=small[:, 6:7], in0=small[:, 5:6], in1=small[:, 4:5], op=ALU.subtract), [b_small], [b_small])
    if 'B' in KEEPC and 'G' in KEEPC:
        S.op('dve', lambda: nc.vector.tensor_scalar(out=small[:, 0:1], in0=small[:, 6:7], scalar1=-LAM_INIT0, scalar2=None, op0=ALU.add), [b_small], [b_small])
    if 'B' in KEEPC and 'H' in KEEPC:
        dma('sp', small[:, 7:8], sublng, [], [b_small], "ld1")
    if 'B' in KEEPC and 'H' in KEEPC:
        S.op('dve', lambda: nc.vector.tensor_scalar(out=small[:, 1:2], in0=small[:, 7:8], scalar1=1.0 - LAM_INIT0, scalar2=None, op0=ALU.mult), [b_small], [b_small])
    if 'C' in KEEPC:
        dma('sp', t1, w_sT, [], [b_t1], "ld2")
    for g in (range(4) if 'C' in KEEPC else []):
        S.op('dve', lambda g=g: nc.vector.tensor_tensor(out=wsT_b[:, g, :], in0=t1[:, g, :], in1=tril_f, op=ALU.mult), [b_t1, b_cst], [b_wsT])
    if 'D' in KEEPC:
        dma('sp', t2[0:64, 0:4, :], w_sT_s, [], [b_t2], "ld3")
    if 'D' in KEEPC:
        dma('sp', t3[0:64, :], mask_s_d, [], [b_t3], "ld3")
    for g in (range(4) if 'D' in KEEPC else []):
        S.op('dve', lambda g=g: nc.vector.tensor_tensor(out=wsTs_b[0:64, g, :], in0=t2[0:64, g, :], in1=t3[0:64, :], op=ALU.mult), [b_t2, b_t3], [b_wsTs])
    A.top = mk

    S.op = _op_real
    dma('sp', A_pti[0], pt_d, [], [A_pti[1]])
    S.op('dve', lambda: nc.vector.tensor_scalar(out=A_idx[0], in0=A_pti[0], scalar1=128.0, scalar2=poskc[:, 0:1], op0=ALU.mult, op1=ALU.add), [A_pti[1], b_cst], [A_idx[1]])
    NEG = -1.0e30
    if cfg.get('STOP', 9) <= 0:
        for sname, v in S.cnt.items():
            if v > 0 and S.seen['sp'].get(sname, 0) < v:
                nc.sync.wait_ge(S.sems[sname], v)
        return nc, dict(peak=A.peak)

    def evac_act(out, in_, reads, writes, func=AF.Copy, scale=1.0):
        S.op('act', lambda: nc.scalar.activation(out=out, in_=in_, func=func, scale=scale), reads, writes)

    def layernorm(x, b_x, xb, b_xb, n, lnidx, scr):
        (sq, b_sq), (m2, b_m2), (rs, b_rs) = scr
        pm, b_pm = psb[0]
        pv, b_pv = psb[1]
        for c in range(8):
            evac_act(xb[:, c, 0:n], x[:, c, 0:n], [b_x], [b_xb])
            evac_act(sq[:, c, 0:n], x[:, c, 0:n], [b_x], [b_sq], func=AF.Square)

        def f():
            last = None
            for c in range(8):
                last = nc.tensor.matmul(pm[:, 0:n], lhsT=onesd, rhs=xb[:, c, 0:n], start=(c == 0), stop=(c == 7))
            return last
        S.op('pe', f, [b_xb, b_onesd], [b_pm])

        def f2():
            last = None
            for c in range(8):
                last = nc.tensor.matmul(pv[:, 0:n], lhsT=onesd, rhs=sq[:, c, 0:n], start=(c == 0), stop=(c == 7))
            return last
        S.op('pe', f2, [b_sq, b_onesd], [b_pv])
        evac_act(m2[:, 0:n], pm[:, 0:n], [b_pm], [b_m2], func=AF.Square)
        S.op('dve', lambda: nc.vector.tensor_tensor(out=rs[:, 0:n], in0=pv[:, 0:n], in1=m2[:, 0:n], op=ALU.subtract), [b_pv, b_m2], [b_rs])
        S.op('dve', lambda: nc.vector.tensor_scalar(out=rs[:, 0:n], in0=rs[:, 0:n], scalar1=0.0, scalar2=EPS, op0=ALU.max, op1=ALU.add), [b_rs], [b_rs])
        S.op('act', lambda: nc.scalar.activation(out=rs[:, 0:n], in_=rs[:, 0:n], func=AF.Sqrt), [b_rs], [b_rs])
        S.op('dve', lambda: nc.vector.reciprocal(out=rs[:, 0:n], in_=rs[:, 0:n]), [b_rs], [b_rs])
        S.op('act', lambda: nc.scalar.activation(out=m2[:, 0:n], in_=pm[:, 0:n], func=AF.Copy), [b_pm], [b_m2])
        for c in range(8):
            S.op('dve', lambda c=c: nc.vector.tensor_tensor(out=x[:, c, 0:n], in0=x[:, c, 0:n], in1=m2[:, 0:n], op=ALU.subtract), [b_x, b_m2], [b_x])
            S.op('pool', lambda c=c: nc.gpsimd.tensor_tensor(out=x[:, c, 0:n], in0=x[:, c, 0:n], in1=rs[:, 0:n], op=ALU.mult), [b_x, b_rs], [b_x])
            S.op('dve', lambda c=c: nc.vector.tensor_scalar(out=x[:, c, 0:n], in0=x[:, c, 0:n], scalar1=lngb[:, lnidx * 8 + c:lnidx * 8 + c + 1],
                                                            scalar2=lngb[:, 32 + lnidx * 8 + c:32 + lnidx * 8 + c + 1], op0=ALU.mult, op1=ALU.add), [b_x, b_lngb], [b_x])
            evac_act(xb[:, c, 0:n], x[:, c, 0:n], [b_x], [b_xb])

    def proj_fm(wsrc, ncols, xb, b_xb, n, kc, consume):
        done = 0
        pi = 0
        while done < ncols:
            cw = min(512, ncols - done)
            wv, b_w = wload(wsrc[:, done:done + cw], kc, cw)
            for j in range(0, cw, 128):
                m = min(128, cw - j)
                ps, b_ps = psb[2 + (pi % 2)]
                pi += 1

                def f(j=j, m=m, ps=ps, wv=wv):
                    last = None
                    for k in range(kc):
                        last = nc.tensor.matmul(ps[0:m, 0:n], lhsT=wv[:, k, j:j + m], rhs=xb[:, k, 0:n], start=(k == 0), stop=(k == kc - 1))
                    return last
                S.op('pe', f, [b_w, b_xb], [b_ps])
                consume((done + j) // 128, m, ps, b_ps)
            done += cw

    def mlp(l, x, b_x, xb, b_xb, n, hT, b_hT):
        for fc in range(8):
            wv, b_w = wload(w_mlp1[l][:, fc * 512:(fc + 1) * 512], 8, 512)
            for sub in range(4):
                ps, b_ps = psb[2 + (sub % 2)]

                def f(sub=sub, ps=ps, wv=wv):
                    last = None
                    for k in range(8):
                        last = nc.tensor.matmul(ps[:, 0:n], lhsT=wv[:, k, sub * 128:(sub + 1) * 128], rhs=xb[:, k, 0:n], start=(k == 0), stop=(k == 7))
                    return last
                S.op('pe', f, [b_w, b_xb], [b_ps])
                fi = fc * 4 + sub
                evac_act(hT[:, fi, 0:n], ps[:, 0:n], [b_ps], [b_hT], func=AF.Relu)
                S.op('pool', lambda fi=fi: nc.gpsimd.tensor_tensor(out=hT[:, fi, 0:n], in0=hT[:, fi, 0:n], in1=hT[:, fi, 0:n], op=ALU.mult), [b_hT], [b_hT])
        for oc in range(8):
            wv, b_w = wload(w_mlp2[l][:, oc * 128:(oc + 1) * 128], 32, 128)
            ps, b_ps = psb[2 + (oc % 2)]

            def f(ps=ps, wv=wv):
                last = None
                for k in range(32):
                    last = nc.tensor.matmul(ps[:, 0:n], lhsT=wv[:, k, :], rhs=hT[:, k, 0:n], start=(k == 0), stop=(k == 31))
                return last
            S.op('pe', f, [b_w, b_hT], [b_ps])
            S.op('dve', lambda oc=oc, ps=ps: nc.vector.scalar_tensor_tensor(out=x[:, oc, 0:n], in0=x[:, oc, 0:n], scalar=ALPHA, in1=ps[:, 0:n], op0=ALU.mult, op1=ALU.add), [b_x, b_ps], [b_x])

    def outproj(wsrc, aT, b_aT, x, b_x, n):
        for half in range(2):
            wv, b_w = wload(wsrc[:, half * 512:(half + 1) * 512], 8, 512)
            for sub in range(4):
                oc = half * 4 + sub
                ps, b_ps = psb[2 + (sub % 2)]

                def f(sub=sub, ps=ps, wv=wv):
                    last = None
                    for k in range(8):
                        last = nc.tensor.matmul(ps[:, 0:n], lhsT=wv[:, k, sub * 128:(sub + 1) * 128], rhs=aT[:, k, 0:n], start=(k == 0), stop=(k == 7))
                    return last
                S.op('pe', f, [b_w, b_aT], [b_ps])
                S.op('dve', lambda oc=oc, ps=ps: nc.vector.scalar_tensor_tensor(out=x[:, oc, 0:n], in0=x[:, oc, 0:n], scalar=ALPHA, in1=ps[:, 0:n], op0=ALU.mult, op1=ALU.add), [b_x, b_ps], [b_x])

    mk = A.top
    wkv, b_wkv = A.alloc("wkv", [8, 1344], BF16)
    for (dst, src, w) in [(0, KA, 512), (512, KB, 128), (640, KI, 64), (704, VA, 512), (1216, VB, 128)]:
        dma('pool', wkv[:, :, dst:dst + w], w_in_e[:, src:src + w].rearrange("(k p) n -> p k n", p=128), [], [b_wkv], "ld1")
    PRECAST = cfg.get('PRECAST', 1)
    if PRECAST:
        bf_in_e = precast("w_in_e", w_in_e, 1024, 2888)
        bf_out_e = precast("w_out_e", w_out_e, 1024, 1024)
        bf_m1 = [None, None]
        bf_m2 = [None, None]
        bf_m1[0] = precast("w_mlp1_0", w_mlp1[0], 1024, 4096)
        bf_m2[0] = precast("w_mlp2_0", w_mlp2[0], 4096, 1024)
        bf_in_o = precast("w_in_o", w_in_o, 1024, 1536)
        bf_out_o = precast("w_out_o", w_out_o, 1024, 1024)
        bf_m1[1] = precast("w_mlp1_1", w_mlp1[1], 1024, 4096)
        bf_m2[1] = precast("w_mlp2_1", w_mlp2[1], 4096, 1024)
    xsb = [A.alloc("xsb%d" % i, [8, 512], BF16) for i in range(2)]
    xstg = [A.alloc("xstg%d" % i, [8, 512], F32) for i in range(2)]
    stg = [A.alloc("stg%d" % i, [704], F32) for i in range(2)]
    stgk = [A.alloc("stgk%d" % i, [512], F32) for i in range(2)]
    sti = 0
    for gi in range(min(SEQ // 512, cfg.get('P1G', 99))):
        xs, b_xs = xsb[gi % 2]
        xg, b_xg = xstg[gi % 2]
        dma('sp', xg, xT_seq[:, gi * 512:(gi + 1) * 512].rearrange("(k p) n -> p k n", p=128), [], [b_xg])
        S.op('dve', lambda xs=xs, xg=xg: nc.vector.tensor_copy(out=xs[:, 0:4, :], in_=xg[:, 0:4, :]), [b_xg], [b_xs])
        S.op('pool', lambda xs=xs, xg=xg: nc.gpsimd.tensor_copy(out=xs[:, 4:8, :], in_=xg[:, 4:8, :]), [b_xg, b_xs], [b_xs])
        P1V = cfg.get('P1V', 9)
        for j in range(min(6, cfg.get('P1J', 6)) if P1V >= 2 else 0):
            m = 128 if j < 5 else 64
            ps, b_ps = psb[j % 2]

            def f(j=j, m=m, ps=ps):
                last = None
                for k in range(8):
                    last = nc.tensor.matmul(ps[0:m, 0:512], lhsT=wkv[:, k, j * 128:j * 128 + m], rhs=xs[:, k, :], start=(k == 0), stop=(k == 7))
                return last
            S.op('pe', f, [b_wkv, b_xs], [b_ps])
            if j < 4:
                dst, b_dst = kaT[:, j, gi * 512:(gi + 1) * 512], b_kaT
            elif j == 4:
                dst, b_dst = kbT[:, gi * 512:(gi + 1) * 512], b_kbT
            else:
                dst, b_dst = kiT[0:64, gi * 512:(gi + 1) * 512], b_kiT
            evac_act(dst, ps[0:m, 0:512], [b_ps], [b_dst])
            if P1V < 3:
                continue
            sk, b_sk = stgk[sti % 2]
            sti += 1
            if cfg.get('P1X', 3) == 3:
                evac_act(sk[0:m, :], ps[0:m, 0:512], [b_ps], [b_sk])
            elif cfg.get('P1X', 0) == 4:
                S.op('dve', lambda m=m, ps=ps, sk=sk: nc.vector.tensor_scalar(out=sk[0:m, :], in0=ps[0:m, 0:512], scalar1=1.0, scalar2=None, op0=ALU.mult), [b_ps], [b_sk])
            elif cfg.get('P1X', 0) != 2:
                S.op('dve', lambda m=m, ps=ps, sk=sk: nc.vector.tensor_copy(out=sk[0:m, :], in_=ps[0:m, 0:512]), [b_ps], [b_sk])
            if cfg.get('P1X', 0) != 1:
                dma('sp', kT_all[j * 128:j * 128 + m, gi * 512:(gi + 1) * 512], sk[0:m, :], [b_sk], [OUTB], "st0")
        for t in range(4 if P1V >= 4 else 0):
            kt = gi * 4 + t
            ps, b_ps = psb[2 + (t % 2)]

            def f(t=t, ps=ps):
                last = None
                for k in range(8):
                    nc.tensor.matmul(ps[:, 0:512], lhsT=xs[:, k, t * 128:(t + 1) * 128], rhs=wkv[:, k, 704:1216], start=(k == 0), stop=(k == 7))
                    last = nc.tensor.matmul(ps[:, 512:640], lhsT=xs[:, k, t * 128:(t + 1) * 128], rhs=wkv[:, k, 1216:1344], start=(k == 0), stop=(k == 7))
                return last
            S.op('pe', f, [b_wkv, b_xs], [b_ps])
            evac_act(va[:, kt, :], ps[:, 0:512], [b_ps], [b_va])
            evac_act(vb[:, kt, :], ps[:, 512:640], [b_ps], [b_vb])
            sg, b_sg = stg[t % 2]
            evac_act(sg[:, 0:512], ps[:, 0:512], [b_ps], [b_sg])
            evac_act(sg[:, 512:640], ps[:, 512:640], [b_ps, b_sg], [b_sg])
            dma('sp', v_all[kt * 128:(kt + 1) * 128, :], sg[:, 0:640], [b_sg], [OUTB], "st1")
    A.top = mk

    class _Stop(Exception):
        pass
    SUB = cfg.get('SUB', 99)

    def chk(level):
        if SUB <= level:
            raise _Stop()

    def finish_all():
        for sname, v in S.cnt.items():
            if v > 0 and S.seen['sp'].get(sname, 0) < v:
                nc.sync.wait_ge(S.sems[sname], v)
        return nc, dict(peak=A.peak)
    STOP = cfg.get('STOP', 9)
    if STOP <= 1:
        return finish_all()
    if PRECAST:
        w_in_e, w_out_e, w_in_o, w_out_o, w_mlp1, w_mlp2 = bf_in_e, bf_out_e, bf_in_o, bf_out_o, bf_m1, bf_m2
    xb, b_xb = A.alloc("xb", [8, GQ], BF16)
    aT, b_aT = A.alloc("aT", [8, GQ], BF16)
    pqr, b_pqr = A.alloc("pqr", [GQ], F32)
    mkg = A.top
    qT, b_qT = A.alloc("qT", [8, GQ], BF16)
    qiT, b_qiT = A.alloc("qiT", [8, GQ], BF16)
    wtok, b_wtok = A.alloc("wtok", [4, 8], F32)
    sc0 = A.top
    score, b_score = A.alloc("score", [max(LCAP, 4096)], F32)
    sc1 = A.top
    A.top = sc0
    PTs = [A.alloc("PT%d" % i, [1024], BF16) for i in range(2)]
    f1, b_f1 = A.alloc("f1", [1024], F32)
    f2, b_f2 = A.alloc("f2", [1024], F32)
    f3, b_f3 = A.alloc("f3", [512], F32)
    f4, b_f4 = A.alloc("f4", [512], BF16)
    assert A.top <= sc1
    A.top = sc1
    maskb, b_maskb = A.alloc("maskb", [LCAP], BF16)
    maskT, b_maskT = A.alloc("maskT", [LCAP // 128, 128], BF16)
    cm = [A.alloc("cm%d" % i, [128], BF16) for i in range(2)]
    rtmp = [A.alloc("rtmp%d" % i, [512], F32) for i in range(2)]
    bis, b_bis = A.alloc("bis", [64], F32)
    pqb, b_pqb = A.alloc("pqb", [64], F32)
    attn_top = A.top
    A.top = mkg
    hT, b_hT = A.alloc("hT", [32, GQ], BF16)
    lnscr = (A.alloc("lnsq", [8, GQ], BF16), A.alloc("lnm2", [GQ], F32), A.alloc("lnrs", [GQ], F32))
    xf, b_xf = A.alloc("xf", [8, GQ], F32)
    post_top = A.top
    A.top = mkg
    xce, b_xce = A.alloc("xce", [4, 16 + GQ], F32)
    lv = [A.alloc("lv%d" % i, [16 + GQ], F32) for i in range(2)]
    poolb, b_poolb = A.alloc("poolb", [4, GQ], BF16)
    uT, b_uT = A.alloc("uT", [4, GQ], BF16)
    vtok, b_vtok = A.alloc("vtok", [4, 512], F32)
    vnb, b_vnb = A.alloc("vnb", [4, 512], BF16)
    vst, b_vst = A.alloc("vst", [4, 8], F32)
    l1_top = A.top
    A.top = max(attn_top, post_top, l1_top)

    def attention_tile(q0, nq, ktiles, krow_len, pqcol, Ktop, nvis=0, after_diff=None):
        nkt = len(ktiles)
        (SA0, b_SA0), (SA1, b_SA1), (O, b_O), (L, b_L) = psb
        SAs = [(SA0, b_SA0), (SA1, b_SA1)]
        W4a = 4 * nq
        for ti, kt in enumerate(ktiles):
            nk = kt['nk']
            SA, b_SA = SAs[ti % 2]
            PT, b_PT = PTs[ti % 2]
            cmk, b_cmk = cm[ti % 2]

            def f(kt=kt, nk=nk, SA=SA):
                last = None
                for h in range(4):
                    for s in range(2):
                        last = nc.tensor.matmul(SA[0:nk, s * 512 + h * nq:s * 512 + (h + 1) * nq], lhsT=kt['kaT'][h][64 * s:64 * s + 64, :],
                                                rhs=qT[64 * s:64 * s + 64, h, q0:q0 + nq], start=True, stop=True)
                return last
            S.op('pe', f, kt['b_diff'] + [b_qT], [b_SA])
            for s_ in range(2):
                c0 = s_ * 512
                S.op('act', lambda nk=nk, SA=SA, PT=PT, c0=c0: nc.scalar.activation(out=PT[0:nk, c0:c0 + W4a], in_=SA[0:nk, c0:c0 + W4a], func=AF.Exp, scale=0.125), [b_SA], [b_PT])
            kcol1 = sum(k_['nk'] for k_ in ktiles[:ti + 1])
            need_mask = kcol1 > nvis
            if need_mask:
                S.op('dve', lambda nk=nk, kt=kt, cmk=cmk: nc.vector.tensor_scalar(out=cmk[0:nk, 0:nq], in0=pqr[0:nk, q0:q0 + nq], scalar1=kt['pkc'], scalar2=None, op0=ALU.is_ge),
                     [b_pqr, b_cst], [b_cmk])
            for s_ in (range(2) if need_mask else []):
                c0 = s_ * 512
                S.op('pool', lambda nk=nk, PT=PT, cmk=cmk, c0=c0: nc.gpsimd.tensor_tensor(out=PT[0:nk, c0:c0 + W4a].rearrange("p (a q) -> p a q", a=4), in0=PT[0:nk, c0:c0 + W4a].rearrange("p (a q) -> p a q", a=4),
                                                                                        in1=cmk[0:nk, 0:nq].unsqueeze(1).to_broadcast([nk, 4, nq]), op=ALU.mult), [b_PT, b_cmk], [b_PT])

            def g(kt=kt, nk=nk, PT=PT, ti=ti):
                last = None
                for s in range(2):
                    for h in range(4):
                        c = s * 512 + h * nq
                        nc.tensor.matmul(O[:, c:c + nq], lhsT=kt['va'][h], rhs=PT[0:nk, c:c + nq],
                                         start=(ti == 0 and h == 0), stop=(ti == nkt - 1), skip_group_check=True)
                for s in range(2):
                    c0 = s * 512
                    last = nc.tensor.matmul(L[:, c0:c0 + W4a], lhsT=onesb[0:nk, :], rhs=PT[0:nk, c0:c0 + W4a], start=(ti == 0), stop=(ti == nkt - 1))
                return last
            S.op('pe', g, kt['b_diff'] + [b_PT, b_onesb], [b_O, b_L])
        if after_diff is not None:
            after_diff()
        chk(2)
        for s_ in range(2):
            c0 = s_ * 512
            S.op('dve', lambda c0=c0: nc.vector.reciprocal(out=f1[:, c0:c0 + W4a], in_=L[:, c0:c0 + W4a]), [b_L], [b_f1])
            S.op('dve', lambda c0=c0: nc.vector.tensor_tensor(out=f2[:, c0:c0 + W4a], in0=O[:, c0:c0 + W4a], in1=f1[:, c0:c0 + W4a], op=ALU.mult), [b_O, b_f1], [b_f2])
        f2s0 = f2[:, 0:W4a].rearrange("p (h q) -> p h q", h=4)
        f2s1 = f2[:, 512:512 + W4a].rearrange("p (h q) -> p h q", h=4)
        f3v = f3[:, 0:4 * nq].rearrange("p (h q) -> p h q", h=4)
        S.op('dve', lambda: nc.vector.scalar_tensor_tensor(out=f3v, in0=f2s1, scalar=small[:, 0:1], in1=f2s0, op0=ALU.mult, op1=ALU.add), [b_f2, b_small], [b_f3])
        S.op('act', lambda: nc.scalar.activation(out=f4[:, 0:4 * nq], in_=f3[:, 0:4 * nq], func=AF.Square), [b_f3], [b_f4])
        S.op('pe', lambda: nc.tensor.matmul(SA0[:, 0:4 * nq], lhsT=ones128, rhs=f4[:, 0:4 * nq], start=True, stop=True), [b_f4, b_ones128], [b_SA0])
        S.op('dve', lambda: nc.vector.tensor_scalar(out=f1[:, 0:4 * nq], in0=SA0[:, 0:4 * nq], scalar1=EPS, scalar2=None, op0=ALU.add), [b_SA0], [b_f1])
        S.op('act', lambda: nc.scalar.activation(out=f1[:, 0:4 * nq], in_=f1[:, 0:4 * nq], func=AF.Sqrt), [b_f1], [b_f1])
        S.op('dve', lambda: nc.vector.reciprocal(out=f1[:, 0:4 * nq], in_=f1[:, 0:4 * nq]), [b_f1], [b_f1])
        S.op('dve', lambda: nc.vector.tensor_tensor(out=f3[:, 0:4 * nq], in0=f3[:, 0:4 * nq], in1=f1[:, 0:4 * nq], op=ALU.mult), [b_f3, b_f1], [b_f3])
        S.op('dve', lambda: nc.vector.tensor_scalar(out=aT[:, 0:4, q0:q0 + nq], in0=f3v, scalar1=small[:, 1:2], scalar2=None, op0=ALU.mult), [b_f3, b_small], [b_aT])

        chk(3)
        (D0, b_D0), (D1, b_D1), (TPp, b_TP), (OL, b_OL) = psb
        Ds = [(D0, b_D0), (D1, b_D1)]
        di = 0
        col = 0
        blocks = []
        for kt in ktiles:
            if blocks and blocks[-1][1] + kt['nk'] <= 512 and blocks[-1][3] is kt['kiT_base']:
                blocks[-1][1] += kt['nk']
                blocks[-1][4] += kt['b_idx']
            else:
                blocks.append([col, kt['nk'], kt['kiT_off'], kt['kiT_base'], list(kt['b_idx'])])
            col += kt['nk']
        Ltot = col
        for (c0, bw, koff, kbase, kbufs) in blocks:
            for h in range(8):
                D, b_D = Ds[di % 2]
                rt, b_rt = rtmp[di % 2]
                di += 1
                S.op('pe', lambda D=D, h=h, bw=bw, koff=koff, kbase=kbase: nc.tensor.matmul(D[0:nq, 0:bw], lhsT=qiT[0:64, h, q0:q0 + nq], rhs=kbase[0:64, koff:koff + bw], start=True, stop=True),
                     kbufs + [b_qiT], [b_D])
                S.op('act', lambda D=D, rt=rt, bw=bw: nc.scalar.activation(out=rt[0:nq, 0:bw], in_=D[0:nq, 0:bw], func=AF.Relu), [b_D], [b_rt])
                if h == 0:
                    S.op('dve', lambda rt=rt, c0=c0, bw=bw: nc.vector.tensor_scalar(out=score[0:nq, c0:c0 + bw], in0=rt[0:nq, 0:bw], scalar1=wtokv[0:nq, 0:1], scalar2=None, op0=ALU.mult),
                         [b_rt, b_wtok, b_wtok_s], [b_score])
                else:
                    S.op('dve', lambda rt=rt, c0=c0, bw=bw, h=h: nc.vector.scalar_tensor_tensor(out=score[0:nq, c0:c0 + bw], in0=rt[0:nq, 0:bw], scalar=wtokv[0:nq, h:h + 1],
                                                                                          in1=score[0:nq, c0:c0 + bw], op0=ALU.mult, op1=ALU.add), [b_rt, b_wtok, b_wtok_s, b_score], [b_score])
        chk(3.3)
        S.op('dve', lambda: nc.vector.tensor_reduce(out=bis[0:nq, 0:1], in_=score[0:nq, 0:Ltot], axis=AX.X, op=ALU.max), [b_score], [b_bis])
        S.op('dve', lambda: nc.vector.tensor_reduce(out=bis[0:nq, 1:2], in_=score[0:nq, 0:Ltot], axis=AX.X, op=ALU.min), [b_score, b_bis], [b_bis])
        S.op('dve', lambda: nc.vector.tensor_tensor(out=bis[0:nq, 2:3], in0=bis[0:nq, 0:1], in1=bis[0:nq, 1:2], op=ALU.subtract), [b_bis], [b_bis])
        S.op('dve', lambda: nc.vector.tensor_scalar(out=bis[0:nq, 2:3], in0=bis[0:nq, 2:3], scalar1=2.0, scalar2=None, op0=ALU.add), [b_bis], [b_bis])
        S.op('dve', lambda: nc.vector.tensor_scalar(out=bis[0:nq, 8:8 + NBIS + 2], in0=pow2[0:nq, 0:NBIS + 2], scalar1=bis[0:nq, 2:3], scalar2=None, op0=ALU.mult), [b_bis, b_cst], [b_bis])
        S.op('dve', lambda: nc.vector.scalar_tensor_tensor(out=bis[0:nq, 3:4], in0=bis[0:nq, 1:2], scalar=-1.0, in1=bis[0:nq, 8:9], op0=ALU.add, op1=ALU.add), [b_bis], [b_bis])
        S.op('dve', lambda: nc.vector.tensor_scalar(out=pqb[0:nq, 0:16], in0=blkoff[0:nq, 0:16], scalar1=-1.0, scalar2=pqcol, op0=ALU.mult, op1=ALU.add), [b_cst, b_posq_col], [b_pqb])
        for c0 in range(0, Ltot, 512):
            bw = min(512, Ltot - c0)
            if c0 + bw <= nvis:
                continue
            bi = c0 // 512
            rt, b_rt = rtmp[bi % 2]
            S.op('pool', lambda rt=rt, bw=bw, bi=bi: nc.gpsimd.tensor_scalar(out=rt[0:nq, 0:bw], in0=ramp512[0:nq, 0:bw], scalar1=pqb[0:nq, bi:bi + 1], scalar2=0.0, op0=ALU.subtract, op1=ALU.max),
                 [b_cst, b_pqb], [b_rt])
            S.op('dve', lambda rt=rt, bw=bw, c0=c0: nc.vector.scalar_tensor_tensor(out=score[0:nq, c0:c0 + bw], in0=rt[0:nq, 0:bw], scalar=NEG, in1=score[0:nq, c0:c0 + bw], op0=ALU.mult, op1=ALU.add),
                 [b_rt, b_score], [b_score])
        chk(3.5)
        for it in range(NBIS):
            S.op('dve', lambda: nc.vector.tensor_scalar(out=maskb[0:nq, 0:Ltot], in0=score[0:nq, 0:Ltot], scalar1=bis[0:nq, 3:4], scalar2=None, op0=ALU.is_ge, op1=ALU.add, accum_out=bis[0:nq, 4:5]),
                 [b_score, b_bis], [b_maskb, b_bis])
            S.op('dve', lambda it=it: nc.vector.tensor_scalar(out=bis[0:nq, 5:6], in0=bis[0:nq, 4:5], scalar1=float(Ktop) - 0.5, scalar2=bis[0:nq, 8 + it:9 + it], op0=ALU.is_ge, op1=ALU.mult), [b_bis], [b_bis])
            S.op('dve', lambda it=it: nc.vector.scalar_tensor_tensor(out=bis[0:nq, 3:4], in0=bis[0:nq, 5:6], scalar=bis[0:nq, 9 + it:10 + it], in1=bis[0:nq, 3:4], op0=ALU.subtract, op1=ALU.add), [b_bis], [b_bis])
        S.op('dve', lambda: nc.vector.tensor_tensor(out=bis[0:nq, 6:7], in0=bis[0:nq, 3:4], in1=bis[0:nq, 8 + NBIS:9 + NBIS], op=ALU.subtract), [b_bis], [b_bis])
        S.op('dve', lambda: nc.vector.tensor_scalar(out=maskb[0:nq, 0:Ltot], in0=score[0:nq, 0:Ltot], scalar1=bis[0:nq, 6:7], scalar2=None, op0=ALU.is_ge), [b_score, b_bis], [b_maskb])
        chk(3.7)
        TPb = TPp.bitcast(BF16)
        col = 0
        for g0 in range(0, nkt, 8):
            g1 = min(nkt, g0 + 8)

            def f(g0=g0, g1=g1, col=col):
                last = None
                c = col
                for ti in range(g0, g1):
                    nk = ktiles[ti]['nk']
                    last = nc.tensor.transpose(TPb[0:nk, (ti - g0) * 128:(ti - g0) * 128 + nq], maskb[0:nq, c:c + nk], identb[0:nq, 0:nq])
                    c += nk
                return last
            S.op('pe', f, [b_maskb, b_identb], [b_TP])
            for ti in range(g0, g1):
                col += ktiles[ti]['nk']
            S.op('act', lambda g0=g0, g1=g1: nc.scalar.activation(out=maskT[:, g0:g1, 0:nq], in_=TPb[:, 0:(g1 - g0) * 128].rearrange("p (t q) -> p t q", q=128)[:, :, 0:nq], func=AF.Copy), [b_TP], [b_maskT])
        chk(4)
        W4 = 4 * nq
        for ti, kt in enumerate(ktiles):
            nk = kt['nk']
            D, b_D = Ds[ti % 2]
            PT, b_PT = PTs[ti % 2]

            def f(kt=kt, nk=nk, D=D):
                last = None
                for h in range(4):
                    last = nc.tensor.matmul(D[0:nk, h * nq:(h + 1) * nq], lhsT=kt['kbT'], rhs=qT[:, 4 + h, q0:q0 + nq], start=True, stop=True)
                return last
            S.op('pe', f, kt['b_sp'] + [b_qT], [b_D])
            S.op('act', lambda nk=nk, D=D, PT=PT: nc.scalar.activation(out=PT[0:nk, 0:W4], in_=D[0:nk, 0:W4], func=AF.Exp, scale=128 ** -0.5), [b_D], [b_PT])
            S.op('pool', lambda nk=nk, PT=PT, ti=ti: nc.gpsimd.tensor_tensor(out=PT[0:nk, 0:W4].rearrange("p (a q) -> p a q", a=4), in0=PT[0:nk, 0:W4].rearrange("p (a q) -> p a q", a=4),
                                                                              in1=maskT[0:nk, ti, 0:nq].unsqueeze(1).to_broadcast([nk, 4, nq]), op=ALU.mult), [b_PT, b_maskT], [b_PT])

            def g(kt=kt, nk=nk, PT=PT, ti=ti):
                nc.tensor.matmul(OL[:, 0:W4], lhsT=kt['vb'], rhs=PT[0:nk, 0:W4], start=(ti == 0), stop=(ti == nkt - 1))
                return nc.tensor.matmul(OL[:, 512:512 + W4], lhsT=onesb[0:nk, :], rhs=PT[0:nk, 0:W4], start=(ti == 0), stop=(ti == nkt - 1))
            S.op('pe', g, kt['b_sp'] + [b_PT, b_onesb], [b_OL])
        S.op('dve', lambda: nc.vector.reciprocal(out=f1[:, 0:W4], in_=OL[:, 512:512 + W4]), [b_OL], [b_f1])
        S.op('dve', lambda: nc.vector.tensor_tensor(out=aT[:, 4:8, q0:q0 + nq], in0=OL[:, 0:W4].rearrange("p (h q) -> p h q", h=4), in1=f1[:, 0:W4].rearrange("p (h q) -> p h q", h=4), op=ALU.mult),
             [b_OL, b_f1], [b_aT])

    wtokv = None

    def prompt_ktiles(n):
        kts = []
        for t in range(n):
            kts.append(dict(nk=128, kaT=[kaT[:, h, t * 128:(t + 1) * 128] for h in range(4)], kbT=kbT[:, t * 128:(t + 1) * 128],
                            kiT_base=kiT, kiT_off=t * 128, va=[va[:, t, h * 128:(h + 1) * 128] for h in range(4)], vb=vb[:, t, :],
                            pkc=poskc[:, t:t + 1], b_diff=[b_kaT, b_va], b_idx=[b_kiT], b_sp=[b_kbT, b_vb]))
        return kts

    def run_group(kind, gi):
        nonlocal wtokv
        if kind == 'halo':
            n, c0, xsrc, tile0 = NH, 0, xT_q, 0
        elif kind == 'own':
            n, c0, xsrc, tile0 = GQ, NH + gi * GQ, xT_q, 1 + gi * 4
        else:
            n, c0, xsrc, tile0 = NS, 0, xT_s, NTQ - 1
        pc0 = c0 if kind != 'sample' else NQ
        dma('pool', xb[:, :, 0:n], xsrc[:, c0:c0 + n].rearrange("(k p) n -> p k n", p=128), [], [b_xb], "ld0")
        dma('sp', pqr[:, 0:n], posq_row_d[:, pc0:pc0 + n], [], [b_pqr], "ld1")
        def cons_q(base):
            def c(j, m, ps, b_ps):
                evac_act(qT[:, base + j, 0:n], ps[:, 0:n], [b_ps], [b_qT])
            return c
        proj_fm(w_in_e[:, QA:QA + 512], 512, xb, b_xb, n, 8, cons_q(0))
        proj_fm(w_in_e[:, QB:QB + 512], 512, xb, b_xb, n, 8, cons_q(4))
        wv, b_w = wload(w_in_e[:, QI:QI + 512], 8, 512)
        for h in range(8):
            ps, b_ps = psb[2 + (h % 2)]

            def f(h=h, ps=ps, wv=wv):
                last = None
                for k in range(8):
                    last = nc.tensor.matmul(ps[0:64, 0:n], lhsT=wv[:, k, h * 64:(h + 1) * 64], rhs=xb[:, k, 0:n], start=(k == 0), stop=(k == 7))
                return last
            S.op('pe', f, [b_w, b_xb], [b_ps])
            evac_act(qiT[0:64, h, 0:n], ps[0:64, 0:n], [b_ps], [b_qiT])
        ntile = (n + 127) // 128
        for t in range(ntile if kind != 'sample' else 0):
            tn = min(128, n - t * 128)
            ps, b_ps = psb[2 + (t % 2)]

            def f(t=t, tn=tn, ps=ps):
                last = None
                for k in range(8):
                    last = nc.tensor.matmul(ps[0:tn, 0:8], lhsT=xb[:, k, t * 128:t * 128 + tn], rhs=wwi[:, k, :], start=(k == 0), stop=(k == 7))
                return last
            S.op('pe', f, [b_wwi, b_xb], [b_ps])
            evac_act(wtok[0:tn, t, :], ps[0:tn, 0:8], [b_ps], [b_wtok], scale=(8 ** -0.5) / 8.0)
        chk(1)
        if kind != 'sample':
            for t in range(ntile):
                tn = min(128, n - t * 128)
                wtokv = wtok[:, t, :]
                if kind == 'halo':
                    nkeys = NKT
                else:
                    nkeys = min(NKT, 8 * (gi + 1))
                attention_tile(t * 128, tn, prompt_ktiles(nkeys), nkeys * 128, posq_col[0:tn, tile0 + t:tile0 + t + 1], KP, nvis=(0 if kind == 'halo' else 1024 * gi))
        else:
            sample_attention(n)
        chk(5)
        dma('sp', xf[:, :, 0:n], xsrc[:, c0:c0 + n].rearrange("(k p) n -> p k n", p=128), [], [b_xf], "ld2")
        outproj(w_out_e, aT, b_aT, xf, b_xf, n)
        chk(6)
        layernorm(xf, b_xf, xb, b_xb, n, 0, lnscr)
        chk(7)
        mlp(0, xf, b_xf, xb, b_xb, n, hT, b_hT)
        chk(8)
        layernorm(xf, b_xf, xb, b_xb, n, 1, lnscr)
        chk(9)
        layer1_mixer(kind, gi, n)
        if kind == 'halo':
            return
        outproj(w_out_o, aT, b_aT, xf, b_xf, n)
        layernorm(xf, b_xf, xb, b_xb, n, 2, lnscr)
        mlp(1, xf, b_xf, xb, b_xb, n, hT, b_hT)
        layernorm(xf, b_xf, xb, b_xb, n, 3, lnscr)
        if kind == 'own':
            dma('sp', yT_q[:, gi * GQ:(gi + 1) * GQ].rearrange("(k p) n -> p k n", p=128), xf[:, :, 0:n], [b_xf], [OUTB], "st0")
        else:
            dma('sp', yT_s.rearrange("(k p) n -> p k n", p=128), xf[:, :, 0:n], [b_xf], [OUTB], "st0")

    def layer1_mixer(kind, gi, n):
        samp = (kind == 'sample')
        HIST = 16
        def cons_xc(j, m, ps, b_ps):
            if samp:
                for g in [j]:
                    S.op('act', lambda: nc.scalar.activation(out=xce[:, j, 0:NB * 20].rearrange("p (b t) -> p b t", t=20)[:, :, 16:20], in_=ps[:, 0:n].rearrange("p (b t) -> p b t", t=4), func=AF.Copy), [b_ps], [b_xce])
            else:
                evac_act(xce[:, j, HIST:HIST + n], ps[:, 0:n], [b_ps], [b_xce])
        proj_fm(w_in_o[:, 0:512], 512, xb, b_xb, n, 8, cons_xc)
        if kind == 'halo':
            for g in range(4):
                S.op('dve', lambda g=g: nc.vector.tensor_tensor(out=xch[:, g, 0:n], in0=xce[:, g, HIST:HIST + n], in1=hval[:, 0:n], op=ALU.mult), [b_xce, b_hval], [b_xch])
            return
        if samp:
            for g in range(4):
                dma('sp', xce[:, g, 0:NB * 20].rearrange("p (b t) -> p b t", t=20)[:, :, 1:16], stateT[g * 128:(g + 1) * 128, :, :], [], [b_xce], "ld3")
            for g in range(4):
                dma('sp', poolT_s[g * 128:(g + 1) * 128, :, :], xce[:, g, 0:NB * 20].rearrange("p (b t) -> p b t", t=20)[:, :, 5:20], [b_xce], [OUTB], "st1")
        else:
            for g in range(4):
                S.op('dve', lambda g=g: nc.vector.tensor_copy(out=xce[:, g, 0:HIST], in_=xch[:, g, gi * 16:(gi + 1) * 16]), [b_xch], [b_xce])
            if gi == NG - 1:
                for g in range(4):
                    dma('sp', xc_tail[g * 128:(g + 1) * 128, :], xce[:, g, HIST + n - 16:HIST + n], [b_xce], [OUTB], "st1")
        for g in range(4):
            w = 2 << g
            if samp:
                E = NB * 20
                src = xce[:, g, 0:E].rearrange("p (b t) -> p b t", t=20)
                cur, b_cur = src, b_xce
                sh = 1
                lo = 0
                for lvl in range(g + 1):
                    dst, b_dst = lv[lvl % 2]
                    dstv = dst[:, 0:E].rearrange("p (b t) -> p b t", t=20)
                    lo += sh
                    S.op('dve', lambda cur=cur, dstv=dstv, lo=lo, sh=sh: nc.vector.tensor_tensor(out=dstv[:, :, lo:20], in0=cur[:, :, lo:20], in1=cur[:, :, lo - sh:20 - sh], op=ALU.add), [b_cur], [b_dst])
                    cur, b_cur = dstv, b_dst
                    sh *= 2
                pv = poolb[:, g, 0:n].rearrange("p (b t) -> p b t", t=4)
                S.op('dve', lambda cur=cur, pv=pv, src=src, w=w: nc.vector.scalar_tensor_tensor(out=pv, in0=cur[:, :, 16:20], scalar=1.0 / w, in1=src[:, :, 16:20], op0=ALU.mult, op1=ALU.subtract), [b_cur, b_xce], [b_poolb])
            else:
                E = HIST + n
                src = xce[:, g, 0:E]
                cur, b_cur = src, b_xce
                sh = 1
                lo = 0
                for lvl in range(g + 1):
                    dst, b_dst = lv[lvl % 2]
                    lo += sh
                    S.op('dve', lambda cur=cur, dst=dst, lo=lo, sh=sh, E=E: nc.vector.tensor_tensor(out=dst[:, lo:E], in0=cur[:, lo:E], in1=cur[:, lo - sh:E - sh], op=ALU.add), [b_cur], [b_dst])
                    cur, b_cur = dst[:, 0:E], b_dst
                    sh *= 2
                S.op('dve', lambda cur=cur, src=src, w=w, g=g: nc.vector.scalar_tensor_tensor(out=poolb[:, g, 0:n], in0=cur[:, HIST:HIST + n], scalar=1.0 / w, in1=src[:, HIST:HIST + n], op0=ALU.mult, op1=ALU.subtract),
                     [b_cur, b_xce], [b_poolb])
                if gi == 0:
                    dstl, b_dl = lv[(g + 1) % 2]
                    S.op('dve', lambda cur=cur, dstl=dstl, g=g: nc.vector.tensor_tensor(out=dstl[:, 0:16], in0=cur[:, HIST:HIST + 16], in1=invc[:, g, :], op=ALU.mult), [b_cur, b_invc], [b_dl])
                    S.op('dve', lambda dstl=dstl, src=src, g=g: nc.vector.tensor_tensor(out=poolb[:, g, 0:16], in0=dstl[:, 0:16], in1=src[:, HIST:HIST + 16], op=ALU.subtract), [b_dl, b_xce], [b_poolb])
        for g in range(4):
            ps, b_ps = psb[2 + (g % 2)]
            S.op('pe', lambda g=g, ps=ps: nc.tensor.matmul(ps[:, 0:n], lhsT=wpool_b[:, g, :], rhs=poolb[:, g, 0:n], start=True, stop=True), [b_wpool, b_poolb], [b_ps])
            S.op('dve', lambda g=g, ps=ps: nc.vector.tensor_scalar(out=aT[:, g, 0:n], in0=ps[:, 0:n], scalar1=pscT[:, g:g + 1], scalar2=None, op0=ALU.mult), [b_ps, b_pscT], [b_aT])
        def cons_u(j, m, ps, b_ps):
            evac_act(uT[:, j, 0:n], ps[:, 0:n], [b_ps], [b_uT], func=AF.Gelu)
        proj_fm(w_in_o[:, 512:1024], 512, xb, b_xb, n, 8, cons_u)
        wv, b_w = wload(w_in_o[:, 1024:1536], 8, 512)
        ntile = (n + 127) // 128
        for t in range(ntile):
            tn = min(128, n - t * 128)
            ps, b_ps = psb[2 + (t % 2)]

            def f(t=t, tn=tn, ps=ps, wv=wv):
                last = None
                for k in range(8):
                    last = nc.tensor.matmul(ps[0:tn, 0:512], lhsT=xb[:, k, t * 128:t * 128 + tn], rhs=wv[:, k, :], start=(k == 0), stop=(k == 7))
                return last
            S.op('pe', f, [b_w, b_xb], [b_ps])
            vt = vtok[0:tn, t, :]
            st = vst[0:tn, t, :]
            S.op('act', lambda ps=ps, tn=tn, vt=vt, st=st: nc.scalar.activation(out=vt, in_=ps[0:tn, 0:512], func=AF.Gelu, accum_out=st[:, 0:1]), [b_ps], [b_vtok, b_vst])
            S.op('dve', lambda st=st: nc.vector.tensor_scalar(out=st[:, 1:2], in0=st[:, 0:1], scalar1=1.0 / 512, scalar2=None, op0=ALU.mult), [b_vst], [b_vst])
            S.op('dve', lambda vt=vt, st=st: nc.vector.tensor_scalar(out=vt, in0=vt, scalar1=st[:, 1:2], scalar2=None, op0=ALU.subtract), [b_vtok, b_vst], [b_vtok])
            lvt, b_lvt = lv[t % 2]
            S.op('act', lambda vt=vt, tn=tn, lvt=lvt, st=st: nc.scalar.activation(out=lvt[0:tn, 0:512], in_=vt, func=AF.Square, accum_out=st[:, 2:3]), [b_vtok], [b_lvt, b_vst])
            S.op('dve', lambda st=st: nc.vector.tensor_scalar(out=st[:, 3:4], in0=st[:, 2:3], scalar1=1.0 / 512, scalar2=EPS, op0=ALU.mult, op1=ALU.add), [b_vst], [b_vst])
            S.op('act', lambda st=st: nc.scalar.activation(out=st[:, 4:5], in_=st[:, 3:4], func=AF.Sqrt), [b_vst], [b_vst])
            S.op('dve', lambda st=st: nc.vector.reciprocal(out=st[:, 5:6], in_=st[:, 4:5]), [b_vst], [b_vst])
            S.op('dve', lambda vt=vt, st=st, tn=tn: nc.vector.scalar_tensor_tensor(out=vt, in0=vt, scalar=st[:, 5:6], in1=sgugb[0:tn, 0:512], op0=ALU.mult, op1=ALU.mult), [b_vtok, b_vst, b_sgugb], [b_vtok])
            S.op('dve', lambda vt=vt, tn=tn: nc.vector.tensor_tensor(out=vt, in0=vt, in1=sgugb[0:tn, 512:1024], op=ALU.add), [b_vtok, b_sgugb], [b_vtok])
            S.op('act', lambda vt=vt, tn=tn, t=t: nc.scalar.activation(out=vnb[0:tn, t, :], in_=vt, func=AF.Copy), [b_vtok], [b_vnb])
            if samp:
                dma('sp', vn_s_o[:, :], vt, [b_vtok], [OUTB], "st1")
            for g in range(4):
                ps2, b_ps2 = psb[g % 2]
                if samp:
                    S.op('pe', lambda g=g, ps2=ps2, tn=tn, t=t: nc.tensor.matmul(ps2[:, 0:tn], lhsT=vnb[0:tn, t, g * 128:(g + 1) * 128], rhs=wsTs_b[0:tn, g, 0:tn], start=True, stop=True), [b_vnb, b_wsTs], [b_ps2])
                    bias = bsrs[:, g, 0:tn]
                    b_bias = b_bsrs
                else:
                    S.op('pe', lambda g=g, ps2=ps2, tn=tn, t=t: nc.tensor.matmul(ps2[:, 0:tn], lhsT=vnb[0:tn, t, g * 128:(g + 1) * 128], rhs=wsT_b[0:tn, g, 0:tn], start=True, stop=True), [b_vnb, b_wsT], [b_ps2])
                    bias = bsr[:, g, 0:tn]
                    b_bias = b_bsr
                lvt2, b_lvt2 = lv[g % 2]
                S.op('dve', lambda ps2=ps2, tn=tn, bias=bias, lvt2=lvt2: nc.vector.tensor_tensor(out=lvt2[:, 0:tn], in0=ps2[:, 0:tn], in1=bias, op=ALU.add), [b_ps2, b_bias], [b_lvt2])
                S.op('dve', lambda g=g, tn=tn, t=t, lvt2=lvt2: nc.vector.tensor_tensor(out=aT[:, 4 + g, t * 128:t * 128 + tn], in0=lvt2[:, 0:tn], in1=uT[:, g, t * 128:t * 128 + tn], op=ALU.mult), [b_lvt2, b_uT], [b_aT])

    def sample_attention(n):
        nonlocal wtokv
        kn, b_kn = f2[:, 0:6 * NS].rearrange("p (j t) -> p j t", j=6), b_f2
        knb, b_knb = A_knb
        wv, b_w = wload(w_in_e[:, KA:KA + 512], 8, 512)
        for j in range(4):
            ps, b_ps = psb[2 + (j % 2)]

            def f(j=j, ps=ps, wv=wv):
                last = None
                for k in range(8):
                    last = nc.tensor.matmul(ps[:, 0:n], lhsT=wv[:, k, j * 128:(j + 1) * 128], rhs=xb[:, k, 0:n], start=(k == 0), stop=(k == 7))
                return last
            S.op('pe', f, [b_w, b_xb], [b_ps])
            evac_act(knb[:, j, 0:n], ps[:, 0:n], [b_ps], [b_knb])
            S.op('dve', lambda j=j, ps=ps: nc.vector.tensor_copy(out=kn[:, j, 0:n], in_=ps[:, 0:n]), [b_ps], [b_kn])
        wv, b_w = wload(w_in_e[:, KB:KB + 128], 8, 128)
        ps, b_ps = psb[2]

        def f(ps=ps, wv=wv):
            last = None
            for k in range(8):
                last = nc.tensor.matmul(ps[:, 0:n], lhsT=wv[:, k, :], rhs=xb[:, k, 0:n], start=(k == 0), stop=(k == 7))
            return last
        S.op('pe', f, [b_w, b_xb], [b_ps])
        evac_act(knb[:, 4, 0:n], ps[:, 0:n], [b_ps], [b_knb])
        S.op('dve', lambda ps=ps: nc.vector.tensor_copy(out=kn[:, 4, 0:n], in_=ps[:, 0:n]), [b_ps], [b_kn])
        wv, b_w = wload(w_in_e[:, KI:KI + 64], 8, 64)
        ps, b_ps = psb[3]

        def f(ps=ps, wv=wv):
            last = None
            for k in range(8):
                last = nc.tensor.matmul(ps[0:64, 0:n], lhsT=wv[:, k, :], rhs=xb[:, k, 0:n], start=(k == 0), stop=(k == 7))
            return last
        S.op('pe', f, [b_w, b_xb], [b_ps])
        evac_act(knb[0:64, 5, 0:n], ps[0:64, 0:n], [b_ps], [b_knb])
        S.op('dve', lambda ps=ps: nc.vector.tensor_copy(out=kn[0:64, 5, 0:n], in_=ps[0:64, 0:n]), [b_ps], [b_kn])
        for j in range(6):
            m = 128 if j < 5 else 64
            dma('sp', kT_s_o[j * 128:j * 128 + m, :], kn[0:m, j, 0:n], [b_kn], [OUTB], "st0")
        wvv, b_wvv = wload(w_in_e[:, VA:VA + 512], 8, 512)
        wvb, b_wvb = wload(w_in_e[:, VB:VB + 128], 8, 128)
        idx, b_idx = A_idx
        pa, b_pa = A_pa
        pb, b_pb = A_pb
        vnew, b_vnew = A_vnew
        vnf, b_vnf = A_vnf
        TPbs = [(psb[2][0].bitcast(BF16), psb[2][1]), (psb[0][0].bitcast(BF16), psb[0][1])]
        vbs, b_vbs = A_vbs

        def issue_gather(b):
            def gat(b=b):
                out = []
                for j in range(NPAGE):
                    bj = b * NPAGE + j
                    out.append(nc.gpsimd.indirect_dma_start(out=pa[:, j, :], out_offset=None, in_=cache_a, in_offset=bass.IndirectOffsetOnAxis(ap=idx[:, bj:bj + 1], axis=0)))
                    out.append(nc.gpsimd.indirect_dma_start(out=pb[:, j, :], out_offset=None, in_=cache_b, in_offset=bass.IndirectOffsetOnAxis(ap=idx[:, bj:bj + 1], axis=0)))
                return out
            S.op('pool', gat, [b_idx], [b_pa, b_pb], sem="gath", inc=16, multi=True)
        issue_gather(0)
        for b in range(NB):
            for j in range(NPAGE):
                TPb, b_TP = TPbs[j % 2]

                def tr(j=j, TPb=TPb):
                    last = None
                    for h in range(4):
                        nc.tensor.transpose(TPb[:, h * 128:(h + 1) * 128], pa[:, j, h * 256:h * 256 + 128], identb)
                    nc.tensor.transpose(TPb[:, 512:640], pb[:, j, 0:128], identb)
                    last = nc.tensor.transpose(TPb[0:64, 640:768], pb[:, j, 256:320], identb)
                    return last
                S.op('pe', tr, [b_pa, b_pb, b_identb], [b_TP])
                S.op('act', lambda j=j, TPb=TPb: nc.scalar.activation(out=kaTs[:, :, j * 128:(j + 1) * 128], in_=TPb[:, 0:512].rearrange("p (h k) -> p h k", h=4), func=AF.Copy), [b_TP], [b_kaTs])
                S.op('dve', lambda j=j, TPb=TPb: nc.vector.tensor_copy(out=kbTs[:, j * 128:(j + 1) * 128], in_=TPb[:, 512:640]), [b_TP], [b_kbTs])
                S.op('dve', lambda j=j, TPb=TPb: nc.vector.tensor_copy(out=kiTs[0:64, j * 128:(j + 1) * 128], in_=TPb[0:64, 640:768]), [b_TP], [b_kiTs])
            S.op('act', lambda: nc.scalar.activation(out=vbs, in_=pb[:, :, 128:256], func=AF.Copy), [b_pb], [b_vbs])
            S.op('act', lambda b=b: nc.scalar.activation(out=kaTs[:, :, PAST:PAST + 4], in_=knb[:, 0:4, 4 * b:4 * b + 4], func=AF.Copy), [b_knb], [b_kaTs])
            S.op('dve', lambda b=b: nc.vector.tensor_copy(out=kbTs[:, PAST:PAST + 4], in_=knb[:, 4, 4 * b:4 * b + 4]), [b_knb], [b_kbTs])
            S.op('dve', lambda b=b: nc.vector.tensor_copy(out=kiTs[0:64, PAST:PAST + 4], in_=knb[0:64, 5, 4 * b:4 * b + 4]), [b_knb], [b_kiTs])
            ps, b_ps = psb[3]

            def fv(b=b, ps=ps):
                last = None
                for k in range(8):
                    nc.tensor.matmul(ps[0:4, 0:512], lhsT=xb[:, k, 4 * b:4 * b + 4], rhs=wvv[:, k, :], start=(k == 0), stop=(k == 7))
                for k in range(8):
                    nc.tensor.matmul(ps[0:4, 512:640], lhsT=xb[:, k, 4 * b:4 * b + 4], rhs=wvb[:, k, :], start=(k == 0), stop=(k == 7))
                for k in range(8):
                    last = nc.tensor.matmul(ps[0:4, 640:648], lhsT=xb[:, k, 4 * b:4 * b + 4], rhs=wwi[:, k, :], start=(k == 0), stop=(k == 7))
                return last
            S.op('pe', fv, [b_wvv, b_wvb, b_wwi, b_xb], [b_ps])
            evac_act(wtok_s[0:4, b, :], ps[0:4, 640:648], [b_ps], [b_wtok_s], scale=(8 ** -0.5) / 8.0)
            evac_act(vnew[0:4, 0:512], ps[0:4, 0:512], [b_ps], [b_vnew])
            evac_act(vnew[0:4, 512:640], ps[0:4, 512:640], [b_ps, b_vnew], [b_vnew])
            S.op('dve', lambda ps=ps: nc.vector.tensor_copy(out=vnf[0:4, 0:512], in_=ps[0:4, 0:512]), [b_ps], [b_vnf])
            S.op('dve', lambda ps=ps: nc.vector.tensor_copy(out=vnf[0:4, 512:640], in_=ps[0:4, 512:640]), [b_ps, b_vnf], [b_vnf])
            dma('sp', v_s_o[b, :, :], vnf[0:4, :], [b_vnf], [OUTB], "st1")
            kts = []
            for j in range(NPAGE):
                kts.append(dict(nk=128, kaT=[kaTs[:, h, j * 128:(j + 1) * 128] for h in range(4)], kbT=kbTs[:, j * 128:(j + 1) * 128],
                                kiT_base=kiTs, kiT_off=j * 128, va=[pa[:, j, h * 256 + 128:h * 256 + 256] for h in range(4)], vb=vbs[:, j, :],
                                pkc=poskc[:, j:j + 1], b_diff=[b_kaTs, b_pa], b_idx=[b_kiTs], b_sp=[b_kbTs, b_vbs]))
            kts.append(dict(nk=4, kaT=[kaTs[:, h, PAST:PAST + 4] for h in range(4)], kbT=kbTs[:, PAST:PAST + 4], kiT_base=kiTs, kiT_off=PAST,
                            va=[vnew[0:4, h * 128:(h + 1) * 128] for h in range(4)], vb=vnew[0:4, 512:640], pkc=poskc[0:4, NPAGE:NPAGE + 1],
                            b_diff=[b_kaTs, b_vnew], b_idx=[b_kiTs], b_sp=[b_kbTs, b_vnew]))
            wtokv = wtok_s[:, b, :]
            attention_tile(4 * b, 4, kts, PAST + 4, posq_col[0:4, NTQ - 1:NTQ], KS, nvis=PAST,
                           after_diff=((lambda b=b: issue_gather(b + 1)) if b + 1 < NB else None))

    sv = A.top
    A.top = 0
    LS = PAST + 128
    kaTs, b_kaTs = A.alloc("kaTs", [4, LS], BF16)
    kbTs, b_kbTs = A.alloc("kbTs", [LS], BF16)
    kiTs, b_kiTs = A.alloc("kiTs", [LS], BF16)
    A_pa = A.alloc("pa", [NPAGE, 1024], BF16)
    A_pb = A.alloc("pb", [NPAGE, 320], BF16)
    A_vbs = A.alloc("vbs", [NPAGE, 128], BF16)
    assert A.top <= (4 * LCAP + 2 * LCAP + (LCAP // 128) * 640) * 2, "sample buffers exceed dead prompt K/V region"
    A.top = sv
    A_knb = A.alloc("knb", [6, 64], BF16)
    A_vnew = A.alloc("vnew", [640], BF16)
    A_vnf = A.alloc("vnf", [640], F32)
    wtok_s, b_wtok_s = A.alloc("wtok_s", [NB, 8], F32)

    GROUPS = cfg.get('GROUPS', 'hos')
    try:
        if 'h' in GROUPS:
            run_group('halo', 0)
    except _Stop:
        return finish_all()
    if STOP <= 2:
        return finish_all()
    for gi in cfg.get('GILIST', range(NG if 'o' in GROUPS else 0)):
        run_group('own', gi)
    if STOP <= 3:
        return finish_all()
    if 's' in GROUPS:
        run_group('sample', 0)
    for sname, v in S.cnt.items():
        if v > 0 and S.seen['sp'].get(sname, 0) < v:
            nc.sync.wait_ge(S.sems[sname], v)
    return nc, dict(peak=A.peak)


def _consts():
    c = np.zeros((128, 992), np.float32)
    c[:, 0:512] = np.arange(512, dtype=np.float32)[None, :]
    c[:, 512:576] = (512.0 * np.arange(64, dtype=np.float32))[None, :]
    c[:, 576:608] = (0.5 ** (np.arange(32) + 1)).astype(np.float32)[None, :]
    c[:, 608:736] = np.eye(128, dtype=np.float32)
    c[:, 736:864] = (np.arange(128)[:, None] <= np.arange(128)[None, :]).astype(np.float32)
    c[:, 864:992] = 128.0 * np.arange(128, dtype=np.float32)[None, :] + np.arange(128, dtype=np.float32)[:, None]
    return c


def make_in_maps(cfg, inp):
    SEQ, NB, NPAGE, NPOOL = cfg['SEQ'], cfg['NB'], cfg['NPAGE'], cfg['NPOOL']
    NCORE = cfg['NCORE']
    PAST = NPAGE * 128
    NG = SEQ // 1024
    TOWN, NH, NS = NG * 512, 16 * NG, NB * 4
    NQ = NH + TOWN
    NTQ = 1 + TOWN // 128 + 1
    f = lambda a: np.ascontiguousarray(a, dtype=np.float32)
    xp = np.asarray(inp['x_prompt'])
    xs = np.asarray(inp['x_sample'])
    ca = np.asarray(inp['cache_a'])[0].reshape(NPOOL * 128, 1024)
    cb = np.asarray(inp['cache_b'])[0].reshape(NPOOL * 128, 320)
    pt = np.asarray(inp['page_table']).astype(np.int32)
    st = np.asarray(inp['state_pool'])[0]
    w_s = np.asarray(inp['w_s'])[0]
    b_s = np.asarray(inp['b_s'])[0]
    shared = dict(
        consts=_consts(), cache_a=f(ca), cache_b=f(cb),
        w_in_e=f(inp['w_in_e'][0]), lam_rep=f(np.broadcast_to(np.asarray(inp['lam_e'])[0].reshape(1, 256), (128, 256))),
        sublng=f(np.asarray(inp['subln_g'])[0].reshape(128, 1)), w_out_e=f(inp['w_out_e'][0]), w_in_o=f(inp['w_in_o'][0]), w_out_o=f(inp['w_out_o'][0]),
        w_pool=f(inp['w_pool'][0]), pool_scaleT=f(np.asarray(inp['pool_scale'])[0].reshape(4, 128).T),
        sgu_gb=f(np.broadcast_to(np.concatenate([np.asarray(inp['sgu_g'])[0], np.asarray(inp['sgu_b'])[0]])[None, :], (128, 1024))),
        w_sT=f(np.transpose(w_s, (2, 0, 1))),
        w_sT_s=f(np.tile(np.transpose(w_s[:, :4, :4], (2, 0, 1)), (16, 1, 16))),
        mask_s=f(np.kron(np.eye(16), (np.arange(4)[:, None] <= np.arange(4)[None, :]).astype(np.float32))),
        bs_rep=f(np.broadcast_to(b_s[None, :, :], (128, 4, 128))),
        bs_rep_s=f(np.broadcast_to(np.tile(b_s[:, :4], (1, 16))[None, :, :], (128, 4, 64))),
        ln_gb=f(np.concatenate([np.asarray(inp['ln_g']).reshape(4, 8, 128).transpose(2, 0, 1).reshape(128, 32),
                                np.asarray(inp['ln_b']).reshape(4, 8, 128).transpose(2, 0, 1).reshape(128, 32)], 1)),
        w_mlp1=f(inp['w_mlp1']), w_mlp2=f(inp['w_mlp2']),
    )
    maps = []
    for c in range(NCORE):
        s, h = c // 2, c % 2
        xT = xp[s].T
        own_pos = np.concatenate([np.arange(1024 * i + 512 * h, 1024 * i + 512 * h + 512) for i in range(NG)])
        halo_pos = np.concatenate([np.arange(1024 * i + 512 * h - 16, 1024 * i + 512 * h) for i in range(NG)])
        hvalid = (halo_pos >= 0).astype(np.float32)
        halo_idx = np.maximum(halo_pos, 0)
        qpos = np.concatenate([halo_idx, own_pos])
        xT_q = xT[:, qpos]
        bsl = slice(NB * c, NB * (c + 1))
        xT_s = xs[bsl].reshape(NS, 1024).T
        spos = (PAST + np.tile(np.arange(4), NB)).astype(np.float32)
        posq_row = np.broadcast_to(np.concatenate([qpos.astype(np.float32), spos])[None, :], (128, NQ + NS))
        posq_col = np.zeros((128, NTQ), np.float32)
        posq_col[:NH, 0] = halo_idx
        posq_col[:, 1:1 + TOWN // 128] = own_pos.reshape(TOWN // 128, 128).T
        posq_col[:, NTQ - 1] = PAST + np.arange(128)
        hv = np.zeros((128, 64), np.float32)
        hv[:, :NH] = hvalid[None, :]
        invc = np.zeros((128, 4, 16), np.float32)
        for g in range(4):
            w = 2 << g
            p0 = own_pos[0] + np.arange(16)
            invc[:, g, :] = (1.0 / np.minimum(w, p0 + 1))[None, :]
        m = dict(shared)
        m.update(xT_seq=f(xT), xT_q=f(xT_q), xT_s=f(xT_s), posq_row=f(posq_row), posq_col=posq_col,
                 pt=np.ascontiguousarray(np.broadcast_to(pt[bsl].reshape(1, NB * NPAGE), (128, NB * NPAGE)).astype(np.int32)),
                 stateT=f(np.transpose(st[bsl], (2, 0, 1))), halo_valid=hv, invc=invc.reshape(128, 64))
        maps.append(m)
    return maps


def assemble(cfg, res, nbatch):
    SEQ, NB, NPAGE = cfg['SEQ'], cfg['NB'], cfg['NPAGE']
    NCORE = cfg['NCORE']
    NG = SEQ // 1024
    NS = NB * 4
    DB = NB * NCORE
    y_p = np.zeros((nbatch, SEQ, 1024), np.float32)
    y_s = np.zeros((DB, 4, 1024), np.float32)
    na_p = np.zeros((1, nbatch, SEQ, 4, 256), np.float32)
    nb_p = np.zeros((1, nbatch, SEQ, 320), np.float32)
    pool_p = np.zeros((1, nbatch, 15, 512), np.float32)
    na_s = np.zeros((1, DB, 4, 4, 256), np.float32)
    nb_s = np.zeros((1, DB, 4, 320), np.float32)
    pool_s = np.zeros((1, DB, 15, 512), np.float32)
    v_s = np.zeros((1, DB, 4, 512), np.float32)
    for c in range(NCORE):
        r = res[c]
        s, h = c // 2, c % 2
        for i in range(NG):
            y_p[s, 1024 * i + 512 * h:1024 * i + 512 * h + 512] = r['yT_q'][:, i * 512:(i + 1) * 512].T
        if h == 0:
            kT = r['kT_all']
            v = r['v_all']
            na_p[0, s, :, :, 0:128] = kT[0:512].T.reshape(SEQ, 4, 128)
            na_p[0, s, :, :, 128:256] = v[:, 0:512].reshape(SEQ, 4, 128)
            nb_p[0, s, :, 0:128] = kT[512:640].T
            nb_p[0, s, :, 128:256] = v[:, 512:640]
            nb_p[0, s, :, 256:320] = kT[640:704].T
        else:
            pool_p[0, s] = r['xc_tail'][:, 1:16].T
        bsl = slice(NB * c, NB * (c + 1))
        y_s[bsl] = r['yT_s'].T.reshape(NB, 4, 1024)
        kTs = r['kT_s']
        vs = r['v_s']
        na_s[0, bsl, :, :, 0:128] = kTs[0:512].T.reshape(NB, 4, 4, 128)
        na_s[0, bsl, :, :, 128:256] = vs[:, :, 0:512].reshape(NB, 4, 4, 128)
        nb_s[0, bsl, :, 0:128] = kTs[512:640].T.reshape(NB, 4, 128)
        nb_s[0, bsl, :, 128:256] = vs[:, :, 512:640]
        nb_s[0, bsl, :, 256:320] = kTs[640:704].T.reshape(NB, 4, 64)
        pool_s[0, bsl] = np.transpose(r['poolT_s'], (1, 2, 0))
        v_s[0, bsl] = r['vn_s'].reshape(NB, 4, 512)
    return (y_p, y_s, na_p, nb_p, pool_p, na_s, nb_s, pool_s, v_s)


_CACHE = {}


def run_cfg(cfg, inp, nbatch):
    key = tuple(sorted(cfg.items()))
    if key not in _CACHE:
        _CACHE[key] = build(cfg)[0]
    nc = _CACHE[key]
    maps = make_in_maps(cfg, inp)
    res = run_bass_kernel_spmd(nc, maps, core_ids=list(range(cfg['NCORE'])))
    return assemble(cfg, res.results, nbatch)


def kernel(**inputs):
    cfg = dict(SEQ=4096, NB=16, NPAGE=16, NPOOL=int(np.asarray(inputs['cache_a']).shape[1]), NCORE=8)
    return run_cfg(cfg, inputs, 4)
```
